# Optimizing a Trainium2 kernel written in Bass

```python
import math
import jax, jax.numpy as jnp
from jax import lax
import numpy as np

D_MODEL = 1024
BATCH = 8
SEQ = 2048
DEPTH = 2

CHUNK = 64
Q_BLOCK = 128
N_MIXERS = 2
ATTN_HEADS = 8
ATTN_HEAD_DIM = 64
SSM_GROUP = 16
SSM_GROUPS = D_MODEL // SSM_GROUP
SSM_STATE = 64
D_FF = 2816
ALPHA = (2 * DEPTH) ** 0.25
BETA = (8 * DEPTH) ** -0.25
N_ATTN_LAYERS = (DEPTH + 1) // 2
N_SSM_LAYERS = DEPTH // 2
LN_EPS = 1e-5
DT_MIN = 1e-3
DT_MAX = 1e-1

kernel_name = "hybrid_diffattn_s5_macaron_deepnorm_adaln"


def layer_norm(x, g, b):
    xf = x.astype(jnp.float32)
    mu = jnp.mean(xf, axis=-1, keepdims=True)
    var = jnp.mean(jnp.square(xf - mu), axis=-1, keepdims=True)
    y = (xf - mu) * lax.rsqrt(var + LN_EPS) * g.astype(jnp.float32) + b.astype(jnp.float32)
    return y.astype(x.dtype)


def modulate(x, shift, scale):
    return x * (1.0 + scale[:, None, :]) + shift[:, None, :]


def post_norm_residual(x, y, gate, g, b):
    return layer_norm(ALPHA * x + gate[:, None, :] * y, g, b)


def swiglu(h, w1, w3, w2):
    return (jax.nn.silu(h @ w1) * (h @ w3)) @ w2


def diff_attention(h, w_in, lam, subln_g, w_out, lam_init):
    bsz, seq, _ = h.shape
    hd = ATTN_HEAD_DIM
    q, k, v = jnp.split(h @ w_in, 3, axis=-1)
    q = q.reshape(bsz, seq, ATTN_HEADS, 2, hd)
    k = k.reshape(bsz, seq, ATTN_HEADS, 2, hd)
    v = v.reshape(bsz, seq, ATTN_HEADS, 2 * hd)
    lam_f = lam.astype(jnp.float32)
    lam_full = (jnp.exp(jnp.sum(lam_f[0] * lam_f[1]))
                - jnp.exp(jnp.sum(lam_f[2] * lam_f[3])) + lam_init)
    scale = hd ** -0.5
    outs = []
    for s0 in range(0, seq, Q_BLOCK):
        kend = s0 + Q_BLOCK
        scores = jnp.einsum('bqhmd,bkhmd->bhmqk', q[:, s0:kend], k[:, :kend])
        scores = scores.astype(jnp.float32) * scale
        q_chunk = (s0 + jnp.arange(Q_BLOCK)) // CHUNK
        k_chunk = jnp.arange(kend) // CHUNK
        allowed = k_chunk[None, :] <= q_chunk[:, None]
        scores = jnp.where(allowed, scores, -jnp.inf)
        p = jax.nn.softmax(scores, axis=-1)
        attn = p[:, :, 0] - lam_full * p[:, :, 1]
        outs.append(jnp.einsum('bhqk,bkhe->bqhe', attn.astype(v.dtype), v[:, :kend]))
    o = jnp.concatenate(outs, axis=1).astype(jnp.float32)
    o = o * lax.rsqrt(jnp.mean(jnp.square(o), axis=-1, keepdims=True) + LN_EPS)
    o = o * subln_g.astype(jnp.float32) * (1.0 - lam_init)
    return o.astype(h.dtype).reshape(bsz, seq, ATTN_HEADS * 2 * hd) @ w_out


def _complex_affine_combine(e1, e2):
    a1r, a1i, b1r, b1i = e1
    a2r, a2i, b2r, b2i = e2
    ar = a1r * a2r - a1i * a2i
    ai = a1r * a2i + a1i * a2r
    br = a2r * b1r - a2i * b1i + b2r
    bi = a2r * b1i + a2i * b1r + b2i
    return (ar, ai, br, bi)


def s5_scan(u, a_re, a_im, log_dt, b_re, b_im, c_re, c_im):
    bsz, seq, g, p = u.shape
    n_chunks = seq // CHUNK
    f32 = jnp.float32
    a_re, a_im = a_re.astype(f32), a_im.astype(f32)
    b_re, b_im = b_re.astype(f32), b_im.astype(f32)
    c_re, c_im = c_re.astype(f32), c_im.astype(f32)
    dt = jnp.exp(log_dt.astype(f32))[:, None]
    mag = jnp.exp(a_re * dt)
    abar_re, abar_im = mag * jnp.cos(a_im * dt), mag * jnp.sin(a_im * dt)
    den = a_re * a_re + a_im * a_im
    pr, pi_ = abar_re - 1.0, abar_im
    coef_re = (pr * a_re + pi_ * a_im) / den
    coef_im = (pi_ * a_re - pr * a_im) / den
    bbar_re = coef_re[..., None] * b_re - coef_im[..., None] * b_im
    bbar_im = coef_re[..., None] * b_im + coef_im[..., None] * b_re
    steps = jnp.arange(1, CHUNK + 1, dtype=f32)[:, None, None]
    pmag = jnp.exp(a_re[None] * dt[None] * steps)
    pow_re = pmag * jnp.cos(a_im[None] * dt[None] * steps)
    pow_im = pmag * jnp.sin(a_im[None] * dt[None] * steps)
    a_b_re = jnp.broadcast_to(abar_re, (CHUNK, bsz, g, SSM_STATE))
    a_b_im = jnp.broadcast_to(abar_im, (CHUNK, bsz, g, SSM_STATE))
    uc = u.reshape(bsz, n_chunks, CHUNK, g, p).transpose(1, 2, 0, 3, 4)

    def step(carry, u_chunk):
        h_re, h_im = carry
        bu_re = jnp.einsum('lbgp,gnp->lbgn', u_chunk, bbar_re)
        bu_im = jnp.einsum('lbgp,gnp->lbgn', u_chunk, bbar_im)
        _, _, s_re, s_im = lax.associative_scan(
            _complex_affine_combine, (a_b_re, a_b_im, bu_re, bu_im), axis=0)
        hr = s_re + pow_re[:, None] * h_re[None] - pow_im[:, None] * h_im[None]
        hi = s_im + pow_re[:, None] * h_im[None] + pow_im[:, None] * h_re[None]
        y = (jnp.einsum('lbgn,gpn->lbgp', hr, c_re)
             - jnp.einsum('lbgn,gpn->lbgp', hi, c_im))
        return (hr[-1], hi[-1]), y

    init = (jnp.zeros((bsz, g, SSM_STATE), f32), jnp.zeros((bsz, g, SSM_STATE), f32))
    _, ys = lax.scan(step, init, uc)
    return ys.transpose(2, 0, 1, 3, 4).reshape(bsz, seq, g, p)


def s5_mixer(h, w_in, a_re, a_im, log_dt, b_re, b_im, c_re, c_im, d, w_gate, w_out):
    bsz, seq, _ = h.shape
    u = (h @ w_in).reshape(bsz, seq, SSM_GROUPS, SSM_GROUP).astype(jnp.float32)
    y = s5_scan(u, a_re, a_im, log_dt, b_re, b_im, c_re, c_im) + d.astype(jnp.float32) * u
    z = jax.nn.gelu(y.reshape(bsz, seq, D_MODEL)).astype(h.dtype)
    z = z * jax.nn.sigmoid(z @ w_gate)
    return z @ w_out


def setup_inputs(seed: int = 0) -> dict:
    key = jax.random.key(seed)
    ks = iter(jax.random.split(key, 32))
    f32 = jnp.float32
    D, F = D_MODEL, D_FF
    NA, NS, G, N, P = N_ATTN_LAYERS, N_SSM_LAYERS, SSM_GROUPS, SSM_STATE, SSM_GROUP
    nrm = lambda shape, std: std * jax.random.normal(next(ks), shape, f32)
    x = nrm((BATCH, SEQ, D), 1.0)
    c = nrm((BATCH, D), 1.0)
    ada_w = nrm((DEPTH, D, 9 * D), 0.5 * D ** -0.5)
    ada_b = nrm((DEPTH, 9 * D), 0.01)
    ln_g = 1.0 + nrm((DEPTH, 3, D), 0.02)
    ln_b = nrm((DEPTH, 3, D), 0.02)
    ffn_w1 = nrm((DEPTH, 2, D, F), D ** -0.5)
    ffn_w3 = nrm((DEPTH, 2, D, F), D ** -0.5)
    ffn_w2 = nrm((DEPTH, 2, F, D), BETA * F ** -0.5)
    qk = nrm((NA, D, 2 * D), D ** -0.5)
    vv = nrm((NA, D, D), BETA * D ** -0.5)
    attn_w_in = jnp.concatenate([qk, vv], axis=-1)
    attn_lam = nrm((NA, 4, ATTN_HEAD_DIM), 0.1)
    attn_subln_g = 1.0 + nrm((NA, 2 * ATTN_HEAD_DIM), 0.02)
    attn_w_out = nrm((NA, D, D), BETA * D ** -0.5)
    ssm_w_in = nrm((NS, D, D), D ** -0.5)
    ssm_a_re = -0.5 + nrm((NS, G, N), 0.01)
    ssm_a_im = math.pi * jnp.arange(N, dtype=f32)[None, None, :] + nrm((NS, G, N), 0.01)
    ssm_log_dt = jax.random.uniform(next(ks), (NS, G), f32,
                                    minval=math.log(DT_MIN), maxval=math.log(DT_MAX))
    ssm_b_re = nrm((NS, G, N, P), (2 * P) ** -0.5)
    ssm_b_im = nrm((NS, G, N, P), (2 * P) ** -0.5)
    ssm_c_re = nrm((NS, G, P, N), (2 * N) ** -0.5)
    ssm_c_im = nrm((NS, G, P, N), (2 * N) ** -0.5)
    ssm_d = nrm((NS, G, P), 1.0)
    ssm_w_gate = nrm((NS, D, D), D ** -0.5)
    ssm_w_out = nrm((NS, D, D), BETA * D ** -0.5)
    return {"x": x, "c": c, "ada_w": ada_w, "ada_b": ada_b, "ln_g": ln_g, "ln_b": ln_b,
            "ffn_w1": ffn_w1, "ffn_w3": ffn_w3, "ffn_w2": ffn_w2,
            "attn_w_in": attn_w_in, "attn_lam": attn_lam, "attn_subln_g": attn_subln_g,
            "attn_w_out": attn_w_out, "ssm_w_in": ssm_w_in, "ssm_a_re": ssm_a_re,
            "ssm_a_im": ssm_a_im, "ssm_log_dt": ssm_log_dt, "ssm_b_re": ssm_b_re,
            "ssm_b_im": ssm_b_im, "ssm_c_re": ssm_c_re, "ssm_c_im": ssm_c_im,
            "ssm_d": ssm_d, "ssm_w_gate": ssm_w_gate, "ssm_w_out": ssm_w_out}


def reference(x, c, ada_w, ada_b, ln_g, ln_b, ffn_w1, ffn_w3, ffn_w2,
              attn_w_in, attn_lam, attn_subln_g, attn_w_out,
              ssm_w_in, ssm_a_re, ssm_a_im, ssm_log_dt, ssm_b_re, ssm_b_im,
              ssm_c_re, ssm_c_im, ssm_d, ssm_w_gate, ssm_w_out):
    bsz = x.shape[0]
    cond = jax.nn.silu(c)
    for layer in range(DEPTH):
        mods = (cond @ ada_w[layer] + ada_b[layer]).reshape(bsz, 3, 3, D_MODEL)
        shift, scale, gate = mods[:, :, 0], mods[:, :, 1], 1.0 + mods[:, :, 2]
        h = modulate(x, shift[:, 0], scale[:, 0])
        y = 0.5 * swiglu(h, ffn_w1[layer, 0], ffn_w3[layer, 0], ffn_w2[layer, 0])
        x = post_norm_residual(x, y, gate[:, 0], ln_g[layer, 0], ln_b[layer, 0])
        h = modulate(x, shift[:, 1], scale[:, 1])
        i = layer // N_MIXERS
        if layer % N_MIXERS == 0:
            lam_init = 0.8 - 0.6 * math.exp(-0.3 * layer)
            y = diff_attention(h, attn_w_in[i], attn_lam[i], attn_subln_g[i],
                               attn_w_out[i], lam_init)
        else:
            y = s5_mixer(h, ssm_w_in[i], ssm_a_re[i], ssm_a_im[i], ssm_log_dt[i],
                         ssm_b_re[i], ssm_b_im[i], ssm_c_re[i], ssm_c_im[i], ssm_d[i],
                         ssm_w_gate[i], ssm_w_out[i])
        x = post_norm_residual(x, y, gate[:, 1], ln_g[layer, 1], ln_b[layer, 1])
        h = modulate(x, shift[:, 2], scale[:, 2])
        y = 0.5 * swiglu(h, ffn_w1[layer, 1], ffn_w3[layer, 1], ffn_w2[layer, 1])
        x = post_norm_residual(x, y, gate[:, 2], ln_g[layer, 2], ln_b[layer, 2])
    return x
```

```python
import math
from contextlib import ExitStack

import numpy as np
import concourse.bass as bass
import concourse.mybir as mybir
from concourse.bass_utils import run_bass_kernel_spmd

F32 = mybir.dt.float32
BF16 = mybir.dt.bfloat16
AF = mybir.ActivationFunctionType
ALU = mybir.AluOpType

D = 1024
S = 2048
NB = 8
DFF = 2816
DC = D // 128
FC = DFF // 128
TT = 512
DEPTH = 2
ALPHA = (2 * DEPTH) ** 0.25
LN_EPS = 1e-5
ENGS = ("pe", "act", "dve", "pool", "sp")
DEBUG = False
MERGE_EXP = True


class Buf:
    __slots__ = ("name", "lw", "rs")

    def __init__(self, name):
        self.name = name
        self.lw = None
        self.rs = []


class Ins:
    __slots__ = ("eng", "fn", "deps", "signal", "cnt", "dma", "dsem", "dval", "dprev", "waits")

    def __init__(self, eng, fn, dma):
        self.eng = eng
        self.fn = fn
        self.dma = dma
        self.deps = []
        self.signal = False
        self.cnt = 0
        self.dsem = None
        self.dval = 0
        self.dprev = 0
        self.waits = []


class Sched:
    def __init__(self, n_dma_slots=28):
        self.streams = {e: [] for e in ENGS}
        self.n_dma_slots = n_dma_slots
        self.dma_rr = {e: 0 for e in ENGS}
        self.dma_cum = {}
        self.dma_last = {}
        self.dma_recent = {}
        self.bar = {e: [] for e in ENGS}
        self.bufs = {}

    def buf(self, name):
        b = self.bufs.get(name)
        if b is None:
            b = self.bufs[name] = Buf(name)
        return b

    def op(self, eng, fn, reads=(), writes=(), dma=False):
        ins = Ins(eng, fn, dma)
        deps = set()
        if self.bar[eng]:
            deps.update(self.bar[eng])
            self.bar[eng] = []
        for b in reads:
            if isinstance(b, str):
                b = self.buf(b)
            if b.lw is not None:
                deps.add(b.lw)
        wl = []
        for b in writes:
            if isinstance(b, str):
                b = self.buf(b)
            wl.append(b)
            deps.update(b.rs)
            if b.lw is not None:
                deps.add(b.lw)
        for b in reads:
            if isinstance(b, str):
                b = self.buf(b)
            b.rs.append(ins)
        for b in wl:
            b.lw = ins
            b.rs = []
        if dma:
            slot = self.dma_rr[eng]
            self.dma_rr[eng] = (slot + 1) % self.n_dma_slots
            key = (eng, slot)
            ins.dsem = key
            ins.dprev = self.dma_cum.get(key, 0)
            ins.dval = ins.dprev + 16
            self.dma_cum[key] = ins.dval
            self.dma_last[key] = ins
            self.dma_recent[key] = ins
        ins.deps = [d for d in deps if d is not ins]
        self.streams[eng].append(ins)
        return ins

    def barrier(self):
        deps = []
        for e in ENGS:
            if self.streams[e]:
                deps.append(self.streams[e][-1])
        deps.extend(self.dma_recent.values())
        self.dma_recent = {}
        nop = self.op("sp", lambda eng: eng.nop(nofuse=True))
        nop.deps = list(set(nop.deps) | set(deps))
        for e in ENGS:
            if e != "sp":
                self.bar[e] = [nop]
        return nop

    def finalize(self):
        for e in ENGS:
            for ins in self.streams[e]:
                for d in ins.deps:
                    if not d.dma:
                        if d.eng == "pe" and ins.eng == "pe":
                            continue
                        d.signal = True
        for e in ENGS:
            c = 0
            for ins in self.streams[e]:
                if ins.signal:
                    c += 1
                    ins.cnt = c
        for e in ENGS:
            known = {}
            for ins in self.streams[e]:
                need = {}
                for d in ins.deps:
                    if d.dma:
                        k = ("dma",) + d.dsem
                        v = d.dval
                    else:
                        if d.eng == "pe" and e == "pe":
                            continue
                        k = ("eng", d.eng)
                        v = d.cnt
                    if v > need.get(k, 0):
                        need[k] = v
                if ins.dma and ins.dprev > 0:
                    k = ("dma",) + ins.dsem
                    if ins.dprev > need.get(k, 0):
                        need[k] = ins.dprev
                ws = []
                for k, v in need.items():
                    if known.get(k, 0) >= v:
                        continue
                    known[k] = v
                    ws.append((k, v))
                ins.waits = ws

    def emit(self, nc, final_waits):
        self.finalize()
        with ExitStack() as es:
            esem = {e: es.enter_context(nc.semaphore("s_" + e)) for e in ENGS}
            dsem = {}
            for key in self.dma_cum:
                dsem[key] = es.enter_context(nc.semaphore("d_%s_%d" % key))
            block = es.enter_context(nc.Block())

            def run(engname):
                def body(eng):
                    for ins in self.streams[engname]:
                        for k, v in ins.waits:
                            if k[0] == "eng":
                                eng.wait_ge(esem[k[1]], v)
                            else:
                                eng.wait_ge(dsem[(k[1], k[2])], v)
                        r = ins.fn(eng)
                        if ins.dma:
                            r.then_inc(dsem[ins.dsem], 16)
                        elif ins.signal:
                            r.then_inc(esem[engname], 1)
                    if engname == "sp":
                        for d in final_waits:
                            eng.wait_ge(dsem[d.dsem], d.dval)
                return body

            block.tensor(run("pe"))
            block.scalar(run("act"))
            block.vector(run("dve"))
            block.gpsimd(run("pool"))
            block.sync(run("sp"))


class Prog:
    def __init__(self, subs, first_in_raw, last_out_raw):
        self.subs = subs
        self.nc = bass.Bass("TRN2", target_bir_lowering=False)
        self.sc = Sched()
        self.es = ExitStack()
        self.first_in_raw = first_in_raw
        self.last_out_raw = last_out_raw
        self.dram = {}

    def din(self, name, shape, dt=F32):
        t = self.nc.dram_tensor(name, list(shape), dt, kind="ExternalInput").ap()
        self.dram[name] = t
        return t

    def dout(self, name, shape, dt=F32):
        t = self.nc.dram_tensor(name, list(shape), dt, kind="ExternalOutput").ap()
        self.dram[name] = t
        return t

    def sb(self, name, shape, dt):
        return self.es.enter_context(self.nc.sbuf_tensor(name, list(shape), dt))

    def ps(self, name, shape, dt=F32):
        return self.es.enter_context(self.nc.psum_tensor(name, list(shape), dt))

    def op(self, eng, fn, reads=(), writes=(), dma=False):
        return self.sc.op(eng, fn, reads, writes, dma)

    def dump(self, name, ap, reads, n):
        if not getattr(self, "dbg", False):
            return
        kind = "f" if ap.dtype == F32 else "b"
        o = self.dbg_off[kind]
        self.dbg_off[kind] = o + n
        self.dbg_map[name] = (kind, o, n)
        dst = (self.dbgf if kind == "f" else self.dbgb)[:, o:o + n]
        d = self.dma("sp", dst, ap, reads, [])
        self.out_dmas.append(d)

    def mm(self, out, lhsT, rhs, start, stop, reads, writes, **kw):
        return self.op("pe", lambda e: e.matmul(out, lhsT, rhs, start=start, stop=stop, **kw),
                       reads, writes)

    def act(self, out, in_, func, reads, writes, bias=None, scale=None, eng="act"):
        kw = {}
        if bias is not None:
            kw["bias"] = bias
        if scale is not None:
            kw["scale"] = scale
        return self.op(eng, lambda e: e.activation(out, in_, func, **kw), reads, writes)

    def ts(self, eng, out, in0, s1, s2, op0, op1, reads, writes):
        if s2 is None:
            return self.op(eng, lambda e: e.tensor_scalar(out, in0, s1, None, op0), reads, writes)
        return self.op(eng, lambda e: e.tensor_scalar(out, in0, s1, s2, op0, op1), reads, writes)

    def tt(self, eng, out, in0, in1, op, reads, writes):
        return self.op(eng, lambda e: e.tensor_tensor(out, in0, in1, op), reads, writes)

    def stt(self, eng, out, in0, scalar, in1, op0, op1, reads, writes):
        return self.op(eng, lambda e: e.scalar_tensor_tensor(out, in0, scalar, in1, op0, op1),
                       reads, writes)

    def dma(self, eng, out, in_, reads, writes, **kw):
        return self.op(eng, lambda e: e.dma_start(out=out, in_=in_, **kw), reads, writes, dma=True)


def _col(t, j):
    return t[:, j:j + 1]


class Builder(Prog):
    def __init__(self, subs):
        super().__init__(subs, True, True)
        self.layers = sorted(set(s // 3 for s in subs))
        self.declare_io()
        self.alloc_common()

    def declare_io(self):
        self.xT = self.din("xT", [D, S])
        self.cvec = self.din("cvec", [128, DC])
        self.ada_r = self.din("ada_r", [2, 18, 128, DC * 512])
        self.ada_b = self.din("ada_b", [128, 2 * 72])
        self.ln_g = self.din("ln_g", [128, 6 * DC])
        self.ln_b = self.din("ln_b", [128, 6 * DC])
        self.w1r = self.din("w1r", [2, 2, 11, 128, DC * 256])
        self.w3r = self.din("w3r", [2, 2, 11, 128, DC * 256])
        self.w2r = self.din("w2r", [2, 2, 4, 128, FC * 256])
        self.wqkv_r = self.din("wqkv_r", [8, 128, 3072])
        self.wout_r = self.din("wout_r", [4, 128, 8 * 256])
        self.lam_b = self.din("lam_b", [128, 256])
        self.subln_b = self.din("subln_b", [128, 128])
        self.ident = self.din("ident", [128, 128])
        self.s5_are = self.din("s5_are", [128, 32])
        self.s5_aim = self.din("s5_aim", [128, 32])
        self.s5_ldt = self.din("s5_ldt", [128, 32])
        self.s5_dcol = self.din("s5_dcol", [128, 32])
        self.s5_br = self.din("s5_br", [128, 512])
        self.s5_bi = self.din("s5_bi", [128, 512])
        self.s5_cr = self.din("s5_cr", [128, 512])
        self.s5_ci = self.din("s5_ci", [128, 512])
        self.s5_mask = self.din("s5_mask", [128, 128])
        self.s5_sel = self.din("s5_sel", [128, 16 * 128])
        self.s5_win = self.din("s5_win", [4, 128, DC * 256])
        self.s5_sel2 = self.din("s5_sel2", [128, 128])
        self.s5_wg = self.din("s5_wg", [4, 128, DC * 256])
        self.s5_wo = self.din("s5_wo", [4, 128, DC * 256])
        self.yT = self.dout("yT", [D, S])
        self.dbg = DEBUG
        if self.dbg:
            self.dbgf = self.dout("dbgf", [128, 16384])
            self.dbgb = self.dout("dbgb", [128, 65536], BF16)
            self.dbg_off = {"f": 0, "b": 0}
            self.dbg_map = {}

    def alloc_common(self):
        self.x = self.sb("x", [128, DC, S], F32)
        self.h = self.sb("h", [128, DC, S], BF16)
        self.onesb = self.sb("onesb", [128, 128], BF16)
        self.sel2_sb = self.sb("sel2_sb", [128, 128], BF16)
        self.mods = self.sb("mods", [128, 2 * 72], F32)
        self.adab = self.sb("adab", [128, 2 * 72], F32)
        self.lng = self.sb("lng", [128, 6 * DC], F32)
        self.lnb = self.sb("lnb", [128, 6 * DC], F32)
        self.cond = self.sb("cond", [128, DC], F32)
        self.condb = self.sb("condb", [128, DC], BF16)
        self.coef = self.sb("coef", [128, 6, 8 * DC], F32)
        self.psum = self.ps("psum", [128, 8, TT], F32)
        self.bank = [self.psum[:, i, :] for i in range(8)]
        self.arena_b = self.sb("arena_b", [128, 44 * 1024], BF16)
        self.arena_f = self.sb("arena_f", [128, 5 * 1024], F32)

    def xt(self, dc, tt):
        return self.x[:, dc, tt * TT:(tt + 1) * TT]

    def ht(self, dc, tt):
        return self.h[:, dc, tt * TT:(tt + 1) * TT]

    def prologue(self):
        p = self
        p.op("pool", lambda e: e.memset(p.onesb[:], 1.0 / D), (), ["onesb"])
        p.dma("sp", p.cond[:], p.cvec, (), ["cond"])
        p.dma("sp", p.adab[:], p.ada_b, (), ["adab"])
        p.dma("sp", p.lng[:], p.ln_g, (), ["lng"])
        p.dma("sp", p.lnb[:], p.ln_b, (), ["lnb"])
        for dc in range(DC):
            for hf in range(2):
                p.dma("sp", p.x[:, dc, hf * 1024:(hf + 1) * 1024],
                      p.xT[dc * 128:(dc + 1) * 128, hf * 1024:(hf + 1) * 1024],
                      (), ["x_%d_%d" % (dc, 2 * hf), "x_%d_%d" % (dc, 2 * hf + 1)])
        p.act(p.cond[:], p.cond[:], AF.Silu, ["cond"], ["cond"])
        p.op("dve", lambda e: e.tensor_copy(p.condb[:], p.cond[:]), ["cond"], ["condb"])
        adabuf = [p.arena_b[:, i * 4096:(i + 1) * 4096] for i in range(2)]
        mps = p.bank[7]
        n = 0
        self.ada_small = []
        stream_rest = (self.subs[0] % 3 != 1)
        adasm = [p.arena_f[:, 4096 + i * 512:4096 + (i + 1) * 512].bitcast(BF16) for i in range(2)]
        st = {"n": 0}

        def small(L, blk, cc):
            def emit():
                bi = st["n"] % 2
                st["n"] += 1
                src = p.ada_r[L, blk].rearrange("p (k c) -> p k c", c=512)[:, :, cc * 128:(cc + 1) * 128]
                dst = adasm[bi].rearrange("p (k c) -> p k c", c=128)
                p.dma("pool", dst, src, (), ["adasm%d" % bi])
                col = L * 72 + blk * 4 + cc
                for k in range(DC):
                    p.mm(mps[:, col:col + 1], adasm[bi][:, k * 128:(k + 1) * 128], p.condb[:, k:k + 1],
                         k == 0, k == DC - 1, ["adasm%d" % bi, "condb"], ["bank7"])
                p.tt("dve", p.mods[:, col:col + 1], mps[:, col:col + 1], p.adab[:, col:col + 1], ALU.add,
                     ["bank7", "adab"], ["mods"])
            return emit

        for L in self.layers:
            for blk in range(18):
                if stream_rest and not (L == self.layers[0] and blk < 4):
                    for cc in range(4):
                        self.ada_small.append(small(L, blk, cc))
                    continue
                bi = n % 2
                n += 1
                p.dma("pool", adabuf[bi], p.ada_r[L, blk], (), ["adabuf%d" % bi],
                      max_dma_last_dim=4096)
                for cc in range(4):
                    col = L * 72 + blk * 4 + cc
                    for k in range(DC):
                        p.mm(mps[:, col:col + 1],
                             adabuf[bi][:, k * 512 + cc * 128:k * 512 + (cc + 1) * 128],
                             p.condb[:, k:k + 1], k == 0, k == DC - 1,
                             ["adabuf%d" % bi, "condb"], ["bank7"])
                c0 = L * 72 + blk * 4
                p.tt("dve", p.mods[:, c0:c0 + 4], mps[:, c0:c0 + 4], p.adab[:, c0:c0 + 4], ALU.add,
                     ["bank7", "adab"], ["mods"])

    def mod_cols(self, s, kind):
        L, sub = divmod(s, 3)
        c0 = L * 72 + (sub * 3 + kind) * DC
        return self.mods[:, c0:c0 + DC]

    def ln_cols(self, s):
        return self.lng[:, s * DC:(s + 1) * DC], self.lnb[:, s * DC:(s + 1) * DC]

    def prep_coefs(self, s, has_next, gate_mult):
        p = self
        cf = p.coef[:, s]
        sl = lambda i: cf[:, i * DC:(i + 1) * DC]
        g, b = p.ln_cols(s)
        w = ["coef%d" % s]
        r = ["mods", "lng", "lnb"]
        p.ts("dve", sl(4), p.mod_cols(s, 2), 1.0, gate_mult, ALU.add, ALU.mult, r, w)
        if has_next:
            p.ts("dve", sl(5), p.mod_cols(s + 1, 1), 1.0, None, ALU.add, None, r, w)
            p.tt("dve", sl(0), g, sl(5), ALU.mult, r + w, w)
            p.tt("dve", sl(1), b, sl(5), ALU.mult, r + w, w)
            p.tt("dve", sl(1), sl(1), p.mod_cols(s + 1, 0), ALU.add, r + w, w)
            p.ts("dve", sl(2), g, ALPHA, None, ALU.mult, None, r, w)
            p.ts("dve", sl(3), b, ALPHA, None, ALU.mult, None, r, w)
        else:
            p.op("dve", lambda e: e.tensor_copy(sl(2), g), r, w)
            p.op("dve", lambda e: e.tensor_copy(sl(3), b), r, w)

    def first_modulate(self, s):
        p = self
        cf = p.coef[:, s]
        tmp = cf[:, 6 * DC:7 * DC]
        p.ts("dve", tmp, p.mod_cols(s, 1), 1.0, None, ALU.add, None, ["mods"], ["coef%d" % s])
        for tt in range(4):
            for dc in range(DC):
                xb = "x_%d_%d" % (dc, tt)
                p.act(p.ht(dc, tt), p.xt(dc, tt), AF.Identity, [xb, "coef%d" % s, "mods"],
                      ["h_%d_%d" % (dc, tt)], bias=_col(p.mod_cols(s, 0), dc), scale=_col(tmp, dc))
                p.ts("dve", p.xt(dc, tt), p.xt(dc, tt), ALPHA, None, ALU.mult, None, [xb], [xb])

    def ln_setup(self):
        af = self.arena_f
        self.mean_sb = [af[:, i * TT:(i + 1) * TT] for i in range(2)]
        self.rstd_sb = [af[:, (2 + i) * TT:(3 + i) * TT] for i in range(2)]
        self.tsc = [af[:, (4 + i) * TT:(5 + i) * TT] for i in range(2)]
        self.silu_sb = [af[:, (6 + i) * TT:(7 + i) * TT] for i in range(2)]
        ab = self.arena_b
        o = 44 * 1024
        self.zb = [ab[:, o - (i + 1) * TT:o - i * TT] for i in range(6)]
        self._zrot = 0
        self._trot = 0
        self._nw2 = 0
        self._ny = 0

    def resid_evac(self, s, y_ps, ybuf, dc, tt, stat_banks, pend):
        p = self
        cf = p.coef[:, s]
        gate = cf[:, 4 * DC + dc:4 * DC + dc + 1]
        xb = "x_%d_%d" % (dc, tt)
        p.stt("dve", p.xt(dc, tt), y_ps, gate, p.xt(dc, tt), ALU.mult, ALU.add,
              [ybuf, xb, "coef%d" % s], [xb])
        r = p._zrot % 3
        p._zrot += 1
        zb, zq = p.zb[r], p.zb[3 + r]
        p.op("act", lambda e: e.copy(zb, p.xt(dc, tt)), [xb], ["zb%d" % r])
        p.act(zq, p.xt(dc, tt), AF.Square, [xb], ["zq%d" % r])
        mb, qb = stat_banks

        def stats():
            p.mm(p.bank[mb][:], p.onesb[:], zb, dc == 0, dc == DC - 1,
                 ["onesb", "zb%d" % r], ["bank%d" % mb])
            p.mm(p.bank[qb][:], p.onesb[:], zq, dc == 0, dc == DC - 1,
                 ["onesb", "zq%d" % r], ["bank%d" % qb])
        pend.append(stats)

    def ln_finalize(self, s, tt, stat_banks, has_next, store):
        for piece in self.ln_pieces(s, tt, stat_banks, has_next, store):
            piece()

    def ln_pieces(self, s, tt, stat_banks, has_next, store):
        p = self
        mb, qb = stat_banks
        par = tt % 2
        mean, rstd = p.mean_sb[par], p.rstd_sb[par]
        mB, rB = "mean%d" % par, "rstd%d" % par
        cf = p.coef[:, s]
        cB = "coef%d" % s
        pieces = []

        def head():
            p.op("act", lambda e: e.copy(mean, p.bank[mb][:]), ["bank%d" % mb], [mB])
            p.tt("dve", rstd, mean, mean, ALU.mult, [mB], [rB])
            p.tt("dve", rstd, p.bank[qb][:], rstd, ALU.subtract, ["bank%d" % qb, rB], [rB])
            p.ts("dve", rstd, rstd, LN_EPS, None, ALU.add, None, [rB], [rB])
            p.act(rstd, rstd, AF.Sqrt, [rB], [rB])
            p.op("dve", lambda e: e.reciprocal(rstd, rstd), [rB], [rB])
        pieces.append(head)
        for dc in range(DC):
            def body(dc=dc):
                xb = "x_%d_%d" % (dc, tt)
                r = p._trot % 2
                p._trot += 1
                t = p.tsc[r]
                tB = "tsc%d" % r
                p.tt("dve", t, p.xt(dc, tt), mean, ALU.subtract, [xb, mB], [tB])
                p.tt("dve", t, t, rstd, ALU.mult, [tB, rB], [tB])
                p.act(p.xt(dc, tt), t, AF.Identity, [tB, cB], [xb],
                      bias=cf[:, 3 * DC + dc:3 * DC + dc + 1], scale=cf[:, 2 * DC + dc:2 * DC + dc + 1])
                if has_next:
                    p.act(p.ht(dc, tt), t, AF.Identity, [tB, cB], ["h_%d_%d" % (dc, tt)],
                          bias=cf[:, 1 * DC + dc:1 * DC + dc + 1], scale=cf[:, 0 * DC + dc:0 * DC + dc + 1])
                if store:
                    d = p.dma("sp", p.yT[dc * 128:(dc + 1) * 128, tt * TT:(tt + 1) * TT], p.xt(dc, tt),
                              [xb], [])
                    p.out_dmas.append(d)
            pieces.append(body)
        return pieces

    def flush_deferred(self):
        while self.deferred:
            self.deferred.pop(0)()

    def ffn(self, s, has_next, store):
        p = self
        L, sub = divmod(s, 3)
        j = 0 if sub == 0 else 1
        p.ln_setup()
        late_coefs = bool(p.ada_small)
        if not late_coefs:
            p.prep_coefs(s, has_next, 0.5)
        ab = p.arena_b
        g = ab[:, 0:22 * 1024]
        w1b = [ab[:, 22528 + i * 2048:22528 + (i + 1) * 2048] for i in range(2)]
        w3b = [ab[:, 26624 + i * 2048:26624 + (i + 1) * 2048] for i in range(2)]
        w2b = [ab[:, 30720 + i * 5632:30720 + (i + 1) * 5632] for i in range(2)]
        cB = "coef%d" % s
        nset = 0
        nw = 0
        for hf in range(2):
            tts = (2 * hf, 2 * hf + 1)
            for fblk in range(11):
                bi = nw % 2
                nw += 1
                p.dma("pool", w1b[bi], p.w1r[L, j, fblk], (), ["w1b%d" % bi], max_dma_last_dim=4096)
                p.dma("pool", w3b[bi], p.w3r[L, j, fblk], (), ["w3b%d" % bi], max_dma_last_dim=4096)
                for fi in range(2):
                    f = fblk * 2 + fi
                    for tt in tts:
                        st = nset % 2
                        nset += 1
                        ub, vb = 2 * st, 2 * st + 1
                        for dc in range(DC):
                            p.mm(p.bank[ub][:], w1b[bi][:, dc * 256 + fi * 128:dc * 256 + (fi + 1) * 128],
                                 p.ht(dc, tt), dc == 0, dc == DC - 1,
                                 ["w1b%d" % bi, "h_%d_%d" % (dc, tt)], ["bank%d" % ub])
                        for dc in range(DC):
                            p.mm(p.bank[vb][:], w3b[bi][:, dc * 256 + fi * 128:dc * 256 + (fi + 1) * 128],
                                 p.ht(dc, tt), dc == 0, dc == DC - 1,
                                 ["w3b%d" % bi, "h_%d_%d" % (dc, tt)], ["bank%d" % vb])
                        sl = p.silu_sb[st]
                        p.act(sl, p.bank[ub][:], AF.Silu, ["bank%d" % ub], ["silu%d" % st])
                        tl = (tt - 2 * hf) * TT
                        gt = g[:, f * 1024 + tl:f * 1024 + tl + TT]
                        p.tt("dve", gt, sl, p.bank[vb][:], ALU.mult,
                             ["silu%d" % st, "bank%d" % vb], ["g_%d_%d" % (f, tt % 2)])
                        if p.deferred and nset % 2 == 0:
                            p.deferred.pop(0)()
                        for _ in range(3):
                            if p.ada_small:
                                p.ada_small.pop(0)()
            p.flush_deferred()
            if late_coefs:
                while p.ada_small:
                    p.ada_small.pop(0)()
                p.prep_coefs(s, has_next, 0.5)
                late_coefs = False
            p.proj_resid_ln(s, hf, FC, lambda f, tt: (g[:, f * 1024 + (tt % 2) * TT:f * 1024 + (tt % 2) * TT + TT],
                                                      "g_%d_%d" % (f, tt % 2)),
                            w2b, lambda dblk: p.w2r[L, j, dblk], "w2b", has_next, store, defer=True)

    def proj_resid_ln(self, s, hf, nk, rhs_fn, wb, wdram, wname, has_next, store, tts=None, defer=False):
        p = self
        if tts is None:
            tts = (2 * hf, 2 * hf + 1)
        pend = []
        for dblk in range(4):
            bi = p._nw2 % 2
            p._nw2 += 1
            p.dma("pool", wb[bi], wdram(dblk), (), ["%s%d" % (wname, bi)], max_dma_last_dim=4096)
            for di in range(2):
                dc = dblk * 2 + di
                for tt in tts:
                    yb = p._ny % 2
                    p._ny += 1
                    for k in range(nk):
                        rhs, rb = rhs_fn(k, tt)
                        p.mm(p.bank[yb][:], wb[bi][:, k * 256 + di * 128:k * 256 + (di + 1) * 128],
                             rhs, k == 0, k == nk - 1,
                             ["%s%d" % (wname, bi), rb], ["bank%d" % yb])
                    while pend:
                        pend.pop(0)()
                    sbk = (4 + 2 * (tt % 2), 5 + 2 * (tt % 2))
                    p.resid_evac(s, p.bank[yb][:], "bank%d" % yb, dc, tt, sbk, pend)
        while pend:
            pend.pop(0)()
        for tt in tts:
            sbk = (4 + 2 * (tt % 2), 5 + 2 * (tt % 2))
            if defer:
                p.deferred.extend(p.ln_pieces(s, tt, sbk, has_next, store))
            else:
                p.ln_finalize(s, tt, sbk, has_next, store)


    def attn(self, s, has_next, store):
        p = self
        lam_init = 0.8 - 0.6 * math.exp(-0.3 * (s // 3))
        p.ln_setup()
        p.prep_coefs(s, has_next, 1.0)
        ab, af = p.arena_b, p.arena_f
        oT = ab[:, 0:16384]
        qk = [ab[:, 16384 + i * 6144:16384 + (i + 1) * 6144] for i in range(2)]
        vh = [ab[:, 28672 + i * 2064:28672 + (i + 1) * 2064] for i in range(2)]
        wq = [ab[:, 32800 + i * 3072:32800 + (i + 1) * 3072] for i in range(2)]
        pT = [ab[:, 38944 + i * 512:38944 + (i + 1) * 512] for i in range(5)]
        onb = [ab[:, 41504 + i * 128:41504 + (i + 1) * 128] for i in range(4)]
        identb = ab[:, 42016:42144]
        fo = 4096
        NACC = 8
        accs = [af[:, i * 258:(i + 1) * 258] for i in range(NACC)]
        junk = af[:, 2304:2432]
        gvec = af[:, fo + 516:fo + 644]
        lamt = af[:, fo + 644:fo + 900]
        sm = af[:, fo + 900:fo + 1024]
        nlam = sm[:, 0:1]
        p.dma("pool", identb, p.ident, (), ["identb"])
        p.dma("sp", lamt, p.lam_b, (), ["lamt"])
        p.dma("sp", gvec, p.subln_b, (), ["gvec"])
        p.ts("dve", gvec, gvec, 1.0 - lam_init, None, ALU.mult, None, ["gvec"], ["gvec"])
        p.tt("dve", lamt[:, 0:64], lamt[:, 0:64], lamt[:, 64:128], ALU.mult, ["lamt"], ["lamt"])
        p.tt("dve", lamt[:, 128:192], lamt[:, 128:192], lamt[:, 192:256], ALU.mult, ["lamt"], ["lamt"])
        p.op("dve", lambda e: e.reduce_sum(sm[:, 1:2], lamt[:, 0:64], mybir.AxisListType.X), ["lamt"], ["sm"])
        p.op("dve", lambda e: e.reduce_sum(sm[:, 2:3], lamt[:, 128:192], mybir.AxisListType.X), ["lamt", "sm"], ["sm"])
        p.act(sm[:, 1:3], sm[:, 1:3], AF.Exp, ["sm"], ["sm"])
        p.tt("dve", nlam, sm[:, 2:3], sm[:, 1:2], ALU.subtract, ["sm"], ["sm"])
        p.ts("dve", nlam, nlam, -lam_init, None, ALU.add, None, ["sm"], ["sm"])
        for i in range(2):
            v3 = vh[i].rearrange("p (j e) -> p j e", e=129)
            p.op("pool", lambda e, v3=v3: e.memset(v3[:, :, 128:129], 1.0), (), ["vh%d" % i])
            p.op("pool", lambda e, i=i: e.memset(qk[i][64:128, 0:2048], 0.0), (), ["qk%dw0" % i])
            p.op("pool", lambda e, i=i: e.memset(qk[i][0:64, 2048:4096], 0.0), (), ["qk%dw0" % i])
        st_ = {"sc": 0, "pt": 0, "ep": 0, "pj": 0}

        def proj_groups(hh):
            par = hh % 2
            wB, qB, vB = "wq%d" % par, "qk%d" % par, "vh%d" % par
            groups = []
            def nextbank():
                b = (5, 7)[st_["pj"] % 2]
                st_["pj"] += 1
                return p.bank[b], "bank%d" % b

            def load():
                p.dma("pool", wq[par], p.wqkv_r[hh], (), [wB], max_dma_last_dim=4096)
            groups.append(load)
            for which in range(2):
                for tt in range(4):
                    def g(which=which, tt=tt):
                        pb, pbB = nextbank()
                        for dc in range(DC):
                            p.mm(pb[:], wq[par][:, (which * 8 + dc) * 128:(which * 8 + dc + 1) * 128],
                                 p.ht(dc, tt), dc == 0, dc == DC - 1, [wB, "h_%d_%d" % (dc, tt)], [pbB])
                        if which == 1:
                            dst = qk[par][:, 4096 + tt * TT:4096 + (tt + 1) * TT]
                            p.op("dve", lambda e, dst=dst: e.tensor_copy(dst, pb[:]), [pbB], [qB + "w1"])
                        else:
                            d0 = qk[par][0:64, tt * TT:(tt + 1) * TT]
                            d1 = qk[par][64:128, 2048 + tt * TT:2048 + (tt + 1) * TT]
                            p.op("dve", lambda e, d0=d0: e.tensor_copy(d0, pb[0:64, :]), [pbB], [qB + "w0"])
                            p.op("dve", lambda e, d1=d1: e.tensor_copy(d1, pb[64:128, :]), [pbB], [qB + "w0"])
                    groups.append(g)
            v3 = vh[par].rearrange("p (j e) -> p j e", e=129)
            for jg in range(4):
                def g(jg=jg):
                    pb, pbB = nextbank()
                    for jj in range(4):
                        jx = jg * 4 + jj
                        for dc in range(DC):
                            p.mm(pb[:, jj * 128:(jj + 1) * 128], p.h[:, dc, jx * 128:(jx + 1) * 128],
                                 wq[par][:, (16 + dc) * 128:(16 + dc + 1) * 128], dc == 0, dc == DC - 1,
                                 [wB, "h_%d_%d" % (dc, jx // 4)], [pbB])
                    src = pb[:].rearrange("p (j e) -> p j e", e=128)
                    p.op("dve", lambda e, src=src: e.tensor_copy(v3[:, jg * 4:(jg + 1) * 4, 0:128], src),
                         [pbB], [vB])
                groups.append(g)
            return groups

        def acc_ap(qi, m):
            return p.bank[qi][:, m * 129:m * 129 + 129], "bank%d" % qi

        later = []

        def tick():
            todo = list(later)
            del later[:]
            for item in todo:
                item[0] -= 1
                if item[0] <= 0:
                    item[1]()
                else:
                    later.append(item)

        trbk = p.bank[6][:].bitcast(BF16)

        def epilogue(hh, qb, qi):
            r = st_["ep"] % NACC
            st_["ep"] += 1
            r4 = r % 4
            Abank = p.bank[qi][:, 0:258]
            aB = "bank%d" % qi
            acc = accs[r]
            acB = "accs%d" % r
            c = sm[:, 8 + r * 4:12 + r * 4]
            cB = "smc%d" % r
            o = acc[:, 0:128]
            p.op("dve", lambda e: e.tensor_copy(acc, Abank), [aB], [acB])
            sums = acc.rearrange("p (m e) -> p m e", m=2)[:, :, 128:129]
            p.op("dve", lambda e: e.reciprocal(c[:, 0:2].unsqueeze(2), sums), [acB], [cB])
            p.tt("dve", c[:, 1:2], c[:, 1:2], nlam, ALU.mult, [cB, "sm"], [cB])

            def stage1():
                p.act(o, o, AF.Identity, [acB, cB], [acB], scale=c[:, 0:1])

            def stage2():
                p.stt("dve", o, acc[:, 129:257], c[:, 1:2], o, ALU.mult, ALU.add, [acB, cB], [acB])

            def stage3():
                p.op("act", lambda e: e.activation(junk, o, AF.Square, accum_out=c[:, 2:3]), [acB, cB], ["junk", cB])
                p.ts("dve", c[:, 2:3], c[:, 2:3], 1.0 / 128.0, LN_EPS, ALU.mult, ALU.add, [cB], [cB])

            def stage3b():
                p.act(c[:, 2:3], c[:, 2:3], AF.Ln, [cB], [cB])
                p.act(c[:, 2:3], c[:, 2:3], AF.Exp, [cB], [cB], scale=-0.5)

            def stage4():
                p.stt("dve", onb[r4], o, c[:, 2:3], gvec, ALU.mult, ALU.mult, [acB, cB, "gvec"], ["onb%d" % r4])

            tr = trbk[:, 0:128]

            def stage4b():
                p.op("pe", lambda e: e.transpose(tr, onb[r4], identb), ["onb%d" % r4, "identb"], ["bank6"])

            def stage5():
                dst = oT[:, hh * 2048 + qb * 128:hh * 2048 + (qb + 1) * 128]
                p.op("dve", lambda e: e.tensor_copy(dst, tr), ["bank6"], ["oT_%d_%d" % (hh, qb // 4)])
            later.append([1, stage1])
            later.append([2, stage2])
            later.append([3, stage3])
            later.append([4, stage3b])
            later.append([5, stage4])
            later.append([7, stage4b])
            later.append([8, stage5])

        SCB = ((2, 3), (4, 5))

        def core(hh, pending):
            par = hh % 2
            qB, vB = "qk%d" % par, "vh%d" % par
            q_ = qk[par]
            steps = [(QP, jx) for QP in range(8) for jx in range(2 * QP + 2)]

            def emit_scores(QP, jx):
                q0 = max(128 * jx, 256 * QP)
                q1 = 256 * (QP + 1)
                n = q1 - q0
                bb = 2 + st_["sc"] % 3
                st_["sc"] += 1
                pr = st_["pt"] % 5
                st_["pt"] += 1
                pt, ptB = pT[pr][:, 0:512], "pT%d" % pr
                for m in range(2):
                    p.mm(p.bank[bb][:, m * 256:m * 256 + n], q_[:, 4096 + 128 * jx:4096 + 128 * jx + 128],
                         q_[:, m * 2048 + q0:m * 2048 + q1], True, True, [qB + "w0", qB + "w1"], ["bank%d" % bb])
                pt3 = pt.rearrange("p (m q) -> p m q", m=2)
                sc3 = p.bank[bb][:].rearrange("p (m q) -> p m q", m=2)
                p.act(pt3[:, :, 0:n], sc3[:, :, 0:n], AF.Exp, ["bank%d" % bb], [ptB], scale=0.125)
                if jx >= 2 * QP:
                    p.op("pool", lambda e, pt3=pt3: e.memset(pt3[64:128, :, 0:64], 0.0), [ptB], [ptB])
                return (QP, jx, q0, pt, ptB)

            def emit_pv(ctx, bank_first):
                QP, jx, q0, pt, ptB = ctx
                for qi in range(2):
                    qb = 2 * QP + qi
                    if jx > qb:
                        continue
                    c0 = 128 * qb - q0
                    for m in range(2):
                        A, aB = acc_ap(qi, m)
                        first = bank_first[qi]
                        bank_first[qi] = False
                        p.mm(A, pt[:, m * 256 + c0:m * 256 + c0 + 128], vh[par][:, jx * 129:(jx + 1) * 129],
                             first, jx == qb, [ptB, vB], [aB], skip_group_check=True)
                    if jx == qb:
                        epilogue(hh, qb, qi)

            LOOK = 2
            ctxs = [emit_scores(*steps[k]) for k in range(LOOK)]
            bank_first = [True, True]
            for i, (QP, jx) in enumerate(steps):
                if i + LOOK < len(steps):
                    ctxs.append(emit_scores(*steps[i + LOOK]))
                ctx = ctxs.pop(0)
                if jx == 0:
                    bank_first = [True, True]
                emit_pv(ctx, bank_first)
                tick()
                if pending and (i + 1) % 5 == 0:
                    pending.pop(0)()
            while pending:
                pending.pop(0)()

        for g in proj_groups(0):
            g()
        for hh in range(8):
            pending = proj_groups(hh + 1) if hh + 1 < 8 else []
            core(hh, pending)
        for _ in range(10):
            tick()
        woutb = [wq[0][:, 0:2048], wq[1][:, 0:2048]]
        for hf in range(2):
            p.proj_resid_ln(s, hf, 8, lambda k, tt: (oT[:, k * 2048 + tt * TT:k * 2048 + (tt + 1) * TT],
                                                     "oT_%d_%d" % (k, tt)),
                            woutb, lambda dblk: p.wout_r[dblk], "wq", has_next, store)

    def s5(self, s, has_next, store):
        p = self
        LB = 4
        NBK = TT // LB
        TWO_PI = 2.0 * math.pi
        GK = 2.0 * math.sqrt(2.0 / math.pi)
        p.ln_setup()
        p.prep_coefs(s, has_next, 1.0)
        ab, af = p.arena_b, p.arena_f
        BBTr, BBTi, CCr, CCni, T0T = [ab[:, i * 4096:(i + 1) * 4096] for i in range(5)]
        selb = ab[:, 20480:22528]
        FB = 22528
        Ff = ab[:, FB:FB + 16384].bitcast(F32)
        F3 = Ff.rearrange("p (b c) -> p b c", c=64)
        U = ab[:, 38912:43008]
        U3 = U.rearrange("p (g b) -> p g b", b=NBK)
        sel2 = p.sel2_sb[:]
        PS = af[:, 4096:5120]
        Sb = af[:, 0:4096].bitcast(BF16)
        Sb3 = Sb.rearrange("p (b c) -> p b c", c=64)
        zT = ab[:, FB:FB + 4096]
        gz = ab[:, FB + 4096:FB + 8192]
        wbuf = [ab[:, FB + 8192 + i * 2048:FB + 8192 + (i + 1) * 2048] for i in range(2)]
        scr = [ab[:, FB + 12288 + i * 1024:FB + 12288 + (i + 1) * 1024].bitcast(F32) for i in range(4)]

        sv = lambda i: Ff[:, i * 32:(i + 1) * 32]
        A_RE, A_IM, LDT, DCOL, DT, AR, AI, T1, T2, T3, DEN, CFR, CFI = [sv(i) for i in range(13)]
        lamr = {k: sv(16 + 2 * (k + 4)) for k in (-4, -3, -2, -1, 0, 1, 2, 3, 4, 8)}
        lami = {k: sv(17 + 2 * (k + 4)) for k in (-4, -3, -2, -1, 0, 1, 2, 3, 4, 8)}
        KI = Ff[:, 1024:1056].bitcast(mybir.dt.int32)
        LAMA = PS[:, 0:64]
        LAMB = PS[:, 64:128]
        Wst = [PS[:, 128:224], PS[:, 224:320]]
        Tst = PS[:, 320:416]
        M1 = PS[:, 416:480]
        M2 = PS[:, 480:544]
        b_r, b_i, c_r, c_i = [Ff[:, 2048 + i * 512:2048 + (i + 1) * 512] for i in range(4)]
        Bb_r, Bb_i = Ff[:, 4096:4608], Ff[:, 4608:5120]
        maskLT, identf = Ff[:, 5120:5248], Ff[:, 5248:5376]
        tmpA, tmpB = Ff[:, 5376:5888], Ff[:, 5888:6400]
        tmpT = Ff[:, 6400:6528]
        tmpT4 = Ff[:, 6400:6912]
        dmat4 = Ff[:, 6912:7424]
        Z = [af[:, i * 1024:(i + 1) * 1024] for i in range(4)]
        SV = "s5sv"
        for dst, src in ((A_RE, p.s5_are), (A_IM, p.s5_aim), (LDT, p.s5_ldt), (DCOL, p.s5_dcol),
                         (b_r, p.s5_br), (b_i, p.s5_bi), (c_r, p.s5_cr), (c_i, p.s5_ci),
                         (maskLT, p.s5_mask), (identf, p.ident)):
            p.dma("sp", dst, src, (), [SV])
        p.dma("pool", selb, p.s5_sel, (), ["selb"], max_dma_last_dim=4096)
        p.dma("pool", sel2, p.s5_sel2, (), ["sel2"])
        V = [SV]
        p.act(DT, LDT, AF.Exp, V, V)
        p.tt("dve", AR, A_RE, DT, ALU.mult, V, V)
        p.tt("dve", AI, A_IM, DT, ALU.mult, V, V)
        p.ts("dve", T1, AI, 1.0 / TWO_PI, None, ALU.mult, None, V, V)
        p.op("dve", lambda e: e.tensor_copy(KI, T1), V, V)
        p.op("dve", lambda e: e.tensor_copy(T1, KI), V, V)
        p.stt("dve", T2, T1, -TWO_PI, AI, ALU.mult, ALU.add, V, V)
        p.ts("dve", T2, T2, -3.1415925, 3.1415925, ALU.max, ALU.min, V, V)
        p.act(T3, T2, AF.Abs, V, V)
        p.ts("dve", T3, T3, -1.0, math.pi / 2, ALU.mult, ALU.add, V, V)
        p.act(T2, T2, AF.Sin, V, V)
        p.act(T3, T3, AF.Sin, V, V)
        p.act(T1, AR, AF.Exp, V, V)
        p.tt("dve", lamr[1], T1, T3, ALU.mult, V, V)
        p.tt("dve", lami[1], T1, T2, ALU.mult, V, V)
        p.act(T1, AR, AF.Exp, V, V, scale=-1.0)
        p.tt("dve", lamr[-1], T1, T3, ALU.mult, V, V)
        p.tt("dve", lami[-1], T1, T2, ALU.mult, V, V)
        p.ts("dve", lami[-1], lami[-1], -1.0, None, ALU.mult, None, V, V)
        p.op("dve", lambda e: e.memset(lamr[0], 1.0), V, V)
        p.op("dve", lambda e: e.memset(lami[0], 0.0), V, V)

        def cmul(outr, outi, ar, ai_, br, bi, shape=None):
            ta, tb = (T1, DEN) if shape is None else (tmpA, tmpB)
            if shape is not None:
                ta = ta.rearrange("p (g e) -> p g e", e=16)
                tb = tb.rearrange("p (g e) -> p g e", e=16)
            p.tt("dve", ta, ar, br, ALU.mult, V, V)
            p.tt("dve", tb, ai_, bi, ALU.mult, V, V)
            p.tt("dve", ta, ta, tb, ALU.subtract, V, V)
            p.tt("dve", tb, ar, bi, ALU.mult, V, V)
            p.op("dve", lambda e: e.tensor_copy(outr, ta), V, V)
            p.tt("dve", ta, ai_, br, ALU.mult, V, V)
            p.tt("dve", outi, ta, tb, ALU.add, V, V)

        for k in (2, 3, 4):
            cmul(lamr[k], lami[k], lamr[k - 1], lami[k - 1], lamr[1], lami[1])
        for k in (-2, -3, -4):
            cmul(lamr[k], lami[k], lamr[k + 1], lami[k + 1], lamr[-1], lami[-1])
        cmul(lamr[8], lami[8], lamr[4], lami[4], lamr[4], lami[4])
        p.tt("dve", DEN, A_RE, A_RE, ALU.mult, V, V)
        p.tt("dve", T1, A_IM, A_IM, ALU.mult, V, V)
        p.tt("dve", DEN, DEN, T1, ALU.add, V, V)
        p.op("dve", lambda e: e.reciprocal(DEN, DEN), V, V)
        p.ts("dve", T3, lamr[1], -1.0, None, ALU.add, None, V, V)
        p.tt("dve", T1, T3, A_RE, ALU.mult, V, V)
        p.tt("dve", T2, lami[1], A_IM, ALU.mult, V, V)
        p.tt("dve", T1, T1, T2, ALU.add, V, V)
        p.tt("dve", CFR, T1, DEN, ALU.mult, V, V)
        p.tt("dve", T1, lami[1], A_RE, ALU.mult, V, V)
        p.tt("dve", T2, T3, A_IM, ALU.mult, V, V)
        p.tt("dve", T1, T1, T2, ALU.subtract, V, V)
        p.tt("dve", CFI, T1, DEN, ALU.mult, V, V)
        p.op("dve", lambda e: e.tensor_copy(LAMA[:, 0:32], lamr[LB]), V, V)
        p.op("dve", lambda e: e.tensor_copy(LAMA[:, 32:64], lamr[LB]), V, V)
        p.ts("dve", LAMB[:, 0:32], lami[LB], -1.0, None, ALU.mult, None, V, V)
        p.op("dve", lambda e: e.tensor_copy(LAMB[:, 32:64], lami[LB]), V, V)
        p.op("dve", lambda e: e.memset(Wst[0], 0.0), V, V + ["W0", "W0h"])
        LAMA2, LAMB2, NA4, NBS4, LBSW = [PS[:, 544 + i * 64:608 + i * 64] for i in range(5)]
        for half in range(2):
            hs = slice(half * 32, (half + 1) * 32)
            p.op("dve", lambda e, hs=hs: e.tensor_copy(LAMA2[:, hs], lamr[2 * LB]), V, V)
            p.op("dve", lambda e, hs=hs: e.tensor_copy(NA4[:, hs], lamr[-LB]), V, V)
        p.ts("dve", LAMB2[:, 0:32], lami[2 * LB], -1.0, None, ALU.mult, None, V, V)
        p.op("dve", lambda e: e.tensor_copy(LAMB2[:, 32:64], lami[2 * LB]), V, V)
        p.op("dve", lambda e: e.tensor_copy(NBS4[:, 0:32], lami[-LB]), V, V)
        p.ts("dve", NBS4[:, 32:64], lami[-LB], -1.0, None, ALU.mult, None, V, V)
        p.op("dve", lambda e: e.tensor_copy(LBSW[:, 0:32], lami[LB]), V, V)
        p.ts("dve", LBSW[:, 32:64], lami[LB], -1.0, None, ALU.mult, None, V, V)
        bc = lambda v: v.unsqueeze(2).to_broadcast([128, 32, 16])
        v3 = lambda t: t.rearrange("p (g e) -> p g e", e=16)
        cmul(v3(Bb_r), v3(Bb_i), bc(CFR), bc(CFI), v3(b_r), v3(b_i), shape=3)
        for ch in range(4):
            g0 = ch * 8
            Zr, Zi, ZCr, ZCni = [z.rearrange("p (g l h e) -> p g l h e", g=8, l=LB, h=2) for z in Z]
            ZB = ["s5ZB", "s5ZC"]
            zero8 = Ff[:, 7680:7808].rearrange("p (g e) -> p g e", e=16)
            if ch == 0:
                p.op("pool", lambda e: e.memset(zero8, 0.0), (), ["s5zero"])
            for (eng, outr, outi, sgn, xr, xi, neg, tA, tBf, tN, zN) in (
                    ("dve", Zr, Zi, -1, Bb_r, Bb_i, False, tmpA, tmpB, "s5tmpD", "s5ZB"),
                    ("pool", ZCr, ZCni, 1, c_r, c_i, True, Ff[:, 7424:7552], Ff[:, 7552:7680], "s5tmpP", "s5ZC")):
                ta = v3(tA)[:, 0:8, :]
                tb = v3(tBf)[:, 0:8, :]
                sl = lambda t: v3(t)[:, g0:g0 + 8, :]
                lb = lambda v: v[:, g0:g0 + 8].unsqueeze(2).to_broadcast([128, 8, 16])
                for l in range(LB):
                    kk = sgn * l
                    lr_, li_ = lb(lamr[kk]), lb(lami[kk])
                    p.tt(eng, ta, lr_, sl(xr), ALU.mult, V + [tN], [tN])
                    p.tt(eng, tb, li_, sl(xi), ALU.mult, V + [tN], [tN])
                    for h in range(2):
                        p.tt(eng, outr[:, :, l, h, :], ta, tb, ALU.subtract, [tN, zN], [zN])
                    p.tt(eng, ta, lr_, sl(xi), ALU.mult, V + [tN], [tN])
                    p.tt(eng, tb, li_, sl(xr), ALU.mult, V + [tN], [tN])
                    if neg:
                        p.tt(eng, ta, ta, tb, ALU.add, [tN], [tN])
                        for h in range(2):
                            p.tt(eng, outi[:, :, l, h, :], zero8, ta, ALU.subtract, [tN, zN, "s5zero"], [zN])
                    else:
                        for h in range(2):
                            p.tt(eng, outi[:, :, l, h, :], ta, tb, ALU.add, [tN, zN], [zN])
                for z4 in (outr, outi):
                    p.op(eng, lambda e, z4=z4: e.memset(z4[0:64, :, :, 1, :], 0.0), [zN], [zN])
                    p.op(eng, lambda e, z4=z4: e.memset(z4[64:128, :, :, 0, :], 0.0), [zN], [zN])
            ccol = slice(g0 * 128, (g0 + 8) * 128)
            p.op("act", lambda e, ccol=ccol: e.copy(CCr[:, ccol], Z[2]), ZB, ["s5tab"])
            p.op("act", lambda e, ccol=ccol: e.copy(CCni[:, ccol], Z[3]), ZB, ["s5tab"])
            for q4 in range(2):
                cols4 = slice((g0 + 4 * q4) * 128, (g0 + 4 * q4 + 4) * 128)
                for i, dstT in ((0, BBTr), (1, BBTi)):
                    b = i
                    for gj in range(4):
                        gi = 4 * q4 + gj
                        p.mm(p.bank[b][:, gj * 128:(gj + 1) * 128], Z[i][:, gi * 128:(gi + 1) * 128], identf,
                             True, True, ZB + [SV], ["bank%d" % b])
                    p.op("act", lambda e, dstT=dstT, cols4=cols4, b=b: e.copy(dstT[:, cols4], p.bank[b][:]),
                         ["bank%d" % b], ["s5tab"])
                b = 2 + q4
                for gj in range(4):
                    gi = 4 * q4 + gj
                    o_ = p.bank[b][:, gj * 128:(gj + 1) * 128]
                    p.mm(o_, Z[0][:, gi * 128:(gi + 1) * 128], Z[2][:, gi * 128:(gi + 1) * 128], True, False,
                         ZB, ["bank%d" % b])
                    p.mm(o_, Z[1][:, gi * 128:(gi + 1) * 128], Z[3][:, gi * 128:(gi + 1) * 128], False, True,
                         ZB, ["bank%d" % b])
                m4 = maskLT.unsqueeze(1).to_broadcast([128, 4, 128])
                i4 = identf.unsqueeze(1).to_broadcast([128, 4, 128])
                d4 = DCOL[:, g0 + 4 * q4:g0 + 4 * q4 + 4].unsqueeze(2).to_broadcast([128, 4, 128])
                t4 = tmpT4.rearrange("p (g c) -> p g c", g=4)
                dm4 = dmat4.rearrange("p (g c) -> p g c", g=4)
                p.tt("dve", t4, p.bank[b][:].rearrange("p (g c) -> p g c", g=4), m4, ALU.mult,
                     ["bank%d" % b, SV], ["s5tmpT"])
                p.tt("dve", dm4, i4, d4, ALU.mult, [SV], ["s5dm"])
                p.tt("dve", T0T[:, cols4], tmpT4, dmat4, ALU.add, ["s5tmpT", "s5dm"], ["s5tab"])
        p.dump("small", Ff[:, 0:1664], [SV], 1664)
        p.dump("Bb", Ff[:, 4096:5120], [SV], 1024)
        p.dump("BBTr", BBTr, ["s5tab"], 4096)
        p.dump("BBTi", BBTi, ["s5tab"], 4096)
        p.dump("CCr", CCr, ["s5tab"], 4096)
        p.dump("CCni", CCni, ["s5tab"], 4096)
        p.dump("T0T", T0T, ["s5tab"], 4096)
        p.sc.barrier()

        for tt in range(4):
            uT = zT
            for blk in range(4):
                bi = blk % 2
                p.dma("pool", wbuf[bi], p.s5_win[blk], (), ["wbuf%d" % bi], max_dma_last_dim=4096)
                for oi in range(2):
                    oc = blk * 2 + oi
                    b = oc % 4
                    for dc in range(DC):
                        p.mm(p.bank[b][:], wbuf[bi][:, dc * 256 + oi * 128:dc * 256 + (oi + 1) * 128],
                             p.ht(dc, tt), dc == 0, dc == DC - 1,
                             ["wbuf%d" % bi, "h_%d_%d" % (dc, tt)], ["bank%d" % b])
                    dst = uT[:, oc * 512:(oc + 1) * 512]
                    if oc % 2:
                        p.op("act", lambda e, dst=dst, b=b: e.copy(dst, p.bank[b][:]), ["bank%d" % b],
                             ["uT%d" % oc, "zT%d" % oc])
                    else:
                        p.op("dve", lambda e, dst=dst, b=b: e.tensor_copy(dst, p.bank[b][:]), ["bank%d" % b],
                             ["uT%d" % oc, "zT%d" % oc])
                    if p.deferred:
                        p.deferred.pop(0)()
            last_scr = None
            for pi_ in range(32):
                dc, j = divmod(pi_, 4)
                b = pi_ % 4
                for l in range(LB):
                    rhs = uT[:, dc * 512 + l:(dc + 1) * 512:LB]
                    last_scr = p.mm(p.bank[b][32 * l:32 * l + 32, 0:NBK], sel2[:, j * 32:(j + 1) * 32], rhs,
                                    True, True, ["sel2", "uT%d" % dc], ["bank%d" % b], tile_position=(0, 32 * l))
                if pi_ % 2:
                    p.op("act", lambda e, pi_=pi_, b=b: e.copy(U3[:, pi_, :], p.bank[b][:, 0:NBK]),
                         ["bank%d" % b], ["U%d" % pi_])
                else:
                    p.op("dve", lambda e, pi_=pi_, b=b: e.tensor_copy(U3[:, pi_, :], p.bank[b][:, 0:NBK]),
                         ["bank%d" % b], ["U%d" % pi_])
            first_f = {"dve": True, "act": True}
            for pi_ in range(32):
                col = slice(pi_ * 128, (pi_ + 1) * 128)
                for i, tabT in ((0, BBTr), (1, BBTi)):
                    b = (2 * pi_ + i) % 4
                    p.mm(p.bank[b][:, 0:NBK], tabT[:, col], U3[:, pi_, :], True, True,
                         ["s5tab", "U%d" % pi_], ["bank%d" % b])
                    dst = F3[:, :, i * 32 + pi_]
                    if i == 0:
                        fi = p.op("dve", lambda e, dst=dst, b=b: e.tensor_copy(dst, p.bank[b][:, 0:NBK]),
                                  ["bank%d" % b], ["F"])
                    else:
                        fi = p.op("act", lambda e, dst=dst, b=b: e.copy(dst, p.bank[b][:, 0:NBK]),
                                  ["bank%d" % b], ["F"])
                    if first_f[fi.eng]:
                        first_f[fi.eng] = False
                        fi.deps.append(last_scr)
            p.flush_deferred()
            NH = NBK // 2
            HSPL = 44
            RNG = ((0, HSPL), (HSPL, NH))
            tmpf = af[:, 0:4096].rearrange("p (b c) -> p b c", c=64)
            for hh_, eng in ((0, "dve"), (1, "pool")):
                lo_, hi_ = RNG[hh_]
                bcb = lambda t, n_=hi_ - lo_: t.unsqueeze(1).to_broadcast([128, n_, 64])
                Fe = F3[:, 2 * lo_:2 * hi_:2, :]
                Fo = F3[:, 2 * lo_ + 1:2 * hi_:2, :]
                tf = tmpf[:, lo_:hi_, :]
                fB, tB = "Fh%d" % hh_, "tmph%d" % hh_
                RB = ["F", SV]
                LNB = ["mean0", "mean1", "rstd0", "rstd1", "tsc0", "tsc1"]
                p.tt(eng, tf, bcb(NA4), Fo, ALU.mult, RB, [tB] + LNB)
                p.tt(eng, tf, tf, Fe, ALU.add, RB + [tB], [tB])
                p.tt(eng, Fo, Fo, bcb(NBS4), ALU.mult, RB, [fB])
                p.tt(eng, tf[:, :, 0:32], tf[:, :, 0:32], Fo[:, :, 32:64], ALU.add, [fB, tB], [tB])
                p.tt(eng, tf[:, :, 32:64], tf[:, :, 32:64], Fo[:, :, 0:32], ALU.add, [fB, tB], [tB])
                p.op(eng, lambda e, Fo=Fo, tf=tf: e.tensor_copy(Fo, tf), [tB, fB], [fB])
            for k2 in range(NH):
                g = tt * NH + k2
                fB, tB = "Fh%d" % (k2 >= HSPL), "tmph%d" % (k2 >= HSPL)
                Wc, Wn = Wst[g % 2], Wst[(g + 1) % 2]
                wcB, wnB = "W%d" % (g % 2), "W%d" % ((g + 1) % 2)
                p.op("act", lambda e, k2=k2, Wc=Wc: e.copy(Sb3[:, 2 * k2, :], Wc[:, 0:64]), [wcB, SV], ["Sb", tB])
                p.tt("dve", Tst[:, 0:64], Wc[:, 0:64], F3[:, 2 * k2 + 1, :], ALU.add, [wcB, fB, SV], ["Tlo"])
                p.tt("pool", Tst[:, 64:96], Wc[:, 64:96], F3[:, 2 * k2 + 1, 0:32], ALU.add, [wcB + "h", fB, SV], ["Thi"])
                p.tt("dve", M1, LAMA2, Tst[:, 0:64], ALU.mult, ["Tlo", SV], ["M1"])
                p.tt("pool", M2, LAMB2, Tst[:, 32:96], ALU.mult, ["Tlo", "Thi", SV], ["M2"])
                p.tt("dve", Wn[:, 0:64], M1, M2, ALU.add, ["M1", "M2"], [wnB])
                p.tt("pool", Wn[:, 64:96], M1[:, 0:32], M2[:, 0:32], ALU.add, ["M1", "M2"], [wnB + "h"])
            for hh_, eng in ((0, "dve"), (1, "pool")):
                lo_, hi_ = RNG[hh_]
                bcb = lambda t, n_=hi_ - lo_: t.unsqueeze(1).to_broadcast([128, n_, 64])
                Fe = F3[:, 2 * lo_:2 * hi_:2, :]
                Fo = F3[:, 2 * lo_ + 1:2 * hi_:2, :]
                Se = Sb3[:, 2 * lo_:2 * hi_:2, :]
                So = Sb3[:, 2 * lo_ + 1:2 * hi_:2, :]
                fB, soB = "Fh%d" % hh_, "So%d" % hh_
                p.tt(eng, Fe, Fe, Se, ALU.add, [fB, "F", "Sb", SV], [fB])
                p.tt(eng, Fo, bcb(LAMA), Fe, ALU.mult, [fB, SV], [fB])
                p.tt(eng, Fe, Fe, bcb(LBSW), ALU.mult, [fB, SV], [fB])
                p.tt(eng, So[:, :, 0:32], Fo[:, :, 0:32], Fe[:, :, 32:64], ALU.add, [fB], [soB])
                p.tt(eng, So[:, :, 32:64], Fo[:, :, 32:64], Fe[:, :, 0:32], ALU.add, [fB], [soB])
            if tt == 0:
                p.dump("U", U, ["U%d" % i for i in range(32)], 4096)
                p.dump("F", Ff, ["F"], 8192)
                p.dump("Sb", Sb, ["Sb", "So0", "So1"], 8192)
            p.sc.barrier()
            for q4 in range(8):
                b = q4 % 4
                for j in range(4):
                    pi_ = q4 * 4 + j
                    col = slice(pi_ * 128, (pi_ + 1) * 128)
                    o_ = p.bank[b][:, j * NBK:(j + 1) * NBK]
                    p.mm(o_, T0T[:, col], U3[:, pi_, :], True, False, ["s5tab", "Uq%d" % q4], ["bank%d" % b])
                    p.mm(o_, CCr[:, col], Sb3[:, :, pi_], False, False, ["s5tab", "Sb", "So0", "So1"], ["bank%d" % b])
                    p.mm(o_, CCni[:, col], Sb3[:, :, 32 + pi_], False, True, ["s5tab", "Sb", "So0", "So1"], ["bank%d" % b])
                r = q4 % 2
                sq, inner = scr[2 * r], scr[2 * r + 1]
                Y = p.bank[b][:]
                yB = "bank%d" % b
                p.act(sq, Y, AF.Square, [yB], ["sq%d" % r])
                p.ts("dve", sq, sq, 0.044715, 1.0, ALU.mult, ALU.add, ["sq%d" % r], ["sq%d" % r])
                p.tt("dve", inner, sq, Y, ALU.mult, ["sq%d" % r, yB], ["in%d" % r])
                p.act(inner, inner, AF.Sigmoid, ["in%d" % r], ["in%d" % r], scale=GK)
                p.tt("dve", U[:, q4 * 512:(q4 + 1) * 512], inner, Y, ALU.mult, ["in%d" % r, yB], ["Uq%d" % q4])
            for dc in range(DC):
                b = dc % 4
                for l in range(LB):
                    for j in range(4):
                        pi_ = dc * 4 + j
                        p.mm(p.bank[b][:, l * NBK:(l + 1) * NBK],
                             selb[:, (j * LB + l) * 128:(j * LB + l + 1) * 128], U3[:, pi_, :],
                             j == 0, j == 3, ["selb", "Uq%d" % dc], ["bank%d" % b])
                dst = zT[:, dc * 512:(dc + 1) * 512].rearrange("p (b l) -> p l b", l=LB)
                src = p.bank[b][:].rearrange("p (l b) -> p l b", l=LB)
                if dc % 2:
                    p.op("act", lambda e, dst=dst, src=src: e.copy(dst, src), ["bank%d" % b], ["zT%d" % dc])
                else:
                    p.op("dve", lambda e, dst=dst, src=src: e.tensor_copy(dst, src), ["bank%d" % b], ["zT%d" % dc])
            if tt == 0:
                p.dump("zs", U, ["Uq%d" % i for i in range(8)], 4096)
                p.dump("zT", zT, ["zT%d" % i for i in range(8)], 4096)
            p.sc.barrier()
            nsc = 0
            for blk in range(4):
                bi = blk % 2
                p.dma("pool", wbuf[bi], p.s5_wg[blk], (), ["wbuf%d" % bi], max_dma_last_dim=4096)
                for oi in range(2):
                    oc = blk * 2 + oi
                    b = oc % 4
                    for dc in range(DC):
                        p.mm(p.bank[b][:], wbuf[bi][:, dc * 256 + oi * 128:dc * 256 + (oi + 1) * 128],
                             zT[:, dc * 512:(dc + 1) * 512], dc == 0, dc == DC - 1,
                             ["wbuf%d" % bi, "zT%d" % dc], ["bank%d" % b])
                    r = nsc % 4
                    nsc += 1
                    p.act(scr[r], p.bank[b][:], AF.Sigmoid, ["bank%d" % b], ["scr%d" % r])
                    p.tt("dve", gz[:, oc * 512:(oc + 1) * 512], scr[r], zT[:, oc * 512:(oc + 1) * 512], ALU.mult,
                         ["scr%d" % r, "zT%d" % oc], ["gz%d" % oc])
            p.proj_resid_ln(s, 0, 8, lambda k, t_: (gz[:, k * 512:(k + 1) * 512], "gz%d" % k),
                            wbuf, lambda dblk: p.s5_wo[dblk], "wbuf", has_next, store, tts=(tt,),
                            defer=(tt < 3))

    def build(self):
        p = self
        p.out_dmas = []
        p.deferred = []
        p.prologue()
        subs = p.subs
        p.first_modulate(subs[0])
        for i, s in enumerate(subs):
            last = i == len(subs) - 1
            if s % 3 != 1:
                p.ffn(s, not last, last)
            elif s == 1:
                p.attn(s, not last, last)
            else:
                p.s5(s, not last, last)
            if not last:
                nxt = subs[i + 1]
                if s % 3 != 1 and nxt % 3 != 1:
                    continue
                p.flush_deferred()
                p.sc.barrier()
        p.flush_deferred()
        p.sc.emit(p.nc, p.out_dmas)
        return p.nc


def host_layout(inp):
    f = lambda a: np.ascontiguousarray(a, dtype=np.float32)
    out = {}
    ada_w = np.asarray(inp["ada_w"])
    out["ada_r"] = f(ada_w.reshape(2, DC, 128, 18, 512).transpose(0, 3, 2, 1, 4).reshape(2, 18, 128, DC * 512))
    ada_b = np.asarray(inp["ada_b"])
    out["ada_b"] = f(ada_b.reshape(2, 72, 128).transpose(2, 0, 1).reshape(128, 144))
    out["ln_g"] = f(np.asarray(inp["ln_g"]).reshape(6, DC, 128).transpose(2, 0, 1).reshape(128, 6 * DC))
    out["ln_b"] = f(np.asarray(inp["ln_b"]).reshape(6, DC, 128).transpose(2, 0, 1).reshape(128, 6 * DC))
    w1 = np.asarray(inp["ffn_w1"])
    w3 = np.asarray(inp["ffn_w3"])
    w2 = np.asarray(inp["ffn_w2"])
    out["w1r"] = f(w1.reshape(2, 2, DC, 128, 11, 256).transpose(0, 1, 4, 3, 2, 5).reshape(2, 2, 11, 128, DC * 256))
    out["w3r"] = f(w3.reshape(2, 2, DC, 128, 11, 256).transpose(0, 1, 4, 3, 2, 5).reshape(2, 2, 11, 128, DC * 256))
    out["w2r"] = f(w2.reshape(2, 2, FC, 128, 4, 256).transpose(0, 1, 4, 3, 2, 5).reshape(2, 2, 4, 128, FC * 256))
    win = np.asarray(inp["attn_w_in"])[0]
    out["wqkv_r"] = f(win.reshape(DC, 128, 3, 8, 128).transpose(3, 1, 2, 0, 4).reshape(8, 128, 3072))
    wo = np.asarray(inp["attn_w_out"])[0]
    out["wout_r"] = f(wo.reshape(8, 128, 4, 256).transpose(2, 1, 0, 3).reshape(4, 128, 8 * 256))
    out["lam_b"] = f(np.broadcast_to(np.asarray(inp["attn_lam"])[0].reshape(1, 256), (128, 256)))
    out["subln_b"] = f(np.broadcast_to(np.asarray(inp["attn_subln_g"])[0].reshape(1, 128), (128, 128)))
    out["ident"] = np.eye(128, dtype=np.float32)
    gn = lambda a: f(np.asarray(a)[0].reshape(32, 2, 64).transpose(1, 2, 0).reshape(128, 32))
    out["s5_are"] = gn(inp["ssm_a_re"])
    out["s5_aim"] = gn(inp["ssm_a_im"])
    ldt = np.asarray(inp["ssm_log_dt"])[0].reshape(32, 2)
    out["s5_ldt"] = f(np.broadcast_to(ldt.transpose(1, 0)[:, None, :], (2, 64, 32)).reshape(128, 32))
    dd = np.asarray(inp["ssm_d"])[0].reshape(32, 2, 16).transpose(1, 2, 0).reshape(32, 32)
    out["s5_dcol"] = f(np.tile(dd, (4, 1)))
    bl = lambda a: f(np.asarray(a)[0].reshape(32, 2, 64, 16).transpose(1, 2, 0, 3).reshape(128, 512))
    cl = lambda a: f(np.asarray(a)[0].reshape(32, 2, 16, 64).transpose(1, 3, 0, 2).reshape(128, 512))
    out["s5_br"] = bl(inp["ssm_b_re"])
    out["s5_bi"] = bl(inp["ssm_b_im"])
    out["s5_cr"] = cl(inp["ssm_c_re"])
    out["s5_ci"] = cl(inp["ssm_c_im"])
    lrow = np.arange(128) // 32
    out["s5_mask"] = f((lrow[None, :] >= lrow[:, None]))
    sel = np.zeros((4, 4, 128, 128), np.float32)
    for j in range(4):
        for l in range(4):
            for gp in range(2):
                for pp in range(16):
                    sel[j, l, l * 32 + gp * 16 + pp, (2 * j + gp) * 16 + pp] = 1.0
    out["s5_sel"] = f(sel.reshape(16, 128, 128).transpose(1, 0, 2).reshape(128, 16 * 128))
    wblk = lambda w: f(np.asarray(w)[0].reshape(DC, 128, 4, 256).transpose(2, 1, 0, 3).reshape(4, 128, DC * 256))
    out["s5_win"] = wblk(inp["ssm_w_in"])
    sel2 = np.zeros((128, 4, 32), np.float32)
    for j in range(4):
        for gp in range(2):
            for pp in range(16):
                sel2[(2 * j + gp) * 16 + pp, j, gp * 16 + pp] = 1.0
    out["s5_sel2"] = f(sel2.reshape(128, 128))
    out["s5_wg"] = wblk(inp["ssm_w_gate"])
    out["s5_wo"] = wblk(inp["ssm_w_out"])
    return out


def core_inputs(shared, x_b, c_b):
    m = dict(shared)
    m["xT"] = np.ascontiguousarray(np.asarray(x_b, dtype=np.float32).T)
    m["cvec"] = np.ascontiguousarray(np.asarray(c_b, dtype=np.float32).reshape(DC, 128).T)
    return m


_NC_CACHE = {}


def run_subs(subs, shared, x, c, n_cores=NB):
    key = tuple(subs)
    if key not in _NC_CACHE:
        bld = Builder(list(subs))
        _NC_CACHE[key] = bld.build()
        if DEBUG:
            global LAST_MAP
            LAST_MAP = bld.dbg_map
    nc = _NC_CACHE[key]
    in_maps = [core_inputs(shared, x[b], c[b]) for b in range(n_cores)]
    res = run_bass_kernel_spmd(nc, in_maps, core_ids=list(range(n_cores)))
    if DEBUG:
        global LAST_DBG
        LAST_DBG = (res.results[0]["dbgf"], res.results[0]["dbgb"])
    return np.stack([np.ascontiguousarray(r["yT"].T) for r in res.results], axis=0)


def kernel(**inputs):
    shared = host_layout(inputs)
    x = np.asarray(inputs["x"], dtype=np.float32)
    c = np.asarray(inputs["c"], dtype=np.float32)
    y = run_subs([0, 1, 2, 3, 4, 5], shared, x, c)
    return y.astype(np.float32)
```

```python
import math
from contextlib import ExitStack

import numpy as np
import concourse.bass as bass
import concourse.mybir as mybir
from concourse.bass_utils import run_bass_kernel_spmd

F32 = mybir.dt.float32
BF16 = mybir.dt.bfloat16
AF = mybir.ActivationFunctionType
ALU = mybir.AluOpType

D = 1024
S = 2048
NB = 8
DFF = 2816
DC = D // 128
FC = DFF // 128
TT = 512
DEPTH = 2
ALPHA = (2 * DEPTH) ** 0.25
LN_EPS = 1e-5
ENGS = ("pe", "act", "dve", "pool", "sp")
DEBUG = False
MERGE_EXP = True


class Buf:
    __slots__ = ("name", "lw", "rs")

    def __init__(self, name):
        self.name = name
        self.lw = None
        self.rs = []


class Ins:
    __slots__ = ("eng", "fn", "deps", "signal", "cnt", "dma", "dsem", "dval", "dprev", "waits")

    def __init__(self, eng, fn, dma):
        self.eng = eng
        self.fn = fn
        self.dma = dma
        self.deps = []
        self.signal = False
        self.cnt = 0
        self.dsem = None
        self.dval = 0
        self.dprev = 0
        self.waits = []


class Sched:
    def __init__(self, n_dma_slots=28):
        self.streams = {e: [] for e in ENGS}
        self.n_dma_slots = n_dma_slots
        self.dma_rr = {e: 0 for e in ENGS}
        self.dma_cum = {}
        self.dma_last = {}
        self.dma_recent = {}
        self.bar = {e: [] for e in ENGS}
        self.bufs = {}

    def buf(self, name):
        b = self.bufs.get(name)
        if b is None:
            b = self.bufs[name] = Buf(name)
        return b

    def op(self, eng, fn, reads=(), writes=(), dma=False):
        ins = Ins(eng, fn, dma)
        deps = set()
        if self.bar[eng]:
            deps.update(self.bar[eng])
            self.bar[eng] = []
        for b in reads:
            if isinstance(b, str):
                b = self.buf(b)
            if b.lw is not None:
                deps.add(b.lw)
        wl = []
        for b in writes:
            if isinstance(b, str):
                b = self.buf(b)
            wl.append(b)
            deps.update(b.rs)
            if b.lw is not None:
                deps.add(b.lw)
        for b in reads:
            if isinstance(b, str):
                b = self.buf(b)
            b.rs.append(ins)
        for b in wl:
            b.lw = ins
            b.rs = []
        if dma:
            slot = self.dma_rr[eng]
            self.dma_rr[eng] = (slot + 1) % self.n_dma_slots
            key = (eng, slot)
            ins.dsem = key
            ins.dprev = self.dma_cum.get(key, 0)
            ins.dval = ins.dprev + 16
            self.dma_cum[key] = ins.dval
            self.dma_last[key] = ins
            self.dma_recent[key] = ins
        ins.deps = [d for d in deps if d is not ins]
        self.streams[eng].append(ins)
        return ins

    def barrier(self):
        deps = []
        for e in ENGS:
            if self.streams[e]:
                deps.append(self.streams[e][-1])
        deps.extend(self.dma_recent.values())
        self.dma_recent = {}
        nop = self.op("sp", lambda eng: eng.nop(nofuse=True))
        nop.deps = list(set(nop.deps) | set(deps))
        for e in ENGS:
            if e != "sp":
                self.bar[e] = [nop]
        return nop

    def finalize(self):
        for e in ENGS:
            for ins in self.streams[e]:
                for d in ins.deps:
                    if not d.dma:
                        if d.eng == "pe" and ins.eng == "pe":
                            continue
                        d.signal = True
        for e in ENGS:
            c = 0
            for ins in self.streams[e]:
                if ins.signal:
                    c += 1
                    ins.cnt = c
        for e in ENGS:
            known = {}
            for ins in self.streams[e]:
                need = {}
                for d in ins.deps:
                    if d.dma:
                        k = ("dma",) + d.dsem
                        v = d.dval
                    else:
                        if d.eng == "pe" and e == "pe":
                            continue
                        k = ("eng", d.eng)
                        v = d.cnt
                    if v > need.get(k, 0):
                        need[k] = v
                if ins.dma and ins.dprev > 0:
                    k = ("dma",) + ins.dsem
                    if ins.dprev > need.get(k, 0):
                        need[k] = ins.dprev
                ws = []
                for k, v in need.items():
                    if known.get(k, 0) >= v:
                        continue
                    known[k] = v
                    ws.append((k, v))
                ins.waits = ws

    def emit(self, nc, final_waits):
        self.finalize()
        with ExitStack() as es:
            esem = {e: es.enter_context(nc.semaphore("s_" + e)) for e in ENGS}
            dsem = {}
            for key in self.dma_cum:
                dsem[key] = es.enter_context(nc.semaphore("d_%s_%d" % key))
            block = es.enter_context(nc.Block())

            def run(engname):
                def body(eng):
                    for ins in self.streams[engname]:
                        for k, v in ins.waits:
                            if k[0] == "eng":
                                eng.wait_ge(esem[k[1]], v)
                            else:
                                eng.wait_ge(dsem[(k[1], k[2])], v)
                        r = ins.fn(eng)
                        if ins.dma:
                            r.then_inc(dsem[ins.dsem], 16)
                        elif ins.signal:
                            r.then_inc(esem[engname], 1)
                    if engname == "sp":
                        for d in final_waits:
                            eng.wait_ge(dsem[d.dsem], d.dval)
                return body

            block.tensor(run("pe"))
            block.scalar(run("act"))
            block.vector(run("dve"))
            block.gpsimd(run("pool"))
            block.sync(run("sp"))


class Prog:
    def __init__(self, subs, first_in_raw, last_out_raw):
        self.subs = subs
        self.nc = bass.Bass("TRN2", target_bir_lowering=False)
        self.sc = Sched()
        self.es = ExitStack()
        self.first_in_raw = first_in_raw
        self.last_out_raw = last_out_raw
        self.dram = {}

    def din(self, name, shape, dt=F32):
        t = self.nc.dram_tensor(name, list(shape), dt, kind="ExternalInput").ap()
        self.dram[name] = t
        return t

    def dout(self, name, shape, dt=F32):
        t = self.nc.dram_tensor(name, list(shape), dt, kind="ExternalOutput").ap()
        self.dram[name] = t
        return t

    def sb(self, name, shape, dt):
        return self.es.enter_context(self.nc.sbuf_tensor(name, list(shape), dt))

    def ps(self, name, shape, dt=F32):
        return self.es.enter_context(self.nc.psum_tensor(name, list(shape), dt))

    def op(self, eng, fn, reads=(), writes=(), dma=False):
        return self.sc.op(eng, fn, reads, writes, dma)

    def dump(self, name, ap, reads, n):
        if not getattr(self, "dbg", False):
            return
        kind = "f" if ap.dtype == F32 else "b"
        o = self.dbg_off[kind]
        self.dbg_off[kind] = o + n
        self.dbg_map[name] = (kind, o, n)
        dst = (self.dbgf if kind == "f" else self.dbgb)[:, o:o + n]
        d = self.dma("sp", dst, ap, reads, [])
        self.out_dmas.append(d)

    def mm(self, out, lhsT, rhs, start, stop, reads, writes, **kw):
        return self.op("pe", lambda e: e.matmul(out, lhsT, rhs, start=start, stop=stop, **kw),
                       reads, writes)

    def act(self, out, in_, func, reads, writes, bias=None, scale=None, eng="act"):
        kw = {}
        if bias is not None:
            kw["bias"] = bias
        if scale is not None:
            kw["scale"] = scale
        return self.op(eng, lambda e: e.activation(out, in_, func, **kw), reads, writes)

    def ts(self, eng, out, in0, s1, s2, op0, op1, reads, writes):
        if s2 is None:
            return self.op(eng, lambda e: e.tensor_scalar(out, in0, s1, None, op0), reads, writes)
        return self.op(eng, lambda e: e.tensor_scalar(out, in0, s1, s2, op0, op1), reads, writes)

    def tt(self, eng, out, in0, in1, op, reads, writes):
        return self.op(eng, lambda e: e.tensor_tensor(out, in0, in1, op), reads, writes)

    def stt(self, eng, out, in0, scalar, in1, op0, op1, reads, writes):
        return self.op(eng, lambda e: e.scalar_tensor_tensor(out, in0, scalar, in1, op0, op1),
                       reads, writes)

    def dma(self, eng, out, in_, reads, writes, **kw):
        return self.op(eng, lambda e: e.dma_start(out=out, in_=in_, **kw), reads, writes, dma=True)


def _col(t, j):
    return t[:, j:j + 1]


class Builder(Prog):
    def __init__(self, subs):
        super().__init__(subs, True, True)
        self.layers = sorted(set(s // 3 for s in subs))
        self.declare_io()
        self.alloc_common()

    def declare_io(self):
        self.xT = self.din("xT", [D, S])
        self.cvec = self.din("cvec", [128, DC])
        self.ada_r = self.din("ada_r", [2, 18, 128, DC * 512])
        self.ada_b = self.din("ada_b", [128, 2 * 72])
        self.ln_g = self.din("ln_g", [128, 6 * DC])
        self.ln_b = self.din("ln_b", [128, 6 * DC])
        self.w1r = self.din("w1r", [2, 2, 11, 128, DC * 256])
        self.w3r = self.din("w3r", [2, 2, 11, 128, DC * 256])
        self.w2r = self.din("w2r", [2, 2, 4, 128, FC * 256])
        self.wqkv_r = self.din("wqkv_r", [8, 128, 3072])
        self.wout_r = self.din("wout_r", [4, 128, 8 * 256])
        self.lam_b = self.din("lam_b", [128, 256])
        self.subln_b = self.din("subln_b", [128, 128])
        self.ident = self.din("ident", [128, 128])
        self.s5_are = self.din("s5_are", [128, 32])
        self.s5_aim = self.din("s5_aim", [128, 32])
        self.s5_ldt = self.din("s5_ldt", [128, 32])
        self.s5_dcol = self.din("s5_dcol", [128, 32])
        self.s5_br = self.din("s5_br", [128, 512])
        self.s5_bi = self.din("s5_bi", [128, 512])
        self.s5_cr = self.din("s5_cr", [128, 512])
        self.s5_ci = self.din("s5_ci", [128, 512])
        self.s5_mask = self.din("s5_mask", [128, 128])
        self.s5_sel = self.din("s5_sel", [128, 16 * 128])
        self.s5_win = self.din("s5_win", [4, 128, DC * 256])
        self.s5_sel2 = self.din("s5_sel2", [128, 128])
        self.s5_wg = self.din("s5_wg", [4, 128, DC * 256])
        self.s5_wo = self.din("s5_wo", [4, 128, DC * 256])
        self.yT = self.dout("yT", [D, S])
        self.dbg = DEBUG
        if self.dbg:
            self.dbgf = self.dout("dbgf", [128, 16384])
            self.dbgb = self.dout("dbgb", [128, 65536], BF16)
            self.dbg_off = {"f": 0, "b": 0}
            self.dbg_map = {}

    def alloc_common(self):
        self.x = self.sb("x", [128, DC, S], F32)
        self.h = self.sb("h", [128, DC, S], BF16)
        self.onesb = self.sb("onesb", [128, 128], BF16)
        self.sel2_sb = self.sb("sel2_sb", [128, 128], BF16)
        self.mods = self.sb("mods", [128, 2 * 72], F32)
        self.adab = self.sb("adab", [128, 2 * 72], F32)
        self.lng = self.sb("lng", [128, 6 * DC], F32)
        self.lnb = self.sb("lnb", [128, 6 * DC], F32)
        self.cond = self.sb("cond", [128, DC], F32)
        self.condb = self.sb("condb", [128, DC], BF16)
        self.coef = self.sb("coef", [128, 6, 8 * DC], F32)
        self.psum = self.ps("psum", [128, 8, TT], F32)
        self.bank = [self.psum[:, i, :] for i in range(8)]
        self.arena_b = self.sb("arena_b", [128, 44 * 1024], BF16)
        self.arena_f = self.sb("arena_f", [128, 5 * 1024], F32)

    def xt(self, dc, tt):
        return self.x[:, dc, tt * TT:(tt + 1) * TT]

    def ht(self, dc, tt):
        return self.h[:, dc, tt * TT:(tt + 1) * TT]

    def prologue(self):
        p = self
        p.op("pool", lambda e: e.memset(p.onesb[:], 1.0 / D), (), ["onesb"])
        p.dma("sp", p.cond[:], p.cvec, (), ["cond"])
        p.dma("sp", p.adab[:], p.ada_b, (), ["adab"])
        p.dma("sp", p.lng[:], p.ln_g, (), ["lng"])
        p.dma("sp", p.lnb[:], p.ln_b, (), ["lnb"])
        for dc in range(DC):
            for hf in range(2):
                p.dma("sp", p.x[:, dc, hf * 1024:(hf + 1) * 1024],
                      p.xT[dc * 128:(dc + 1) * 128, hf * 1024:(hf + 1) * 1024],
                      (), ["x_%d_%d" % (dc, 2 * hf), "x_%d_%d" % (dc, 2 * hf + 1)])
        p.act(p.cond[:], p.cond[:], AF.Silu, ["cond"], ["cond"])
        p.op("dve", lambda e: e.tensor_copy(p.condb[:], p.cond[:]), ["cond"], ["condb"])
        adabuf = [p.arena_b[:, i * 4096:(i + 1) * 4096] for i in range(2)]
        mps = p.bank[7]
        n = 0
        for L in self.layers:
            for blk in range(18):
                bi = n % 2
                n += 1
                p.dma("pool", adabuf[bi], p.ada_r[L, blk], (), ["adabuf%d" % bi],
                      max_dma_last_dim=4096)
                for cc in range(4):
                    col = L * 72 + blk * 4 + cc
                    for k in range(DC):
                        p.mm(mps[:, col:col + 1],
                             adabuf[bi][:, k * 512 + cc * 128:k * 512 + (cc + 1) * 128],
                             p.condb[:, k:k + 1], k == 0, k == DC - 1,
                             ["adabuf%d" % bi, "condb"], ["bank7"])
                c0 = L * 72 + blk * 4
                p.tt("dve", p.mods[:, c0:c0 + 4], mps[:, c0:c0 + 4], p.adab[:, c0:c0 + 4], ALU.add,
                     ["bank7", "adab"], ["mods"])

    def mod_cols(self, s, kind):
        L, sub = divmod(s, 3)
        c0 = L * 72 + (sub * 3 + kind) * DC
        return self.mods[:, c0:c0 + DC]

    def ln_cols(self, s):
        return self.lng[:, s * DC:(s + 1) * DC], self.lnb[:, s * DC:(s + 1) * DC]

    def prep_coefs(self, s, has_next, gate_mult):
        p = self
        cf = p.coef[:, s]
        sl = lambda i: cf[:, i * DC:(i + 1) * DC]
        g, b = p.ln_cols(s)
        w = ["coef%d" % s]
        r = ["mods", "lng", "lnb"]
        p.ts("dve", sl(4), p.mod_cols(s, 2), 1.0, gate_mult, ALU.add, ALU.mult, r, w)
        if has_next:
            p.ts("dve", sl(5), p.mod_cols(s + 1, 1), 1.0, None, ALU.add, None, r, w)
            p.tt("dve", sl(0), g, sl(5), ALU.mult, r + w, w)
            p.tt("dve", sl(1), b, sl(5), ALU.mult, r + w, w)
            p.tt("dve", sl(1), sl(1), p.mod_cols(s + 1, 0), ALU.add, r + w, w)
            p.ts("dve", sl(2), g, ALPHA, None, ALU.mult, None, r, w)
            p.ts("dve", sl(3), b, ALPHA, None, ALU.mult, None, r, w)
        else:
            p.op("dve", lambda e: e.tensor_copy(sl(2), g), r, w)
            p.op("dve", lambda e: e.tensor_copy(sl(3), b), r, w)

    def first_modulate(self, s):
        p = self
        cf = p.coef[:, s]
        tmp = cf[:, 6 * DC:7 * DC]
        p.ts("dve", tmp, p.mod_cols(s, 1), 1.0, None, ALU.add, None, ["mods"], ["coef%d" % s])
        for tt in range(4):
            for dc in range(DC):
                xb = "x_%d_%d" % (dc, tt)
                p.act(p.ht(dc, tt), p.xt(dc, tt), AF.Identity, [xb, "coef%d" % s, "mods"],
                      ["h_%d_%d" % (dc, tt)], bias=_col(p.mod_cols(s, 0), dc), scale=_col(tmp, dc))
                p.ts("dve", p.xt(dc, tt), p.xt(dc, tt), ALPHA, None, ALU.mult, None, [xb], [xb])

    def ln_setup(self):
        af = self.arena_f
        self.mean_sb = [af[:, i * TT:(i + 1) * TT] for i in range(2)]
        self.rstd_sb = [af[:, (2 + i) * TT:(3 + i) * TT] for i in range(2)]
        self.tsc = [af[:, (4 + i) * TT:(5 + i) * TT] for i in range(2)]
        self.silu_sb = [af[:, (6 + i) * TT:(7 + i) * TT] for i in range(2)]
        ab = self.arena_b
        o = 44 * 1024
        self.zb = [ab[:, o - (i + 1) * TT:o - i * TT] for i in range(6)]
        self._zrot = 0
        self._trot = 0
        self._nw2 = 0
        self._ny = 0

    def resid_evac(self, s, y_ps, ybuf, dc, tt, stat_banks, pend):
        p = self
        cf = p.coef[:, s]
        gate = cf[:, 4 * DC + dc:4 * DC + dc + 1]
        xb = "x_%d_%d" % (dc, tt)
        p.stt("dve", p.xt(dc, tt), y_ps, gate, p.xt(dc, tt), ALU.mult, ALU.add,
              [ybuf, xb, "coef%d" % s], [xb])
        r = p._zrot % 3
        p._zrot += 1
        zb, zq = p.zb[r], p.zb[3 + r]
        p.op("act", lambda e: e.copy(zb, p.xt(dc, tt)), [xb], ["zb%d" % r])
        p.act(zq, p.xt(dc, tt), AF.Square, [xb], ["zq%d" % r])
        mb, qb = stat_banks

        def stats():
            p.mm(p.bank[mb][:], p.onesb[:], zb, dc == 0, dc == DC - 1,
                 ["onesb", "zb%d" % r], ["bank%d" % mb])
            p.mm(p.bank[qb][:], p.onesb[:], zq, dc == 0, dc == DC - 1,
                 ["onesb", "zq%d" % r], ["bank%d" % qb])
        pend.append(stats)

    def ln_finalize(self, s, tt, stat_banks, has_next, store):
        for piece in self.ln_pieces(s, tt, stat_banks, has_next, store):
            piece()

    def ln_pieces(self, s, tt, stat_banks, has_next, store):
        p = self
        mb, qb = stat_banks
        par = tt % 2
        mean, rstd = p.mean_sb[par], p.rstd_sb[par]
        mB, rB = "mean%d" % par, "rstd%d" % par
        cf = p.coef[:, s]
        cB = "coef%d" % s
        pieces = []

        def head():
            p.op("act", lambda e: e.copy(mean, p.bank[mb][:]), ["bank%d" % mb], [mB])
            p.tt("dve", rstd, mean, mean, ALU.mult, [mB], [rB])
            p.tt("dve", rstd, p.bank[qb][:], rstd, ALU.subtract, ["bank%d" % qb, rB], [rB])
            p.ts("dve", rstd, rstd, LN_EPS, None, ALU.add, None, [rB], [rB])
            p.act(rstd, rstd, AF.Sqrt, [rB], [rB])
            p.op("dve", lambda e: e.reciprocal(rstd, rstd), [rB], [rB])
        pieces.append(head)
        for dc in range(DC):
            def body(dc=dc):
                xb = "x_%d_%d" % (dc, tt)
                r = p._trot % 2
                p._trot += 1
                t = p.tsc[r]
                tB = "tsc%d" % r
                p.tt("dve", t, p.xt(dc, tt), mean, ALU.subtract, [xb, mB], [tB])
                p.tt("dve", t, t, rstd, ALU.mult, [tB, rB], [tB])
                p.act(p.xt(dc, tt), t, AF.Identity, [tB, cB], [xb],
                      bias=cf[:, 3 * DC + dc:3 * DC + dc + 1], scale=cf[:, 2 * DC + dc:2 * DC + dc + 1])
                if has_next:
                    p.act(p.ht(dc, tt), t, AF.Identity, [tB, cB], ["h_%d_%d" % (dc, tt)],
                          bias=cf[:, 1 * DC + dc:1 * DC + dc + 1], scale=cf[:, 0 * DC + dc:0 * DC + dc + 1])
                if store:
                    d = p.dma("sp", p.yT[dc * 128:(dc + 1) * 128, tt * TT:(tt + 1) * TT], p.xt(dc, tt),
                              [xb], [])
                    p.out_dmas.append(d)
            pieces.append(body)
        return pieces

    def flush_deferred(self):
        while self.deferred:
            self.deferred.pop(0)()

    def ffn(self, s, has_next, store):
        p = self
        L, sub = divmod(s, 3)
        j = 0 if sub == 0 else 1
        p.ln_setup()
        p.prep_coefs(s, has_next, 0.5)
        ab = p.arena_b
        g = ab[:, 0:22 * 1024]
        w1b = [ab[:, 22528 + i * 2048:22528 + (i + 1) * 2048] for i in range(2)]
        w3b = [ab[:, 26624 + i * 2048:26624 + (i + 1) * 2048] for i in range(2)]
        w2b = [ab[:, 30720 + i * 5632:30720 + (i + 1) * 5632] for i in range(2)]
        cB = "coef%d" % s
        nset = 0
        nw = 0
        for hf in range(2):
            tts = (2 * hf, 2 * hf + 1)
            for fblk in range(11):
                bi = nw % 2
                nw += 1
                p.dma("pool", w1b[bi], p.w1r[L, j, fblk], (), ["w1b%d" % bi], max_dma_last_dim=4096)
                p.dma("pool", w3b[bi], p.w3r[L, j, fblk], (), ["w3b%d" % bi], max_dma_last_dim=4096)
                for fi in range(2):
                    f = fblk * 2 + fi
                    for tt in tts:
                        st = nset % 2
                        nset += 1
                        ub, vb = 2 * st, 2 * st + 1
                        for dc in range(DC):
                            p.mm(p.bank[ub][:], w1b[bi][:, dc * 256 + fi * 128:dc * 256 + (fi + 1) * 128],
                                 p.ht(dc, tt), dc == 0, dc == DC - 1,
                                 ["w1b%d" % bi, "h_%d_%d" % (dc, tt)], ["bank%d" % ub])
                        for dc in range(DC):
                            p.mm(p.bank[vb][:], w3b[bi][:, dc * 256 + fi * 128:dc * 256 + (fi + 1) * 128],
                                 p.ht(dc, tt), dc == 0, dc == DC - 1,
                                 ["w3b%d" % bi, "h_%d_%d" % (dc, tt)], ["bank%d" % vb])
                        sl = p.silu_sb[st]
                        p.act(sl, p.bank[ub][:], AF.Silu, ["bank%d" % ub], ["silu%d" % st])
                        tl = (tt - 2 * hf) * TT
                        gt = g[:, f * 1024 + tl:f * 1024 + tl + TT]
                        p.tt("dve", gt, sl, p.bank[vb][:], ALU.mult,
                             ["silu%d" % st, "bank%d" % vb], ["g_%d_%d" % (f, tt % 2)])
                        if p.deferred and nset % 2 == 0:
                            p.deferred.pop(0)()
            p.flush_deferred()
            p.proj_resid_ln(s, hf, FC, lambda f, tt: (g[:, f * 1024 + (tt % 2) * TT:f * 1024 + (tt % 2) * TT + TT],
                                                      "g_%d_%d" % (f, tt % 2)),
                            w2b, lambda dblk: p.w2r[L, j, dblk], "w2b", has_next, store, defer=True)

    def proj_resid_ln(self, s, hf, nk, rhs_fn, wb, wdram, wname, has_next, store, tts=None, defer=False):
        p = self
        if tts is None:
            tts = (2 * hf, 2 * hf + 1)
        pend = []
        for dblk in range(4):
            bi = p._nw2 % 2
            p._nw2 += 1
            p.dma("pool", wb[bi], wdram(dblk), (), ["%s%d" % (wname, bi)], max_dma_last_dim=4096)
            for di in range(2):
                dc = dblk * 2 + di
                for tt in tts:
                    yb = p._ny % 2
                    p._ny += 1
                    for k in range(nk):
                        rhs, rb = rhs_fn(k, tt)
                        p.mm(p.bank[yb][:], wb[bi][:, k * 256 + di * 128:k * 256 + (di + 1) * 128],
                             rhs, k == 0, k == nk - 1,
                             ["%s%d" % (wname, bi), rb], ["bank%d" % yb])
                    while pend:
                        pend.pop(0)()
                    sbk = (4 + 2 * (tt % 2), 5 + 2 * (tt % 2))
                    p.resid_evac(s, p.bank[yb][:], "bank%d" % yb, dc, tt, sbk, pend)
        while pend:
            pend.pop(0)()
        for tt in tts:
            sbk = (4 + 2 * (tt % 2), 5 + 2 * (tt % 2))
            if defer:
                p.deferred.extend(p.ln_pieces(s, tt, sbk, has_next, store))
            else:
                p.ln_finalize(s, tt, sbk, has_next, store)


    def attn(self, s, has_next, store):
        p = self
        lam_init = 0.8 - 0.6 * math.exp(-0.3 * (s // 3))
        p.ln_setup()
        p.prep_coefs(s, has_next, 1.0)
        ab, af = p.arena_b, p.arena_f
        oT = ab[:, 0:16384]
        qk = [ab[:, 16384 + i * 6144:16384 + (i + 1) * 6144] for i in range(2)]
        vh = [ab[:, 28672 + i * 2064:28672 + (i + 1) * 2064] for i in range(2)]
        wq = [ab[:, 32800 + i * 3072:32800 + (i + 1) * 3072] for i in range(2)]
        pT = [ab[:, 38944 + i * 512:38944 + (i + 1) * 512] for i in range(5)]
        onb = [ab[:, 41504 + i * 128:41504 + (i + 1) * 128] for i in range(8)]
        identb = ab[:, 42528:42656]
        fo = 4096
        NACC = 8
        accs = [af[:, i * 258:(i + 1) * 258] for i in range(NACC)]
        junk = af[:, 2304:2432]
        gvec = af[:, fo + 516:fo + 644]
        lamt = af[:, fo + 644:fo + 900]
        sm = af[:, fo + 900:fo + 1024]
        nlam = sm[:, 0:1]
        p.dma("pool", identb, p.ident, (), ["identb"])
        p.dma("sp", lamt, p.lam_b, (), ["lamt"])
        p.dma("sp", gvec, p.subln_b, (), ["gvec"])
        p.ts("dve", gvec, gvec, 1.0 - lam_init, None, ALU.mult, None, ["gvec"], ["gvec"])
        p.tt("dve", lamt[:, 0:64], lamt[:, 0:64], lamt[:, 64:128], ALU.mult, ["lamt"], ["lamt"])
        p.tt("dve", lamt[:, 128:192], lamt[:, 128:192], lamt[:, 192:256], ALU.mult, ["lamt"], ["lamt"])
        p.op("dve", lambda e: e.reduce_sum(sm[:, 1:2], lamt[:, 0:64], mybir.AxisListType.X), ["lamt"], ["sm"])
        p.op("dve", lambda e: e.reduce_sum(sm[:, 2:3], lamt[:, 128:192], mybir.AxisListType.X), ["lamt", "sm"], ["sm"])
        p.act(sm[:, 1:3], sm[:, 1:3], AF.Exp, ["sm"], ["sm"])
        p.tt("dve", nlam, sm[:, 2:3], sm[:, 1:2], ALU.subtract, ["sm"], ["sm"])
        p.ts("dve", nlam, nlam, -lam_init, None, ALU.add, None, ["sm"], ["sm"])
        for i in range(2):
            v3 = vh[i].rearrange("p (j e) -> p j e", e=129)
            p.op("pool", lambda e, v3=v3: e.memset(v3[:, :, 128:129], 1.0), (), ["vh%d" % i])
            p.op("pool", lambda e, i=i: e.memset(qk[i][64:128, 0:2048], 0.0), (), ["qk%dw0" % i])
            p.op("pool", lambda e, i=i: e.memset(qk[i][0:64, 2048:4096], 0.0), (), ["qk%dw0" % i])
        st_ = {"sc": 0, "pt": 0, "ep": 0, "pj": 0}

        def proj_groups(hh):
            par = hh % 2
            wB, qB, vB = "wq%d" % par, "qk%d" % par, "vh%d" % par
            groups = []
            def nextbank():
                b = (5, 7)[st_["pj"] % 2]
                st_["pj"] += 1
                return p.bank[b], "bank%d" % b

            def load():
                p.dma("pool", wq[par], p.wqkv_r[hh], (), [wB], max_dma_last_dim=4096)
            groups.append(load)
            for which in range(2):
                for tt in range(4):
                    def g(which=which, tt=tt):
                        pb, pbB = nextbank()
                        for dc in range(DC):
                            p.mm(pb[:], wq[par][:, (which * 8 + dc) * 128:(which * 8 + dc + 1) * 128],
                                 p.ht(dc, tt), dc == 0, dc == DC - 1, [wB, "h_%d_%d" % (dc, tt)], [pbB])
                        if which == 1:
                            dst = qk[par][:, 4096 + tt * TT:4096 + (tt + 1) * TT]
                            p.op("dve", lambda e, dst=dst: e.tensor_copy(dst, pb[:]), [pbB], [qB + "w1"])
                        else:
                            d0 = qk[par][0:64, tt * TT:(tt + 1) * TT]
                            d1 = qk[par][64:128, 2048 + tt * TT:2048 + (tt + 1) * TT]
                            p.op("dve", lambda e, d0=d0: e.tensor_copy(d0, pb[0:64, :]), [pbB], [qB + "w0"])
                            p.op("dve", lambda e, d1=d1: e.tensor_copy(d1, pb[64:128, :]), [pbB], [qB + "w0"])
                    groups.append(g)
            v3 = vh[par].rearrange("p (j e) -> p j e", e=129)
            for jg in range(4):
                def g(jg=jg):
                    pb, pbB = nextbank()
                    for jj in range(4):
                        jx = jg * 4 + jj
                        for dc in range(DC):
                            p.mm(pb[:, jj * 128:(jj + 1) * 128], p.h[:, dc, jx * 128:(jx + 1) * 128],
                                 wq[par][:, (16 + dc) * 128:(16 + dc + 1) * 128], dc == 0, dc == DC - 1,
                                 [wB, "h_%d_%d" % (dc, jx // 4)], [pbB])
                    src = pb[:].rearrange("p (j e) -> p j e", e=128)
                    p.op("dve", lambda e, src=src: e.tensor_copy(v3[:, jg * 4:(jg + 1) * 4, 0:128], src),
                         [pbB], [vB])
                groups.append(g)
            return groups

        def acc_ap(qi, m):
            return p.bank[qi][:, m * 129:m * 129 + 129], "bank%d" % qi

        later = []

        def tick():
            todo = list(later)
            del later[:]
            for item in todo:
                item[0] -= 1
                if item[0] <= 0:
                    item[1]()
                else:
                    later.append(item)

        trbk = p.bank[6][:].bitcast(BF16)

        def epilogue(hh, qb, qi):
            r = st_["ep"] % NACC
            st_["ep"] += 1
            r4 = r % 8
            Abank = p.bank[qi][:, 0:258]
            aB = "bank%d" % qi
            acc = accs[r]
            acB = "accs%d" % r
            c = sm[:, 8 + r * 4:12 + r * 4]
            cB = "smc%d" % r
            o = acc[:, 0:128]
            p.op("dve", lambda e: e.tensor_copy(acc, Abank), [aB], [acB])
            sums = acc.rearrange("p (m e) -> p m e", m=2)[:, :, 128:129]
            p.op("dve", lambda e: e.reciprocal(c[:, 0:2].unsqueeze(2), sums), [acB], [cB])
            p.tt("dve", c[:, 1:2], c[:, 1:2], nlam, ALU.mult, [cB, "sm"], [cB])

            def stage1():
                p.act(o, o, AF.Identity, [acB, cB], [acB], scale=c[:, 0:1])

            def stage2():
                p.stt("dve", o, acc[:, 129:257], c[:, 1:2], o, ALU.mult, ALU.add, [acB, cB], [acB])

            def stage3():
                p.op("act", lambda e: e.activation(junk, o, AF.Square, accum_out=c[:, 2:3]), [acB, cB], ["junk", cB])
                p.ts("dve", c[:, 2:3], c[:, 2:3], 1.0 / 128.0, LN_EPS, ALU.mult, ALU.add, [cB], [cB])

            def stage3b():
                p.act(c[:, 2:3], c[:, 2:3], AF.Ln, [cB], [cB])
                p.act(c[:, 2:3], c[:, 2:3], AF.Exp, [cB], [cB], scale=-0.5)

            def stage4():
                p.stt("dve", onb[r4], o, c[:, 2:3], gvec, ALU.mult, ALU.mult, [acB, cB, "gvec"], ["onb%d" % r4])

            tr = trbk[:, 0:128]

            def stage4b():
                p.op("pe", lambda e: e.transpose(tr, onb[r4], identb), ["onb%d" % r4, "identb"], ["bank6"])

            def stage5():
                dst = oT[:, hh * 2048 + qb * 128:hh * 2048 + (qb + 1) * 128]
                p.op("dve", lambda e: e.tensor_copy(dst, tr), ["bank6"], ["oT_%d_%d" % (hh, qb // 4)])
            later.append([1, stage1])
            later.append([2, stage2])
            later.append([3, stage3])
            later.append([4, stage3b])
            later.append([5, stage4])
            later.append([10, stage4b])
            later.append([11, stage5])

        SCB = ((2, 3), (4, 5))

        def core(hh, pending):
            par = hh % 2
            qB, vB = "qk%d" % par, "vh%d" % par
            q_ = qk[par]
            steps = [(QP, jx) for QP in range(8) for jx in range(2 * QP + 2)]

            def emit_scores(QP, jx):
                q0 = max(128 * jx, 256 * QP)
                q1 = 256 * (QP + 1)
                n = q1 - q0
                bb = 2 + st_["sc"] % 3
                st_["sc"] += 1
                pr = st_["pt"] % 5
                st_["pt"] += 1
                pt, ptB = pT[pr][:, 0:512], "pT%d" % pr
                for m in range(2):
                    p.mm(p.bank[bb][:, m * 256:m * 256 + n], q_[:, 4096 + 128 * jx:4096 + 128 * jx + 128],
                         q_[:, m * 2048 + q0:m * 2048 + q1], True, True, [qB + "w0", qB + "w1"], ["bank%d" % bb])
                pt3 = pt.rearrange("p (m q) -> p m q", m=2)
                sc3 = p.bank[bb][:].rearrange("p (m q) -> p m q", m=2)
                p.act(pt3[:, :, 0:n], sc3[:, :, 0:n], AF.Exp, ["bank%d" % bb], [ptB], scale=0.125)
                if jx >= 2 * QP:
                    p.op("pool", lambda e, pt3=pt3: e.memset(pt3[64:128, :, 0:64], 0.0), [ptB], [ptB])
                return (QP, jx, q0, pt, ptB)

            def emit_pv(ctx, bank_first):
                QP, jx, q0, pt, ptB = ctx
                for qi in range(2):
                    qb = 2 * QP + qi
                    if jx > qb:
                        continue
                    c0 = 128 * qb - q0
                    for m in range(2):
                        A, aB = acc_ap(qi, m)
                        first = bank_first[qi]
                        bank_first[qi] = False
                        p.mm(A, pt[:, m * 256 + c0:m * 256 + c0 + 128], vh[par][:, jx * 129:(jx + 1) * 129],
                             first, jx == qb, [ptB, vB], [aB], skip_group_check=True)
                    if jx == qb:
                        epilogue(hh, qb, qi)

            LOOK = 2
            ctxs = [emit_scores(*steps[k]) for k in range(LOOK)]
            bank_first = [True, True]
            for i, (QP, jx) in enumerate(steps):
                if i + LOOK < len(steps):
                    ctxs.append(emit_scores(*steps[i + LOOK]))
                ctx = ctxs.pop(0)
                if jx == 0:
                    bank_first = [True, True]
                emit_pv(ctx, bank_first)
                tick()
                if pending and (i + 1) % 5 == 0:
                    pending.pop(0)()
            while pending:
                pending.pop(0)()

        for g in proj_groups(0):
            g()
        for hh in range(8):
            pending = proj_groups(hh + 1) if hh + 1 < 8 else []
            core(hh, pending)
        for _ in range(13):
            tick()
        woutb = [wq[0][:, 0:2048], wq[1][:, 0:2048]]
        for hf in range(2):
            p.proj_resid_ln(s, hf, 8, lambda k, tt: (oT[:, k * 2048 + tt * TT:k * 2048 + (tt + 1) * TT],
                                                     "oT_%d_%d" % (k, tt)),
                            woutb, lambda dblk: p.wout_r[dblk], "wq", has_next, store)

    def s5(self, s, has_next, store):
        p = self
        LB = 4
        NBK = TT // LB
        TWO_PI = 2.0 * math.pi
        GK = 2.0 * math.sqrt(2.0 / math.pi)
        p.ln_setup()
        p.prep_coefs(s, has_next, 1.0)
        ab, af = p.arena_b, p.arena_f
        BBTr, BBTi, CCr, CCni, T0T = [ab[:, i * 4096:(i + 1) * 4096] for i in range(5)]
        selb = ab[:, 20480:22528]
        FB = 22528
        Ff = ab[:, FB:FB + 16384].bitcast(F32)
        F3 = Ff.rearrange("p (b c) -> p b c", c=64)
        U = ab[:, 38912:43008]
        U3 = U.rearrange("p (g b) -> p g b", b=NBK)
        sel2 = p.sel2_sb[:]
        PS = af[:, 4096:5120]
        Sb = af[:, 0:4096].bitcast(BF16)
        Sb3 = Sb.rearrange("p (b c) -> p b c", c=64)
        zT = ab[:, FB:FB + 4096]
        gz = ab[:, FB + 4096:FB + 8192]
        wbuf = [ab[:, FB + 8192 + i * 2048:FB + 8192 + (i + 1) * 2048] for i in range(2)]
        scr = [ab[:, FB + 12288 + i * 1024:FB + 12288 + (i + 1) * 1024].bitcast(F32) for i in range(4)]

        sv = lambda i: Ff[:, i * 32:(i + 1) * 32]
        A_RE, A_IM, LDT, DCOL, DT, AR, AI, T1, T2, T3, DEN, CFR, CFI = [sv(i) for i in range(13)]
        lamr = {k: sv(16 + 2 * (k + 4)) for k in (-4, -3, -2, -1, 0, 1, 2, 3, 4, 8)}
        lami = {k: sv(17 + 2 * (k + 4)) for k in (-4, -3, -2, -1, 0, 1, 2, 3, 4, 8)}
        KI = Ff[:, 1024:1056].bitcast(mybir.dt.int32)
        LAMA = PS[:, 0:64]
        LAMB = PS[:, 64:128]
        Wst = [PS[:, 128:224], PS[:, 224:320]]
        Tst = PS[:, 320:416]
        M1 = PS[:, 416:480]
        M2 = PS[:, 480:544]
        b_r, b_i, c_r, c_i = [Ff[:, 2048 + i * 512:2048 + (i + 1) * 512] for i in range(4)]
        Bb_r, Bb_i = Ff[:, 4096:4608], Ff[:, 4608:5120]
        maskLT, identf = Ff[:, 5120:5248], Ff[:, 5248:5376]
        tmpA, tmpB = Ff[:, 5376:5888], Ff[:, 5888:6400]
        tmpT = Ff[:, 6400:6528]
        tmpT4 = Ff[:, 6400:6912]
        dmat4 = Ff[:, 6912:7424]
        Z = [af[:, i * 1024:(i + 1) * 1024] for i in range(4)]
        SV = "s5sv"
        for dst, src in ((A_RE, p.s5_are), (A_IM, p.s5_aim), (LDT, p.s5_ldt), (DCOL, p.s5_dcol),
                         (b_r, p.s5_br), (b_i, p.s5_bi), (c_r, p.s5_cr), (c_i, p.s5_ci),
                         (maskLT, p.s5_mask), (identf, p.ident)):
            p.dma("sp", dst, src, (), [SV])
        p.dma("pool", selb, p.s5_sel, (), ["selb"], max_dma_last_dim=4096)
        p.dma("pool", sel2, p.s5_sel2, (), ["sel2"])
        V = [SV]
        p.act(DT, LDT, AF.Exp, V, V)
        p.tt("dve", AR, A_RE, DT, ALU.mult, V, V)
        p.tt("dve", AI, A_IM, DT, ALU.mult, V, V)
        p.ts("dve", T1, AI, 1.0 / TWO_PI, None, ALU.mult, None, V, V)
        p.op("dve", lambda e: e.tensor_copy(KI, T1), V, V)
        p.op("dve", lambda e: e.tensor_copy(T1, KI), V, V)
        p.stt("dve", T2, T1, -TWO_PI, AI, ALU.mult, ALU.add, V, V)
        p.ts("dve", T2, T2, -3.1415925, 3.1415925, ALU.max, ALU.min, V, V)
        p.act(T3, T2, AF.Abs, V, V)
        p.ts("dve", T3, T3, -1.0, math.pi / 2, ALU.mult, ALU.add, V, V)
        p.act(T2, T2, AF.Sin, V, V)
        p.act(T3, T3, AF.Sin, V, V)
        p.act(T1, AR, AF.Exp, V, V)
        p.tt("dve", lamr[1], T1, T3, ALU.mult, V, V)
        p.tt("dve", lami[1], T1, T2, ALU.mult, V, V)
        p.act(T1, AR, AF.Exp, V, V, scale=-1.0)
        p.tt("dve", lamr[-1], T1, T3, ALU.mult, V, V)
        p.tt("dve", lami[-1], T1, T2, ALU.mult, V, V)
        p.ts("dve", lami[-1], lami[-1], -1.0, None, ALU.mult, None, V, V)
        p.op("dve", lambda e: e.memset(lamr[0], 1.0), V, V)
        p.op("dve", lambda e: e.memset(lami[0], 0.0), V, V)

        def cmul(outr, outi, ar, ai_, br, bi, shape=None):
            ta, tb = (T1, DEN) if shape is None else (tmpA, tmpB)
            if shape is not None:
                ta = ta.rearrange("p (g e) -> p g e", e=16)
                tb = tb.rearrange("p (g e) -> p g e", e=16)
            p.tt("dve", ta, ar, br, ALU.mult, V, V)
            p.tt("dve", tb, ai_, bi, ALU.mult, V, V)
            p.tt("dve", ta, ta, tb, ALU.subtract, V, V)
            p.tt("dve", tb, ar, bi, ALU.mult, V, V)
            p.op("dve", lambda e: e.tensor_copy(outr, ta), V, V)
            p.tt("dve", ta, ai_, br, ALU.mult, V, V)
            p.tt("dve", outi, ta, tb, ALU.add, V, V)

        for k in (2, 3, 4):
            cmul(lamr[k], lami[k], lamr[k - 1], lami[k - 1], lamr[1], lami[1])
        for k in (-2, -3, -4):
            cmul(lamr[k], lami[k], lamr[k + 1], lami[k + 1], lamr[-1], lami[-1])
        cmul(lamr[8], lami[8], lamr[4], lami[4], lamr[4], lami[4])
        p.tt("dve", DEN, A_RE, A_RE, ALU.mult, V, V)
        p.tt("dve", T1, A_IM, A_IM, ALU.mult, V, V)
        p.tt("dve", DEN, DEN, T1, ALU.add, V, V)
        p.op("dve", lambda e: e.reciprocal(DEN, DEN), V, V)
        p.ts("dve", T3, lamr[1], -1.0, None, ALU.add, None, V, V)
        p.tt("dve", T1, T3, A_RE, ALU.mult, V, V)
        p.tt("dve", T2, lami[1], A_IM, ALU.mult, V, V)
        p.tt("dve", T1, T1, T2, ALU.add, V, V)
        p.tt("dve", CFR, T1, DEN, ALU.mult, V, V)
        p.tt("dve", T1, lami[1], A_RE, ALU.mult, V, V)
        p.tt("dve", T2, T3, A_IM, ALU.mult, V, V)
        p.tt("dve", T1, T1, T2, ALU.subtract, V, V)
        p.tt("dve", CFI, T1, DEN, ALU.mult, V, V)
        p.op("dve", lambda e: e.tensor_copy(LAMA[:, 0:32], lamr[LB]), V, V)
        p.op("dve", lambda e: e.tensor_copy(LAMA[:, 32:64], lamr[LB]), V, V)
        p.ts("dve", LAMB[:, 0:32], lami[LB], -1.0, None, ALU.mult, None, V, V)
        p.op("dve", lambda e: e.tensor_copy(LAMB[:, 32:64], lami[LB]), V, V)
        p.op("dve", lambda e: e.memset(Wst[0], 0.0), V, V + ["W0", "W0h"])
        LAMA2, LAMB2, NA4, NBS4, LBSW = [PS[:, 544 + i * 64:608 + i * 64] for i in range(5)]
        for half in range(2):
            hs = slice(half * 32, (half + 1) * 32)
            p.op("dve", lambda e, hs=hs: e.tensor_copy(LAMA2[:, hs], lamr[2 * LB]), V, V)
            p.op("dve", lambda e, hs=hs: e.tensor_copy(NA4[:, hs], lamr[-LB]), V, V)
        p.ts("dve", LAMB2[:, 0:32], lami[2 * LB], -1.0, None, ALU.mult, None, V, V)
        p.op("dve", lambda e: e.tensor_copy(LAMB2[:, 32:64], lami[2 * LB]), V, V)
        p.op("dve", lambda e: e.tensor_copy(NBS4[:, 0:32], lami[-LB]), V, V)
        p.ts("dve", NBS4[:, 32:64], lami[-LB], -1.0, None, ALU.mult, None, V, V)
        p.op("dve", lambda e: e.tensor_copy(LBSW[:, 0:32], lami[LB]), V, V)
        p.ts("dve", LBSW[:, 32:64], lami[LB], -1.0, None, ALU.mult, None, V, V)
        bc = lambda v: v.unsqueeze(2).to_broadcast([128, 32, 16])
        v3 = lambda t: t.rearrange("p (g e) -> p g e", e=16)
        cmul(v3(Bb_r), v3(Bb_i), bc(CFR), bc(CFI), v3(b_r), v3(b_i), shape=3)
        for ch in range(4):
            g0 = ch * 8
            Zr, Zi, ZCr, ZCni = [z.rearrange("p (g l h e) -> p g l h e", g=8, l=LB, h=2) for z in Z]
            ZB = ["s5ZB", "s5ZC"]
            zero8 = Ff[:, 7680:7808].rearrange("p (g e) -> p g e", e=16)
            if ch == 0:
                p.op("pool", lambda e: e.memset(zero8, 0.0), (), ["s5zero"])
            for (eng, outr, outi, sgn, xr, xi, neg, tA, tBf, tN, zN) in (
                    ("dve", Zr, Zi, -1, Bb_r, Bb_i, False, tmpA, tmpB, "s5tmpD", "s5ZB"),
                    ("pool", ZCr, ZCni, 1, c_r, c_i, True, Ff[:, 7424:7552], Ff[:, 7552:7680], "s5tmpP", "s5ZC")):
                ta = v3(tA)[:, 0:8, :]
                tb = v3(tBf)[:, 0:8, :]
                sl = lambda t: v3(t)[:, g0:g0 + 8, :]
                lb = lambda v: v[:, g0:g0 + 8].unsqueeze(2).to_broadcast([128, 8, 16])
                for l in range(LB):
                    kk = sgn * l
                    lr_, li_ = lb(lamr[kk]), lb(lami[kk])
                    p.tt(eng, ta, lr_, sl(xr), ALU.mult, V + [tN], [tN])
                    p.tt(eng, tb, li_, sl(xi), ALU.mult, V + [tN], [tN])
                    for h in range(2):
                        p.tt(eng, outr[:, :, l, h, :], ta, tb, ALU.subtract, [tN, zN], [zN])
                    p.tt(eng, ta, lr_, sl(xi), ALU.mult, V + [tN], [tN])
                    p.tt(eng, tb, li_, sl(xr), ALU.mult, V + [tN], [tN])
                    if neg:
                        p.tt(eng, ta, ta, tb, ALU.add, [tN], [tN])
                        for h in range(2):
                            p.tt(eng, outi[:, :, l, h, :], zero8, ta, ALU.subtract, [tN, zN, "s5zero"], [zN])
                    else:
                        for h in range(2):
                            p.tt(eng, outi[:, :, l, h, :], ta, tb, ALU.add, [tN, zN], [zN])
                for z4 in (outr, outi):
                    p.op(eng, lambda e, z4=z4: e.memset(z4[0:64, :, :, 1, :], 0.0), [zN], [zN])
                    p.op(eng, lambda e, z4=z4: e.memset(z4[64:128, :, :, 0, :], 0.0), [zN], [zN])
            ccol = slice(g0 * 128, (g0 + 8) * 128)
            p.op("act", lambda e, ccol=ccol: e.copy(CCr[:, ccol], Z[2]), ZB, ["s5tab"])
            p.op("act", lambda e, ccol=ccol: e.copy(CCni[:, ccol], Z[3]), ZB, ["s5tab"])
            for q4 in range(2):
                cols4 = slice((g0 + 4 * q4) * 128, (g0 + 4 * q4 + 4) * 128)
                for i, dstT in ((0, BBTr), (1, BBTi)):
                    b = i
                    for gj in range(4):
                        gi = 4 * q4 + gj
                        p.mm(p.bank[b][:, gj * 128:(gj + 1) * 128], Z[i][:, gi * 128:(gi + 1) * 128], identf,
                             True, True, ZB + [SV], ["bank%d" % b])
                    p.op("act", lambda e, dstT=dstT, cols4=cols4, b=b: e.copy(dstT[:, cols4], p.bank[b][:]),
                         ["bank%d" % b], ["s5tab"])
                b = 2 + q4
                for gj in range(4):
                    gi = 4 * q4 + gj
                    o_ = p.bank[b][:, gj * 128:(gj + 1) * 128]
                    p.mm(o_, Z[0][:, gi * 128:(gi + 1) * 128], Z[2][:, gi * 128:(gi + 1) * 128], True, False,
                         ZB, ["bank%d" % b])
                    p.mm(o_, Z[1][:, gi * 128:(gi + 1) * 128], Z[3][:, gi * 128:(gi + 1) * 128], False, True,
                         ZB, ["bank%d" % b])
                m4 = maskLT.unsqueeze(1).to_broadcast([128, 4, 128])
                i4 = identf.unsqueeze(1).to_broadcast([128, 4, 128])
                d4 = DCOL[:, g0 + 4 * q4:g0 + 4 * q4 + 4].unsqueeze(2).to_broadcast([128, 4, 128])
                t4 = tmpT4.rearrange("p (g c) -> p g c", g=4)
                dm4 = dmat4.rearrange("p (g c) -> p g c", g=4)
                p.tt("dve", t4, p.bank[b][:].rearrange("p (g c) -> p g c", g=4), m4, ALU.mult,
                     ["bank%d" % b, SV], ["s5tmpT"])
                p.tt("dve", dm4, i4, d4, ALU.mult, [SV], ["s5dm"])
                p.tt("dve", T0T[:, cols4], tmpT4, dmat4, ALU.add, ["s5tmpT", "s5dm"], ["s5tab"])
        p.dump("small", Ff[:, 0:1664], [SV], 1664)
        p.dump("Bb", Ff[:, 4096:5120], [SV], 1024)
        p.dump("BBTr", BBTr, ["s5tab"], 4096)
        p.dump("BBTi", BBTi, ["s5tab"], 4096)
        p.dump("CCr", CCr, ["s5tab"], 4096)
        p.dump("CCni", CCni, ["s5tab"], 4096)
        p.dump("T0T", T0T, ["s5tab"], 4096)
        p.sc.barrier()

        for tt in range(4):
            uT = zT
            for blk in range(4):
                bi = blk % 2
                p.dma("pool", wbuf[bi], p.s5_win[blk], (), ["wbuf%d" % bi], max_dma_last_dim=4096)
                for oi in range(2):
                    oc = blk * 2 + oi
                    b = oc % 4
                    for dc in range(DC):
                        p.mm(p.bank[b][:], wbuf[bi][:, dc * 256 + oi * 128:dc * 256 + (oi + 1) * 128],
                             p.ht(dc, tt), dc == 0, dc == DC - 1,
                             ["wbuf%d" % bi, "h_%d_%d" % (dc, tt)], ["bank%d" % b])
                    dst = uT[:, oc * 512:(oc + 1) * 512]
                    if oc % 2:
                        p.op("act", lambda e, dst=dst, b=b: e.copy(dst, p.bank[b][:]), ["bank%d" % b],
                             ["uT%d" % oc, "zT%d" % oc])
                    else:
                        p.op("dve", lambda e, dst=dst, b=b: e.tensor_copy(dst, p.bank[b][:]), ["bank%d" % b],
                             ["uT%d" % oc, "zT%d" % oc])
                    if p.deferred:
                        p.deferred.pop(0)()
            last_scr = None
            for pi_ in range(32):
                dc, j = divmod(pi_, 4)
                b = pi_ % 4
                for l in range(LB):
                    rhs = uT[:, dc * 512 + l:(dc + 1) * 512:LB]
                    last_scr = p.mm(p.bank[b][32 * l:32 * l + 32, 0:NBK], sel2[:, j * 32:(j + 1) * 32], rhs,
                                    True, True, ["sel2", "uT%d" % dc], ["bank%d" % b], tile_position=(0, 32 * l))
                if pi_ % 2:
                    p.op("act", lambda e, pi_=pi_, b=b: e.copy(U3[:, pi_, :], p.bank[b][:, 0:NBK]),
                         ["bank%d" % b], ["U%d" % pi_])
                else:
                    p.op("dve", lambda e, pi_=pi_, b=b: e.tensor_copy(U3[:, pi_, :], p.bank[b][:, 0:NBK]),
                         ["bank%d" % b], ["U%d" % pi_])
            first_f = {"dve": True, "act": True}
            for pi_ in range(32):
                col = slice(pi_ * 128, (pi_ + 1) * 128)
                for i, tabT in ((0, BBTr), (1, BBTi)):
                    b = (2 * pi_ + i) % 4
                    p.mm(p.bank[b][:, 0:NBK], tabT[:, col], U3[:, pi_, :], True, True,
                         ["s5tab", "U%d" % pi_], ["bank%d" % b])
                    dst = F3[:, :, i * 32 + pi_]
                    if i == 0:
                        fi = p.op("dve", lambda e, dst=dst, b=b: e.tensor_copy(dst, p.bank[b][:, 0:NBK]),
                                  ["bank%d" % b], ["F"])
                    else:
                        fi = p.op("act", lambda e, dst=dst, b=b: e.copy(dst, p.bank[b][:, 0:NBK]),
                                  ["bank%d" % b], ["F"])
                    if first_f[fi.eng]:
                        first_f[fi.eng] = False
                        fi.deps.append(last_scr)
            p.flush_deferred()
            NH = NBK // 2
            HSPL = 44
            RNG = ((0, HSPL), (HSPL, NH))
            tmpf = af[:, 0:4096].rearrange("p (b c) -> p b c", c=64)
            for hh_, eng in ((0, "dve"), (1, "pool")):
                lo_, hi_ = RNG[hh_]
                bcb = lambda t, n_=hi_ - lo_: t.unsqueeze(1).to_broadcast([128, n_, 64])
                Fe = F3[:, 2 * lo_:2 * hi_:2, :]
                Fo = F3[:, 2 * lo_ + 1:2 * hi_:2, :]
                tf = tmpf[:, lo_:hi_, :]
                fB, tB = "Fh%d" % hh_, "tmph%d" % hh_
                RB = ["F", SV]
                LNB = ["mean0", "mean1", "rstd0", "rstd1", "tsc0", "tsc1"]
                p.tt(eng, tf, bcb(NA4), Fo, ALU.mult, RB, [tB] + LNB)
                p.tt(eng, tf, tf, Fe, ALU.add, RB + [tB], [tB])
                p.tt(eng, Fo, Fo, bcb(NBS4), ALU.mult, RB, [fB])
                p.tt(eng, tf[:, :, 0:32], tf[:, :, 0:32], Fo[:, :, 32:64], ALU.add, [fB, tB], [tB])
                p.tt(eng, tf[:, :, 32:64], tf[:, :, 32:64], Fo[:, :, 0:32], ALU.add, [fB, tB], [tB])
                p.op(eng, lambda e, Fo=Fo, tf=tf: e.tensor_copy(Fo, tf), [tB, fB], [fB])
            for k2 in range(NH):
                g = tt * NH + k2
                fB, tB = "Fh%d" % (k2 >= HSPL), "tmph%d" % (k2 >= HSPL)
                Wc, Wn = Wst[g % 2], Wst[(g + 1) % 2]
                wcB, wnB = "W%d" % (g % 2), "W%d" % ((g + 1) % 2)
                p.op("act", lambda e, k2=k2, Wc=Wc: e.copy(Sb3[:, 2 * k2, :], Wc[:, 0:64]), [wcB, SV], ["Sb", tB])
                p.tt("dve", Tst[:, 0:64], Wc[:, 0:64], F3[:, 2 * k2 + 1, :], ALU.add, [wcB, fB, SV], ["Tlo"])
                p.tt("pool", Tst[:, 64:96], Wc[:, 64:96], F3[:, 2 * k2 + 1, 0:32], ALU.add, [wcB + "h", fB, SV], ["Thi"])
                p.tt("dve", M1, LAMA2, Tst[:, 0:64], ALU.mult, ["Tlo", SV], ["M1"])
                p.tt("pool", M2, LAMB2, Tst[:, 32:96], ALU.mult, ["Tlo", "Thi", SV], ["M2"])
                p.tt("dve", Wn[:, 0:64], M1, M2, ALU.add, ["M1", "M2"], [wnB])
                p.tt("pool", Wn[:, 64:96], M1[:, 0:32], M2[:, 0:32], ALU.add, ["M1", "M2"], [wnB + "h"])
            for hh_, eng in ((0, "dve"), (1, "pool")):
                lo_, hi_ = RNG[hh_]
                bcb = lambda t, n_=hi_ - lo_: t.unsqueeze(1).to_broadcast([128, n_, 64])
                Fe = F3[:, 2 * lo_:2 * hi_:2, :]
                Fo = F3[:, 2 * lo_ + 1:2 * hi_:2, :]
                Se = Sb3[:, 2 * lo_:2 * hi_:2, :]
                So = Sb3[:, 2 * lo_ + 1:2 * hi_:2, :]
                fB, soB = "Fh%d" % hh_, "So%d" % hh_
                p.tt(eng, Fe, Fe, Se, ALU.add, [fB, "F", "Sb", SV], [fB])
                p.tt(eng, Fo, bcb(LAMA), Fe, ALU.mult, [fB, SV], [fB])
                p.tt(eng, Fe, Fe, bcb(LBSW), ALU.mult, [fB, SV], [fB])
                p.tt(eng, So[:, :, 0:32], Fo[:, :, 0:32], Fe[:, :, 32:64], ALU.add, [fB], [soB])
                p.tt(eng, So[:, :, 32:64], Fo[:, :, 32:64], Fe[:, :, 0:32], ALU.add, [fB], [soB])
            if tt == 0:
                p.dump("U", U, ["U%d" % i for i in range(32)], 4096)
                p.dump("F", Ff, ["F"], 8192)
                p.dump("Sb", Sb, ["Sb", "So0", "So1"], 8192)
            p.sc.barrier()
            for q4 in range(8):
                b = q4 % 4
                for j in range(4):
                    pi_ = q4 * 4 + j
                    col = slice(pi_ * 128, (pi_ + 1) * 128)
                    o_ = p.bank[b][:, j * NBK:(j + 1) * NBK]
                    p.mm(o_, T0T[:, col], U3[:, pi_, :], True, False, ["s5tab", "Uq%d" % q4], ["bank%d" % b])
                    p.mm(o_, CCr[:, col], Sb3[:, :, pi_], False, False, ["s5tab", "Sb", "So0", "So1"], ["bank%d" % b])
                    p.mm(o_, CCni[:, col], Sb3[:, :, 32 + pi_], False, True, ["s5tab", "Sb", "So0", "So1"], ["bank%d" % b])
                r = q4 % 2
                sq, inner = scr[2 * r], scr[2 * r + 1]
                Y = p.bank[b][:]
                yB = "bank%d" % b
                p.act(sq, Y, AF.Square, [yB], ["sq%d" % r])
                p.ts("dve", sq, sq, 0.044715, 1.0, ALU.mult, ALU.add, ["sq%d" % r], ["sq%d" % r])
                p.tt("dve", inner, sq, Y, ALU.mult, ["sq%d" % r, yB], ["in%d" % r])
                p.act(inner, inner, AF.Sigmoid, ["in%d" % r], ["in%d" % r], scale=GK)
                p.tt("dve", U[:, q4 * 512:(q4 + 1) * 512], inner, Y, ALU.mult, ["in%d" % r, yB], ["Uq%d" % q4])
            for dc in range(DC):
                b = dc % 4
                for l in range(LB):
                    for j in range(4):
                        pi_ = dc * 4 + j
                        p.mm(p.bank[b][:, l * NBK:(l + 1) * NBK],
                             selb[:, (j * LB + l) * 128:(j * LB + l + 1) * 128], U3[:, pi_, :],
                             j == 0, j == 3, ["selb", "Uq%d" % dc], ["bank%d" % b])
                dst = zT[:, dc * 512:(dc + 1) * 512].rearrange("p (b l) -> p l b", l=LB)
                src = p.bank[b][:].rearrange("p (l b) -> p l b", l=LB)
                if dc % 2:
                    p.op("act", lambda e, dst=dst, src=src: e.copy(dst, src), ["bank%d" % b], ["zT%d" % dc])
                else:
                    p.op("dve", lambda e, dst=dst, src=src: e.tensor_copy(dst, src), ["bank%d" % b], ["zT%d" % dc])
            if tt == 0:
                p.dump("zs", U, ["Uq%d" % i for i in range(8)], 4096)
                p.dump("zT", zT, ["zT%d" % i for i in range(8)], 4096)
            p.sc.barrier()
            nsc = 0
            for blk in range(4):
                bi = blk % 2
                p.dma("pool", wbuf[bi], p.s5_wg[blk], (), ["wbuf%d" % bi], max_dma_last_dim=4096)
                for oi in range(2):
                    oc = blk * 2 + oi
                    b = oc % 4
                    for dc in range(DC):
                        p.mm(p.bank[b][:], wbuf[bi][:, dc * 256 + oi * 128:dc * 256 + (oi + 1) * 128],
                             zT[:, dc * 512:(dc + 1) * 512], dc == 0, dc == DC - 1,
                             ["wbuf%d" % bi, "zT%d" % dc], ["bank%d" % b])
                    r = nsc % 4
                    nsc += 1
                    p.act(scr[r], p.bank[b][:], AF.Sigmoid, ["bank%d" % b], ["scr%d" % r])
                    p.tt("dve", gz[:, oc * 512:(oc + 1) * 512], scr[r], zT[:, oc * 512:(oc + 1) * 512], ALU.mult,
                         ["scr%d" % r, "zT%d" % oc], ["gz%d" % oc])
            p.proj_resid_ln(s, 0, 8, lambda k, t_: (gz[:, k * 512:(k + 1) * 512], "gz%d" % k),
                            wbuf, lambda dblk: p.s5_wo[dblk], "wbuf", has_next, store, tts=(tt,),
                            defer=(tt < 3))

    def build(self):
        p = self
        p.out_dmas = []
        p.deferred = []
        p.prologue()
        subs = p.subs
        p.first_modulate(subs[0])
        for i, s in enumerate(subs):
            last = i == len(subs) - 1
            if s % 3 != 1:
                p.ffn(s, not last, last)
            elif s == 1:
                p.attn(s, not last, last)
            else:
                p.s5(s, not last, last)
            if not last:
                nxt = subs[i + 1]
                if s % 3 != 1 and nxt % 3 != 1:
                    continue
                p.flush_deferred()
                p.sc.barrier()
        p.flush_deferred()
        p.sc.emit(p.nc, p.out_dmas)
        return p.nc


def host_layout(inp):
    f = lambda a: np.ascontiguousarray(a, dtype=np.float32)
    out = {}
    ada_w = np.asarray(inp["ada_w"])
    out["ada_r"] = f(ada_w.reshape(2, DC, 128, 18, 512).transpose(0, 3, 2, 1, 4).reshape(2, 18, 128, DC * 512))
    ada_b = np.asarray(inp["ada_b"])
    out["ada_b"] = f(ada_b.reshape(2, 72, 128).transpose(2, 0, 1).reshape(128, 144))
    out["ln_g"] = f(np.asarray(inp["ln_g"]).reshape(6, DC, 128).transpose(2, 0, 1).reshape(128, 6 * DC))
    out["ln_b"] = f(np.asarray(inp["ln_b"]).reshape(6, DC, 128).transpose(2, 0, 1).reshape(128, 6 * DC))
    w1 = np.asarray(inp["ffn_w1"])
    w3 = np.asarray(inp["ffn_w3"])
    w2 = np.asarray(inp["ffn_w2"])
    out["w1r"] = f(w1.reshape(2, 2, DC, 128, 11, 256).transpose(0, 1, 4, 3, 2, 5).reshape(2, 2, 11, 128, DC * 256))
    out["w3r"] = f(w3.reshape(2, 2, DC, 128, 11, 256).transpose(0, 1, 4, 3, 2, 5).reshape(2, 2, 11, 128, DC * 256))
    out["w2r"] = f(w2.reshape(2, 2, FC, 128, 4, 256).transpose(0, 1, 4, 3, 2, 5).reshape(2, 2, 4, 128, FC * 256))
    win = np.asarray(inp["attn_w_in"])[0]
    out["wqkv_r"] = f(win.reshape(DC, 128, 3, 8, 128).transpose(3, 1, 2, 0, 4).reshape(8, 128, 3072))
    wo = np.asarray(inp["attn_w_out"])[0]
    out["wout_r"] = f(wo.reshape(8, 128, 4, 256).transpose(2, 1, 0, 3).reshape(4, 128, 8 * 256))
    out["lam_b"] = f(np.broadcast_to(np.asarray(inp["attn_lam"])[0].reshape(1, 256), (128, 256)))
    out["subln_b"] = f(np.broadcast_to(np.asarray(inp["attn_subln_g"])[0].reshape(1, 128), (128, 128)))
    out["ident"] = np.eye(128, dtype=np.float32)
    gn = lambda a: f(np.asarray(a)[0].reshape(32, 2, 64).transpose(1, 2, 0).reshape(128, 32))
    out["s5_are"] = gn(inp["ssm_a_re"])
    out["s5_aim"] = gn(inp["ssm_a_im"])
    ldt = np.asarray(inp["ssm_log_dt"])[0].reshape(32, 2)
    out["s5_ldt"] = f(np.broadcast_to(ldt.transpose(1, 0)[:, None, :], (2, 64, 32)).reshape(128, 32))
    dd = np.asarray(inp["ssm_d"])[0].reshape(32, 2, 16).transpose(1, 2, 0).reshape(32, 32)
    out["s5_dcol"] = f(np.tile(dd, (4, 1)))
    bl = lambda a: f(np.asarray(a)[0].reshape(32, 2, 64, 16).transpose(1, 2, 0, 3).reshape(128, 512))
    cl = lambda a: f(np.asarray(a)[0].reshape(32, 2, 16, 64).transpose(1, 3, 0, 2).reshape(128, 512))
    out["s5_br"] = bl(inp["ssm_b_re"])
    out["s5_bi"] = bl(inp["ssm_b_im"])
    out["s5_cr"] = cl(inp["ssm_c_re"])
    out["s5_ci"] = cl(inp["ssm_c_im"])
    lrow = np.arange(128) // 32
    out["s5_mask"] = f((lrow[None, :] >= lrow[:, None]))
    sel = np.zeros((4, 4, 128, 128), np.float32)
    for j in range(4):
        for l in range(4):
            for gp in range(2):
                for pp in range(16):
                    sel[j, l, l * 32 + gp * 16 + pp, (2 * j + gp) * 16 + pp] = 1.0
    out["s5_sel"] = f(sel.reshape(16, 128, 128).transpose(1, 0, 2).reshape(128, 16 * 128))
    wblk = lambda w: f(np.asarray(w)[0].reshape(DC, 128, 4, 256).transpose(2, 1, 0, 3).reshape(4, 128, DC * 256))
    out["s5_win"] = wblk(inp["ssm_w_in"])
    sel2 = np.zeros((128, 4, 32), np.float32)
    for j in range(4):
        for gp in range(2):
            for pp in range(16):
                sel2[(2 * j + gp) * 16 + pp, j, gp * 16 + pp] = 1.0
    out["s5_sel2"] = f(sel2.reshape(128, 128))
    out["s5_wg"] = wblk(inp["ssm_w_gate"])
    out["s5_wo"] = wblk(inp["ssm_w_out"])
    return out


def core_inputs(shared, x_b, c_b):
    m = dict(shared)
    m["xT"] = np.ascontiguousarray(np.asarray(x_b, dtype=np.float32).T)
    m["cvec"] = np.ascontiguousarray(np.asarray(c_b, dtype=np.float32).reshape(DC, 128).T)
    return m


_NC_CACHE = {}


def run_subs(subs, shared, x, c, n_cores=NB):
    key = tuple(subs)
    if key not in _NC_CACHE:
        bld = Builder(list(subs))
        _NC_CACHE[key] = bld.build()
        if DEBUG:
            global LAST_MAP
            LAST_MAP = bld.dbg_map
    nc = _NC_CACHE[key]
    in_maps = [core_inputs(shared, x[b], c[b]) for b in range(n_cores)]
    res = run_bass_kernel_spmd(nc, in_maps, core_ids=list(range(n_cores)))
    if DEBUG:
        global LAST_DBG
        LAST_DBG = (res.results[0]["dbgf"], res.results[0]["dbgb"])
    return np.stack([np.ascontiguousarray(r["yT"].T) for r in res.results], axis=0)


def kernel(**inputs):
    shared = host_layout(inputs)
    x = np.asarray(inputs["x"], dtype=np.float32)
    c = np.asarray(inputs["c"], dtype=np.float32)
    y = run_subs([0, 1, 2, 3, 4, 5], shared, x, c)
    return y.astype(np.float32)
```

```python
import math
from contextlib import ExitStack

import numpy as np
import concourse.bass as bass
import concourse.mybir as mybir
from concourse.bass_utils import run_bass_kernel_spmd

F32 = mybir.dt.float32
BF16 = mybir.dt.bfloat16
AF = mybir.ActivationFunctionType
ALU = mybir.AluOpType

D = 1024
S = 2048
NB = 8
DFF = 2816
DC = D // 128
FC = DFF // 128
TT = 512
DEPTH = 2
ALPHA = (2 * DEPTH) ** 0.25
LN_EPS = 1e-5
ENGS = ("pe", "act", "dve", "pool", "sp")
DEBUG = False
MERGE_EXP = True


class Buf:
    __slots__ = ("name", "lw", "rs")

    def __init__(self, name):
        self.name = name
        self.lw = None
        self.rs = []


class Ins:
    __slots__ = ("eng", "fn", "deps", "signal", "cnt", "dma", "dsem", "dval", "dprev", "waits")

    def __init__(self, eng, fn, dma):
        self.eng = eng
        self.fn = fn
        self.dma = dma
        self.deps = []
        self.signal = False
        self.cnt = 0
        self.dsem = None
        self.dval = 0
        self.dprev = 0
        self.waits = []


class Sched:
    def __init__(self, n_dma_slots=28):
        self.streams = {e: [] for e in ENGS}
        self.n_dma_slots = n_dma_slots
        self.dma_rr = {e: 0 for e in ENGS}
        self.dma_cum = {}
        self.dma_last = {}
        self.dma_recent = {}
        self.bar = {e: [] for e in ENGS}
        self.bufs = {}

    def buf(self, name):
        b = self.bufs.get(name)
        if b is None:
            b = self.bufs[name] = Buf(name)
        return b

    def op(self, eng, fn, reads=(), writes=(), dma=False):
        ins = Ins(eng, fn, dma)
        deps = set()
        if self.bar[eng]:
            deps.update(self.bar[eng])
            self.bar[eng] = []
        for b in reads:
            if isinstance(b, str):
                b = self.buf(b)
            if b.lw is not None:
                deps.add(b.lw)
        wl = []
        for b in writes:
            if isinstance(b, str):
                b = self.buf(b)
            wl.append(b)
            deps.update(b.rs)
            if b.lw is not None:
                deps.add(b.lw)
        for b in reads:
            if isinstance(b, str):
                b = self.buf(b)
            b.rs.append(ins)
        for b in wl:
            b.lw = ins
            b.rs = []
        if dma:
            slot = self.dma_rr[eng]
            self.dma_rr[eng] = (slot + 1) % self.n_dma_slots
            key = (eng, slot)
            ins.dsem = key
            ins.dprev = self.dma_cum.get(key, 0)
            ins.dval = ins.dprev + 16
            self.dma_cum[key] = ins.dval
            self.dma_last[key] = ins
            self.dma_recent[key] = ins
        ins.deps = [d for d in deps if d is not ins]
        self.streams[eng].append(ins)
        return ins

    def barrier(self):
        deps = []
        for e in ENGS:
            if self.streams[e]:
                deps.append(self.streams[e][-1])
        deps.extend(self.dma_recent.values())
        self.dma_recent = {}
        nop = self.op("sp", lambda eng: eng.nop(nofuse=True))
        nop.deps = list(set(nop.deps) | set(deps))
        for e in ENGS:
            if e != "sp":
                self.bar[e] = [nop]
        return nop

    def finalize(self):
        for e in ENGS:
            for ins in self.streams[e]:
                for d in ins.deps:
                    if not d.dma:
                        if d.eng == "pe" and ins.eng == "pe":
                            continue
                        d.signal = True
        for e in ENGS:
            c = 0
            for ins in self.streams[e]:
                if ins.signal:
                    c += 1
                    ins.cnt = c
        for e in ENGS:
            known = {}
            for ins in self.streams[e]:
                need = {}
                for d in ins.deps:
                    if d.dma:
                        k = ("dma",) + d.dsem
                        v = d.dval
                    else:
                        if d.eng == "pe" and e == "pe":
                            continue
                        k = ("eng", d.eng)
                        v = d.cnt
                    if v > need.get(k, 0):
                        need[k] = v
                if ins.dma and ins.dprev > 0:
                    k = ("dma",) + ins.dsem
                    if ins.dprev > need.get(k, 0):
                        need[k] = ins.dprev
                ws = []
                for k, v in need.items():
                    if known.get(k, 0) >= v:
                        continue
                    known[k] = v
                    ws.append((k, v))
                ins.waits = ws

    def emit(self, nc, final_waits):
        self.finalize()
        with ExitStack() as es:
            esem = {e: es.enter_context(nc.semaphore("s_" + e)) for e in ENGS}
            dsem = {}
            for key in self.dma_cum:
                dsem[key] = es.enter_context(nc.semaphore("d_%s_%d" % key))
            block = es.enter_context(nc.Block())

            def run(engname):
                def body(eng):
                    for ins in self.streams[engname]:
                        for k, v in ins.waits:
                            if k[0] == "eng":
                                eng.wait_ge(esem[k[1]], v)
                            else:
                                eng.wait_ge(dsem[(k[1], k[2])], v)
                        r = ins.fn(eng)
                        if ins.dma:
                            r.then_inc(dsem[ins.dsem], 16)
                        elif ins.signal:
                            r.then_inc(esem[engname], 1)
                    if engname == "sp":
                        for d in final_waits:
                            eng.wait_ge(dsem[d.dsem], d.dval)
                return body

            block.tensor(run("pe"))
            block.scalar(run("act"))
            block.vector(run("dve"))
            block.gpsimd(run("pool"))
            block.sync(run("sp"))


class Prog:
    def __init__(self, subs, first_in_raw, last_out_raw):
        self.subs = subs
        self.nc = bass.Bass("TRN2", target_bir_lowering=False)
        self.sc = Sched()
        self.es = ExitStack()
        self.first_in_raw = first_in_raw
        self.last_out_raw = last_out_raw
        self.dram = {}

    def din(self, name, shape, dt=F32):
        t = self.nc.dram_tensor(name, list(shape), dt, kind="ExternalInput").ap()
        self.dram[name] = t
        return t

    def dout(self, name, shape, dt=F32):
        t = self.nc.dram_tensor(name, list(shape), dt, kind="ExternalOutput").ap()
        self.dram[name] = t
        return t

    def sb(self, name, shape, dt):
        return self.es.enter_context(self.nc.sbuf_tensor(name, list(shape), dt))

    def ps(self, name, shape, dt=F32):
        return self.es.enter_context(self.nc.psum_tensor(name, list(shape), dt))

    def op(self, eng, fn, reads=(), writes=(), dma=False):
        return self.sc.op(eng, fn, reads, writes, dma)

    def dump(self, name, ap, reads, n):
        if not getattr(self, "dbg", False):
            return
        kind = "f" if ap.dtype == F32 else "b"
        o = self.dbg_off[kind]
        self.dbg_off[kind] = o + n
        self.dbg_map[name] = (kind, o, n)
        dst = (self.dbgf if kind == "f" else self.dbgb)[:, o:o + n]
        d = self.dma("sp", dst, ap, reads, [])
        self.out_dmas.append(d)

    def mm(self, out, lhsT, rhs, start, stop, reads, writes, **kw):
        return self.op("pe", lambda e: e.matmul(out, lhsT, rhs, start=start, stop=stop, **kw),
                       reads, writes)

    def act(self, out, in_, func, reads, writes, bias=None, scale=None, eng="act"):
        kw = {}
        if bias is not None:
            kw["bias"] = bias
        if scale is not None:
            kw["scale"] = scale
        return self.op(eng, lambda e: e.activation(out, in_, func, **kw), reads, writes)

    def ts(self, eng, out, in0, s1, s2, op0, op1, reads, writes):
        if s2 is None:
            return self.op(eng, lambda e: e.tensor_scalar(out, in0, s1, None, op0), reads, writes)
        return self.op(eng, lambda e: e.tensor_scalar(out, in0, s1, s2, op0, op1), reads, writes)

    def tt(self, eng, out, in0, in1, op, reads, writes):
        return self.op(eng, lambda e: e.tensor_tensor(out, in0, in1, op), reads, writes)

    def stt(self, eng, out, in0, scalar, in1, op0, op1, reads, writes):
        return self.op(eng, lambda e: e.scalar_tensor_tensor(out, in0, scalar, in1, op0, op1),
                       reads, writes)

    def dma(self, eng, out, in_, reads, writes, **kw):
        return self.op(eng, lambda e: e.dma_start(out=out, in_=in_, **kw), reads, writes, dma=True)


def _col(t, j):
    return t[:, j:j + 1]


class Builder(Prog):
    def __init__(self, subs):
        super().__init__(subs, True, True)
        self.layers = sorted(set(s // 3 for s in subs))
        self.declare_io()
        self.alloc_common()

    def declare_io(self):
        self.xT = self.din("xT", [D, S])
        self.cvec = self.din("cvec", [128, DC])
        self.ada_r = self.din("ada_r", [2, 18, 128, DC * 512])
        self.ada_b = self.din("ada_b", [128, 2 * 72])
        self.ln_g = self.din("ln_g", [128, 6 * DC])
        self.ln_b = self.din("ln_b", [128, 6 * DC])
        self.w1r = self.din("w1r", [2, 2, 11, 128, DC * 256])
        self.w3r = self.din("w3r", [2, 2, 11, 128, DC * 256])
        self.w2r = self.din("w2r", [2, 2, 4, 128, FC * 256])
        self.wqkv_r = self.din("wqkv_r", [8, 128, 3072])
        self.wout_r = self.din("wout_r", [4, 128, 8 * 256])
        self.lam_b = self.din("lam_b", [128, 256])
        self.subln_b = self.din("subln_b", [128, 128])
        self.ident = self.din("ident", [128, 128])
        self.s5_are = self.din("s5_are", [128, 32])
        self.s5_aim = self.din("s5_aim", [128, 32])
        self.s5_ldt = self.din("s5_ldt", [128, 32])
        self.s5_dcol = self.din("s5_dcol", [128, 32])
        self.s5_br = self.din("s5_br", [128, 512])
        self.s5_bi = self.din("s5_bi", [128, 512])
        self.s5_cr = self.din("s5_cr", [128, 512])
        self.s5_ci = self.din("s5_ci", [128, 512])
        self.s5_mask = self.din("s5_mask", [128, 128])
        self.s5_sel = self.din("s5_sel", [128, 16 * 128])
        self.s5_win = self.din("s5_win", [4, 128, DC * 256])
        self.s5_sel2 = self.din("s5_sel2", [128, 128])
        self.s5_wg = self.din("s5_wg", [4, 128, DC * 256])
        self.s5_wo = self.din("s5_wo", [4, 128, DC * 256])
        self.yT = self.dout("yT", [D, S])
        self.dbg = DEBUG
        if self.dbg:
            self.dbgf = self.dout("dbgf", [128, 16384])
            self.dbgb = self.dout("dbgb", [128, 65536], BF16)
            self.dbg_off = {"f": 0, "b": 0}
            self.dbg_map = {}

    def alloc_common(self):
        self.x = self.sb("x", [128, DC, S], F32)
        self.h = self.sb("h", [128, DC, S], BF16)
        self.onesb = self.sb("onesb", [128, 128], BF16)
        self.sel2_sb = self.sb("sel2_sb", [128, 128], BF16)
        self.mods = self.sb("mods", [128, 2 * 72], F32)
        self.adab = self.sb("adab", [128, 2 * 72], F32)
        self.lng = self.sb("lng", [128, 6 * DC], F32)
        self.lnb = self.sb("lnb", [128, 6 * DC], F32)
        self.cond = self.sb("cond", [128, DC], F32)
        self.condb = self.sb("condb", [128, DC], BF16)
        self.coef = self.sb("coef", [128, 6, 8 * DC], F32)
        self.psum = self.ps("psum", [128, 8, TT], F32)
        self.bank = [self.psum[:, i, :] for i in range(8)]
        self.arena_b = self.sb("arena_b", [128, 44 * 1024], BF16)
        self.arena_f = self.sb("arena_f", [128, 5 * 1024], F32)

    def xt(self, dc, tt):
        return self.x[:, dc, tt * TT:(tt + 1) * TT]

    def ht(self, dc, tt):
        return self.h[:, dc, tt * TT:(tt + 1) * TT]

    def prologue(self):
        p = self
        p.op("pool", lambda e: e.memset(p.onesb[:], 1.0 / D), (), ["onesb"])
        p.dma("sp", p.cond[:], p.cvec, (), ["cond"])
        p.dma("sp", p.adab[:], p.ada_b, (), ["adab"])
        p.dma("sp", p.lng[:], p.ln_g, (), ["lng"])
        p.dma("sp", p.lnb[:], p.ln_b, (), ["lnb"])
        for dc in range(DC):
            for hf in range(2):
                p.dma("sp", p.x[:, dc, hf * 1024:(hf + 1) * 1024],
                      p.xT[dc * 128:(dc + 1) * 128, hf * 1024:(hf + 1) * 1024],
                      (), ["x_%d_%d" % (dc, 2 * hf), "x_%d_%d" % (dc, 2 * hf + 1)])
        p.act(p.cond[:], p.cond[:], AF.Silu, ["cond"], ["cond"])
        p.op("dve", lambda e: e.tensor_copy(p.condb[:], p.cond[:]), ["cond"], ["condb"])
        adabuf = [p.arena_b[:, i * 4096:(i + 1) * 4096] for i in range(2)]
        mps = p.bank[7]
        n = 0
        for L in self.layers:
            for blk in range(18):
                bi = n % 2
                n += 1
                p.dma("pool", adabuf[bi], p.ada_r[L, blk], (), ["adabuf%d" % bi],
                      max_dma_last_dim=4096)
                for cc in range(4):
                    col = L * 72 + blk * 4 + cc
                    for k in range(DC):
                        p.mm(mps[:, col:col + 1],
                             adabuf[bi][:, k * 512 + cc * 128:k * 512 + (cc + 1) * 128],
                             p.condb[:, k:k + 1], k == 0, k == DC - 1,
                             ["adabuf%d" % bi, "condb"], ["bank7"])
                c0 = L * 72 + blk * 4
                p.tt("dve", p.mods[:, c0:c0 + 4], mps[:, c0:c0 + 4], p.adab[:, c0:c0 + 4], ALU.add,
                     ["bank7", "adab"], ["mods"])

    def mod_cols(self, s, kind):
        L, sub = divmod(s, 3)
        c0 = L * 72 + (sub * 3 + kind) * DC
        return self.mods[:, c0:c0 + DC]

    def ln_cols(self, s):
        return self.lng[:, s * DC:(s + 1) * DC], self.lnb[:, s * DC:(s + 1) * DC]

    def prep_coefs(self, s, has_next, gate_mult):
        p = self
        cf = p.coef[:, s]
        sl = lambda i: cf[:, i * DC:(i + 1) * DC]
        g, b = p.ln_cols(s)
        w = ["coef%d" % s]
        r = ["mods", "lng", "lnb"]
        p.ts("dve", sl(4), p.mod_cols(s, 2), 1.0, gate_mult, ALU.add, ALU.mult, r, w)
        if has_next:
            p.ts("dve", sl(5), p.mod_cols(s + 1, 1), 1.0, None, ALU.add, None, r, w)
            p.tt("dve", sl(0), g, sl(5), ALU.mult, r + w, w)
            p.tt("dve", sl(1), b, sl(5), ALU.mult, r + w, w)
            p.tt("dve", sl(1), sl(1), p.mod_cols(s + 1, 0), ALU.add, r + w, w)
            p.ts("dve", sl(2), g, ALPHA, None, ALU.mult, None, r, w)
            p.ts("dve", sl(3), b, ALPHA, None, ALU.mult, None, r, w)
        else:
            p.op("dve", lambda e: e.tensor_copy(sl(2), g), r, w)
            p.op("dve", lambda e: e.tensor_copy(sl(3), b), r, w)

    def first_modulate(self, s):
        p = self
        cf = p.coef[:, s]
        tmp = cf[:, 6 * DC:7 * DC]
        p.ts("dve", tmp, p.mod_cols(s, 1), 1.0, None, ALU.add, None, ["mods"], ["coef%d" % s])
        for tt in range(4):
            for dc in range(DC):
                xb = "x_%d_%d" % (dc, tt)
                p.act(p.ht(dc, tt), p.xt(dc, tt), AF.Identity, [xb, "coef%d" % s, "mods"],
                      ["h_%d_%d" % (dc, tt)], bias=_col(p.mod_cols(s, 0), dc), scale=_col(tmp, dc))
                p.ts("dve", p.xt(dc, tt), p.xt(dc, tt), ALPHA, None, ALU.mult, None, [xb], [xb])

    def ln_setup(self):
        af = self.arena_f
        self.mean_sb = [af[:, i * TT:(i + 1) * TT] for i in range(2)]
        self.rstd_sb = [af[:, (2 + i) * TT:(3 + i) * TT] for i in range(2)]
        self.tsc = [af[:, (4 + i) * TT:(5 + i) * TT] for i in range(2)]
        self.silu_sb = [af[:, (6 + i) * TT:(7 + i) * TT] for i in range(2)]
        ab = self.arena_b
        o = 44 * 1024
        self.zb = [ab[:, o - (i + 1) * TT:o - i * TT] for i in range(6)]
        self._zrot = 0
        self._trot = 0
        self._nw2 = 0
        self._ny = 0

    def resid_evac(self, s, y_ps, ybuf, dc, tt, stat_banks, pend):
        p = self
        cf = p.coef[:, s]
        gate = cf[:, 4 * DC + dc:4 * DC + dc + 1]
        xb = "x_%d_%d" % (dc, tt)
        p.stt("dve", p.xt(dc, tt), y_ps, gate, p.xt(dc, tt), ALU.mult, ALU.add,
              [ybuf, xb, "coef%d" % s], [xb])
        r = p._zrot % 3
        p._zrot += 1
        zb, zq = p.zb[r], p.zb[3 + r]
        p.op("act", lambda e: e.copy(zb, p.xt(dc, tt)), [xb], ["zb%d" % r])
        p.act(zq, p.xt(dc, tt), AF.Square, [xb], ["zq%d" % r])
        mb, qb = stat_banks

        def stats():
            p.mm(p.bank[mb][:], p.onesb[:], zb, dc == 0, dc == DC - 1,
                 ["onesb", "zb%d" % r], ["bank%d" % mb])
            p.mm(p.bank[qb][:], p.onesb[:], zq, dc == 0, dc == DC - 1,
                 ["onesb", "zq%d" % r], ["bank%d" % qb])
        pend.append(stats)

    def ln_finalize(self, s, tt, stat_banks, has_next, store):
        for piece in self.ln_pieces(s, tt, stat_banks, has_next, store):
            piece()

    def ln_pieces(self, s, tt, stat_banks, has_next, store):
        p = self
        mb, qb = stat_banks
        par = tt % 2
        mean, rstd = p.mean_sb[par], p.rstd_sb[par]
        mB, rB = "mean%d" % par, "rstd%d" % par
        cf = p.coef[:, s]
        cB = "coef%d" % s
        pieces = []

        def head():
            p.op("act", lambda e: e.copy(mean, p.bank[mb][:]), ["bank%d" % mb], [mB])
            p.tt("dve", rstd, mean, mean, ALU.mult, [mB], [rB])
            p.tt("dve", rstd, p.bank[qb][:], rstd, ALU.subtract, ["bank%d" % qb, rB], [rB])
            p.ts("dve", rstd, rstd, LN_EPS, None, ALU.add, None, [rB], [rB])
            p.act(rstd, rstd, AF.Sqrt, [rB], [rB])
            p.op("dve", lambda e: e.reciprocal(rstd, rstd), [rB], [rB])
        pieces.append(head)
        for dc in range(DC):
            def body(dc=dc):
                xb = "x_%d_%d" % (dc, tt)
                r = p._trot % 2
                p._trot += 1
                t = p.tsc[r]
                tB = "tsc%d" % r
                p.tt("dve", t, p.xt(dc, tt), mean, ALU.subtract, [xb, mB], [tB])
                p.tt("dve", t, t, rstd, ALU.mult, [tB, rB], [tB])
                p.act(p.xt(dc, tt), t, AF.Identity, [tB, cB], [xb],
                      bias=cf[:, 3 * DC + dc:3 * DC + dc + 1], scale=cf[:, 2 * DC + dc:2 * DC + dc + 1])
                if has_next:
                    p.act(p.ht(dc, tt), t, AF.Identity, [tB, cB], ["h_%d_%d" % (dc, tt)],
                          bias=cf[:, 1 * DC + dc:1 * DC + dc + 1], scale=cf[:, 0 * DC + dc:0 * DC + dc + 1])
                if store:
                    d = p.dma("sp", p.yT[dc * 128:(dc + 1) * 128, tt * TT:(tt + 1) * TT], p.xt(dc, tt),
                              [xb], [])
                    p.out_dmas.append(d)
            pieces.append(body)
        return pieces

    def flush_deferred(self):
        while self.deferred:
            self.deferred.pop(0)()

    def ffn(self, s, has_next, store):
        p = self
        L, sub = divmod(s, 3)
        j = 0 if sub == 0 else 1
        p.ln_setup()
        p.prep_coefs(s, has_next, 0.5)
        ab = p.arena_b
        g = ab[:, 0:22 * 1024]
        w1b = [ab[:, 22528 + i * 2048:22528 + (i + 1) * 2048] for i in range(2)]
        w3b = [ab[:, 26624 + i * 2048:26624 + (i + 1) * 2048] for i in range(2)]
        w2b = [ab[:, 30720 + i * 5632:30720 + (i + 1) * 5632] for i in range(2)]
        cB = "coef%d" % s
        nset = 0
        nw = 0
        for hf in range(2):
            tts = (2 * hf, 2 * hf + 1)
            for fblk in range(11):
                bi = nw % 2
                nw += 1
                p.dma("pool", w1b[bi], p.w1r[L, j, fblk], (), ["w1b%d" % bi], max_dma_last_dim=4096)
                p.dma("pool", w3b[bi], p.w3r[L, j, fblk], (), ["w3b%d" % bi], max_dma_last_dim=4096)
                for fi in range(2):
                    f = fblk * 2 + fi
                    for tt in tts:
                        st = nset % 2
                        nset += 1
                        ub, vb = 2 * st, 2 * st + 1
                        for dc in range(DC):
                            p.mm(p.bank[ub][:], w1b[bi][:, dc * 256 + fi * 128:dc * 256 + (fi + 1) * 128],
                                 p.ht(dc, tt), dc == 0, dc == DC - 1,
                                 ["w1b%d" % bi, "h_%d_%d" % (dc, tt)], ["bank%d" % ub])
                        for dc in range(DC):
                            p.mm(p.bank[vb][:], w3b[bi][:, dc * 256 + fi * 128:dc * 256 + (fi + 1) * 128],
                                 p.ht(dc, tt), dc == 0, dc == DC - 1,
                                 ["w3b%d" % bi, "h_%d_%d" % (dc, tt)], ["bank%d" % vb])
                        sl = p.silu_sb[st]
                        p.act(sl, p.bank[ub][:], AF.Silu, ["bank%d" % ub], ["silu%d" % st])
                        tl = (tt - 2 * hf) * TT
                        gt = g[:, f * 1024 + tl:f * 1024 + tl + TT]
                        p.tt("dve", gt, sl, p.bank[vb][:], ALU.mult,
                             ["silu%d" % st, "bank%d" % vb], ["g_%d_%d" % (f, tt % 2)])
                        if p.deferred and nset % 2 == 0:
                            p.deferred.pop(0)()
            p.flush_deferred()
            p.proj_resid_ln(s, hf, FC, lambda f, tt: (g[:, f * 1024 + (tt % 2) * TT:f * 1024 + (tt % 2) * TT + TT],
                                                      "g_%d_%d" % (f, tt % 2)),
                            w2b, lambda dblk: p.w2r[L, j, dblk], "w2b", has_next, store, defer=True)

    def proj_resid_ln(self, s, hf, nk, rhs_fn, wb, wdram, wname, has_next, store, tts=None, defer=False):
        p = self
        if tts is None:
            tts = (2 * hf, 2 * hf + 1)
        pend = []
        for dblk in range(4):
            bi = p._nw2 % 2
            p._nw2 += 1
            p.dma("pool", wb[bi], wdram(dblk), (), ["%s%d" % (wname, bi)], max_dma_last_dim=4096)
            for di in range(2):
                dc = dblk * 2 + di
                for tt in tts:
                    yb = p._ny % 2
                    p._ny += 1
                    for k in range(nk):
                        rhs, rb = rhs_fn(k, tt)
                        p.mm(p.bank[yb][:], wb[bi][:, k * 256 + di * 128:k * 256 + (di + 1) * 128],
                             rhs, k == 0, k == nk - 1,
                             ["%s%d" % (wname, bi), rb], ["bank%d" % yb])
                    while pend:
                        pend.pop(0)()
                    sbk = (4 + 2 * (tt % 2), 5 + 2 * (tt % 2))
                    p.resid_evac(s, p.bank[yb][:], "bank%d" % yb, dc, tt, sbk, pend)
        while pend:
            pend.pop(0)()
        for tt in tts:
            sbk = (4 + 2 * (tt % 2), 5 + 2 * (tt % 2))
            if defer:
                p.deferred.extend(p.ln_pieces(s, tt, sbk, has_next, store))
            else:
                p.ln_finalize(s, tt, sbk, has_next, store)


    def attn(self, s, has_next, store):
        p = self
        lam_init = 0.8 - 0.6 * math.exp(-0.3 * (s // 3))
        p.ln_setup()
        p.prep_coefs(s, has_next, 1.0)
        ab, af = p.arena_b, p.arena_f
        oT = ab[:, 0:16384]
        qk = [ab[:, 16384 + i * 6144:16384 + (i + 1) * 6144] for i in range(2)]
        vh = [ab[:, 28672 + i * 2064:28672 + (i + 1) * 2064] for i in range(2)]
        wq = [ab[:, 32800 + i * 3072:32800 + (i + 1) * 3072] for i in range(2)]
        pT = [ab[:, 38944 + i * 512:38944 + (i + 1) * 512] for i in range(5)]
        onb = [ab[:, 41504 + i * 128:41504 + (i + 1) * 128] for i in range(8)]
        identb = ab[:, 42528:42656]
        fo = 4096
        NACC = 8
        accs = [af[:, i * 258:(i + 1) * 258] for i in range(NACC)]
        junk = af[:, 2304:2432]
        gvec = af[:, fo + 516:fo + 644]
        lamt = af[:, fo + 644:fo + 900]
        sm = af[:, fo + 900:fo + 1024]
        nlam = sm[:, 0:1]
        p.dma("pool", identb, p.ident, (), ["identb"])
        p.dma("sp", lamt, p.lam_b, (), ["lamt"])
        p.dma("sp", gvec, p.subln_b, (), ["gvec"])
        p.ts("dve", gvec, gvec, 1.0 - lam_init, None, ALU.mult, None, ["gvec"], ["gvec"])
        p.tt("dve", lamt[:, 0:64], lamt[:, 0:64], lamt[:, 64:128], ALU.mult, ["lamt"], ["lamt"])
        p.tt("dve", lamt[:, 128:192], lamt[:, 128:192], lamt[:, 192:256], ALU.mult, ["lamt"], ["lamt"])
        p.op("dve", lambda e: e.reduce_sum(sm[:, 1:2], lamt[:, 0:64], mybir.AxisListType.X), ["lamt"], ["sm"])
        p.op("dve", lambda e: e.reduce_sum(sm[:, 2:3], lamt[:, 128:192], mybir.AxisListType.X), ["lamt", "sm"], ["sm"])
        p.act(sm[:, 1:3], sm[:, 1:3], AF.Exp, ["sm"], ["sm"])
        p.tt("dve", nlam, sm[:, 2:3], sm[:, 1:2], ALU.subtract, ["sm"], ["sm"])
        p.ts("dve", nlam, nlam, -lam_init, None, ALU.add, None, ["sm"], ["sm"])
        for i in range(2):
            v3 = vh[i].rearrange("p (j e) -> p j e", e=129)
            p.op("pool", lambda e, v3=v3: e.memset(v3[:, :, 128:129], 1.0), (), ["vh%d" % i])
            p.op("pool", lambda e, i=i: e.memset(qk[i][64:128, 0:2048], 0.0), (), ["qk%dw0" % i])
            p.op("pool", lambda e, i=i: e.memset(qk[i][0:64, 2048:4096], 0.0), (), ["qk%dw0" % i])
        st_ = {"sc": 0, "pt": 0, "ep": 0, "pj": 0}

        def proj_groups(hh):
            par = hh % 2
            wB, qB, vB = "wq%d" % par, "qk%d" % par, "vh%d" % par
            groups = []
            def nextbank():
                b = (5, 7)[st_["pj"] % 2]
                st_["pj"] += 1
                return p.bank[b], "bank%d" % b

            def load():
                p.dma("pool", wq[par], p.wqkv_r[hh], (), [wB], max_dma_last_dim=4096)
            groups.append(load)
            for which in range(2):
                for tt in range(4):
                    def g(which=which, tt=tt):
                        pb, pbB = nextbank()
                        for dc in range(DC):
                            p.mm(pb[:], wq[par][:, (which * 8 + dc) * 128:(which * 8 + dc + 1) * 128],
                                 p.ht(dc, tt), dc == 0, dc == DC - 1, [wB, "h_%d_%d" % (dc, tt)], [pbB])
                        if which == 1:
                            dst = qk[par][:, 4096 + tt * TT:4096 + (tt + 1) * TT]
                            p.op("dve", lambda e, dst=dst: e.tensor_copy(dst, pb[:]), [pbB], [qB + "w1"])
                        else:
                            d0 = qk[par][0:64, tt * TT:(tt + 1) * TT]
                            d1 = qk[par][64:128, 2048 + tt * TT:2048 + (tt + 1) * TT]
                            p.op("dve", lambda e, d0=d0: e.tensor_copy(d0, pb[0:64, :]), [pbB], [qB + "w0"])
                            p.op("dve", lambda e, d1=d1: e.tensor_copy(d1, pb[64:128, :]), [pbB], [qB + "w0"])
                    groups.append(g)
            v3 = vh[par].rearrange("p (j e) -> p j e", e=129)
            for jg in range(4):
                def g(jg=jg):
                    pb, pbB = nextbank()
                    for jj in range(4):
                        jx = jg * 4 + jj
                        for dc in range(DC):
                            p.mm(pb[:, jj * 128:(jj + 1) * 128], p.h[:, dc, jx * 128:(jx + 1) * 128],
                                 wq[par][:, (16 + dc) * 128:(16 + dc + 1) * 128], dc == 0, dc == DC - 1,
                                 [wB, "h_%d_%d" % (dc, jx // 4)], [pbB])
                    src = pb[:].rearrange("p (j e) -> p j e", e=128)
                    p.op("dve", lambda e, src=src: e.tensor_copy(v3[:, jg * 4:(jg + 1) * 4, 0:128], src),
                         [pbB], [vB])
                groups.append(g)
            return groups

        def acc_ap(qi, m):
            return p.bank[qi][:, m * 129:m * 129 + 129], "bank%d" % qi

        later = []

        def tick():
            todo = list(later)
            del later[:]
            for item in todo:
                item[0] -= 1
                if item[0] <= 0:
                    item[1]()
                else:
                    later.append(item)

        trbk = p.bank[6][:].bitcast(BF16)

        def epilogue(hh, qb, qi):
            r = st_["ep"] % NACC
            st_["ep"] += 1
            r4 = r % 8
            Abank = p.bank[qi][:, 0:258]
            aB = "bank%d" % qi
            acc = accs[r]
            acB = "accs%d" % r
            c = sm[:, 8 + r * 4:12 + r * 4]
            cB = "smc%d" % r
            o = acc[:, 0:128]
            p.op("dve", lambda e: e.tensor_copy(acc, Abank), [aB], [acB])
            sums = acc.rearrange("p (m e) -> p m e", m=2)[:, :, 128:129]
            p.op("dve", lambda e: e.reciprocal(c[:, 0:2].unsqueeze(2), sums), [acB], [cB])
            p.tt("dve", c[:, 1:2], c[:, 1:2], nlam, ALU.mult, [cB, "sm"], [cB])

            def stage1():
                p.act(o, o, AF.Identity, [acB, cB], [acB], scale=c[:, 0:1])

            def stage2():
                p.stt("dve", o, acc[:, 129:257], c[:, 1:2], o, ALU.mult, ALU.add, [acB, cB], [acB])

            def stage3():
                p.op("act", lambda e: e.activation(junk, o, AF.Square, accum_out=c[:, 2:3]), [acB, cB], ["junk", cB])
                p.ts("dve", c[:, 2:3], c[:, 2:3], 1.0 / 128.0, LN_EPS, ALU.mult, ALU.add, [cB], [cB])

            def stage3b():
                p.act(c[:, 2:3], c[:, 2:3], AF.Ln, [cB], [cB])
                p.act(c[:, 2:3], c[:, 2:3], AF.Exp, [cB], [cB], scale=-0.5)

            def stage4():
                p.stt("dve", onb[r4], o, c[:, 2:3], gvec, ALU.mult, ALU.mult, [acB, cB, "gvec"], ["onb%d" % r4])

            tr = trbk[:, 0:128]

            def stage4b():
                p.op("pe", lambda e: e.transpose(tr, onb[r4], identb), ["onb%d" % r4, "identb"], ["bank6"])

            def stage5():
                dst = oT[:, hh * 2048 + qb * 128:hh * 2048 + (qb + 1) * 128]
                p.op("dve", lambda e: e.tensor_copy(dst, tr), ["bank6"], ["oT_%d_%d" % (hh, qb // 4)])
            later.append([1, stage1])
            later.append([2, stage2])
            later.append([3, stage3])
            later.append([4, stage3b])
            later.append([5, stage4])
            later.append([10, stage4b])
            later.append([11, stage5])

        SCB = ((2, 3), (4, 5))

        def core(hh, pending):
            par = hh % 2
            qB, vB = "qk%d" % par, "vh%d" % par
            q_ = qk[par]
            steps = [(QP, jx) for QP in range(8) for jx in range(2 * QP + 2)]

            def emit_scores(QP, jx):
                q0 = max(128 * jx, 256 * QP)
                q1 = 256 * (QP + 1)
                n = q1 - q0
                bb = 2 + st_["sc"] % 3
                st_["sc"] += 1
                pr = st_["pt"] % 5
                st_["pt"] += 1
                pt, ptB = pT[pr][:, 0:512], "pT%d" % pr
                for m in range(2):
                    p.mm(p.bank[bb][:, m * 256:m * 256 + n], q_[:, 4096 + 128 * jx:4096 + 128 * jx + 128],
                         q_[:, m * 2048 + q0:m * 2048 + q1], True, True, [qB + "w0", qB + "w1"], ["bank%d" % bb])
                pt3 = pt.rearrange("p (m q) -> p m q", m=2)
                sc3 = p.bank[bb][:].rearrange("p (m q) -> p m q", m=2)
                p.act(pt3[:, :, 0:n], sc3[:, :, 0:n], AF.Exp, ["bank%d" % bb], [ptB], scale=0.125)
                if jx >= 2 * QP:
                    p.op("pool", lambda e, pt3=pt3: e.memset(pt3[64:128, :, 0:64], 0.0), [ptB], [ptB])
                return (QP, jx, q0, pt, ptB)

            def emit_pv(ctx, bank_first):
                QP, jx, q0, pt, ptB = ctx
                for qi in range(2):
                    qb = 2 * QP + qi
                    if jx > qb:
                        continue
                    c0 = 128 * qb - q0
                    for m in range(2):
                        A, aB = acc_ap(qi, m)
                        first = bank_first[qi]
                        bank_first[qi] = False
                        p.mm(A, pt[:, m * 256 + c0:m * 256 + c0 + 128], vh[par][:, jx * 129:(jx + 1) * 129],
                             first, jx == qb, [ptB, vB], [aB], skip_group_check=True)
                    if jx == qb:
                        epilogue(hh, qb, qi)

            LOOK = 2
            ctxs = [emit_scores(*steps[k]) for k in range(LOOK)]
            bank_first = [True, True]
            for i, (QP, jx) in enumerate(steps):
                if i + LOOK < len(steps):
                    ctxs.append(emit_scores(*steps[i + LOOK]))
                ctx = ctxs.pop(0)
                if jx == 0:
                    bank_first = [True, True]
                emit_pv(ctx, bank_first)
                tick()
                if pending and (i + 1) % 5 == 0:
                    pending.pop(0)()
            while pending:
                pending.pop(0)()

        for g in proj_groups(0):
            g()
        for hh in range(8):
            pending = proj_groups(hh + 1) if hh + 1 < 8 else []
            core(hh, pending)
        for _ in range(13):
            tick()
        woutb = [wq[0][:, 0:2048], wq[1][:, 0:2048]]
        for hf in range(2):
            p.proj_resid_ln(s, hf, 8, lambda k, tt: (oT[:, k * 2048 + tt * TT:k * 2048 + (tt + 1) * TT],
                                                     "oT_%d_%d" % (k, tt)),
                            woutb, lambda dblk: p.wout_r[dblk], "wq", has_next, store)

    def s5(self, s, has_next, store):
        p = self
        LB = 4
        NBK = TT // LB
        TWO_PI = 2.0 * math.pi
        GK = 2.0 * math.sqrt(2.0 / math.pi)
        p.ln_setup()
        p.prep_coefs(s, has_next, 1.0)
        ab, af = p.arena_b, p.arena_f
        BBTr, BBTi, CCr, CCni, T0T = [ab[:, i * 4096:(i + 1) * 4096] for i in range(5)]
        selb = ab[:, 20480:22528]
        FB = 22528
        Ff = ab[:, FB:FB + 16384].bitcast(F32)
        F3 = Ff.rearrange("p (b c) -> p b c", c=64)
        U = ab[:, 38912:43008]
        U3 = U.rearrange("p (g b) -> p g b", b=NBK)
        sel2 = p.sel2_sb[:]
        PS = af[:, 4096:5120]
        Sb = af[:, 0:4096].bitcast(BF16)
        Sb3 = Sb.rearrange("p (b c) -> p b c", c=64)
        zT = ab[:, FB:FB + 4096]
        gz = ab[:, FB + 4096:FB + 8192]
        wbuf = [ab[:, FB + 8192 + i * 2048:FB + 8192 + (i + 1) * 2048] for i in range(2)]
        scr = [ab[:, FB + 12288 + i * 1024:FB + 12288 + (i + 1) * 1024].bitcast(F32) for i in range(4)]

        sv = lambda i: Ff[:, i * 32:(i + 1) * 32]
        A_RE, A_IM, LDT, DCOL, DT, AR, AI, T1, T2, T3, DEN, CFR, CFI = [sv(i) for i in range(13)]
        lamr = {k: sv(16 + 2 * (k + 4)) for k in (-4, -3, -2, -1, 0, 1, 2, 3, 4, 8)}
        lami = {k: sv(17 + 2 * (k + 4)) for k in (-4, -3, -2, -1, 0, 1, 2, 3, 4, 8)}
        KI = Ff[:, 1024:1056].bitcast(mybir.dt.int32)
        LAMA = PS[:, 0:64]
        LAMB = PS[:, 64:128]
        Wst = [PS[:, 128:224], PS[:, 224:320]]
        Tst = PS[:, 320:416]
        M1 = PS[:, 416:480]
        M2 = PS[:, 480:544]
        b_r, b_i, c_r, c_i = [Ff[:, 2048 + i * 512:2048 + (i + 1) * 512] for i in range(4)]
        Bb_r, Bb_i = Ff[:, 4096:4608], Ff[:, 4608:5120]
        maskLT, identf = Ff[:, 5120:5248], Ff[:, 5248:5376]
        tmpA, tmpB = Ff[:, 5376:5888], Ff[:, 5888:6400]
        tmpT = Ff[:, 6400:6528]
        tmpT4 = Ff[:, 6400:6912]
        dmat4 = Ff[:, 6912:7424]
        Z = [af[:, i * 1024:(i + 1) * 1024] for i in range(4)]
        SV = "s5sv"
        for dst, src in ((A_RE, p.s5_are), (A_IM, p.s5_aim), (LDT, p.s5_ldt), (DCOL, p.s5_dcol),
                         (b_r, p.s5_br), (b_i, p.s5_bi), (c_r, p.s5_cr), (c_i, p.s5_ci),
                         (maskLT, p.s5_mask), (identf, p.ident)):
            p.dma("sp", dst, src, (), [SV])
        p.dma("pool", selb, p.s5_sel, (), ["selb"], max_dma_last_dim=4096)
        p.dma("pool", sel2, p.s5_sel2, (), ["sel2"])
        V = [SV]
        p.act(DT, LDT, AF.Exp, V, V)
        p.tt("dve", AR, A_RE, DT, ALU.mult, V, V)
        p.tt("dve", AI, A_IM, DT, ALU.mult, V, V)
        p.ts("dve", T1, AI, 1.0 / TWO_PI, None, ALU.mult, None, V, V)
        p.op("dve", lambda e: e.tensor_copy(KI, T1), V, V)
        p.op("dve", lambda e: e.tensor_copy(T1, KI), V, V)
        p.stt("dve", T2, T1, -TWO_PI, AI, ALU.mult, ALU.add, V, V)
        p.ts("dve", T2, T2, -3.1415925, 3.1415925, ALU.max, ALU.min, V, V)
        p.act(T3, T2, AF.Abs, V, V)
        p.ts("dve", T3, T3, -1.0, math.pi / 2, ALU.mult, ALU.add, V, V)
        p.act(T2, T2, AF.Sin, V, V)
        p.act(T3, T3, AF.Sin, V, V)
        p.act(T1, AR, AF.Exp, V, V)
        p.tt("dve", lamr[1], T1, T3, ALU.mult, V, V)
        p.tt("dve", lami[1], T1, T2, ALU.mult, V, V)
        p.act(T1, AR, AF.Exp, V, V, scale=-1.0)
        p.tt("dve", lamr[-1], T1, T3, ALU.mult, V, V)
        p.tt("dve", lami[-1], T1, T2, ALU.mult, V, V)
        p.ts("dve", lami[-1], lami[-1], -1.0, None, ALU.mult, None, V, V)
        p.op("dve", lambda e: e.memset(lamr[0], 1.0), V, V)
        p.op("dve", lambda e: e.memset(lami[0], 0.0), V, V)

        def cmul(outr, outi, ar, ai_, br, bi, shape=None):
            ta, tb = (T1, DEN) if shape is None else (tmpA, tmpB)
            if shape is not None:
                ta = ta.rearrange("p (g e) -> p g e", e=16)
                tb = tb.rearrange("p (g e) -> p g e", e=16)
            p.tt("dve", ta, ar, br, ALU.mult, V, V)
            p.tt("dve", tb, ai_, bi, ALU.mult, V, V)
            p.tt("dve", ta, ta, tb, ALU.subtract, V, V)
            p.tt("dve", tb, ar, bi, ALU.mult, V, V)
            p.op("dve", lambda e: e.tensor_copy(outr, ta), V, V)
            p.tt("dve", ta, ai_, br, ALU.mult, V, V)
            p.tt("dve", outi, ta, tb, ALU.add, V, V)

        for k in (2, 3, 4):
            cmul(lamr[k], lami[k], lamr[k - 1], lami[k - 1], lamr[1], lami[1])
        for k in (-2, -3, -4):
            cmul(lamr[k], lami[k], lamr[k + 1], lami[k + 1], lamr[-1], lami[-1])
        cmul(lamr[8], lami[8], lamr[4], lami[4], lamr[4], lami[4])
        p.tt("dve", DEN, A_RE, A_RE, ALU.mult, V, V)
        p.tt("dve", T1, A_IM, A_IM, ALU.mult, V, V)
        p.tt("dve", DEN, DEN, T1, ALU.add, V, V)
        p.op("dve", lambda e: e.reciprocal(DEN, DEN), V, V)
        p.ts("dve", T3, lamr[1], -1.0, None, ALU.add, None, V, V)
        p.tt("dve", T1, T3, A_RE, ALU.mult, V, V)
        p.tt("dve", T2, lami[1], A_IM, ALU.mult, V, V)
        p.tt("dve", T1, T1, T2, ALU.add, V, V)
        p.tt("dve", CFR, T1, DEN, ALU.mult, V, V)
        p.tt("dve", T1, lami[1], A_RE, ALU.mult, V, V)
        p.tt("dve", T2, T3, A_IM, ALU.mult, V, V)
        p.tt("dve", T1, T1, T2, ALU.subtract, V, V)
        p.tt("dve", CFI, T1, DEN, ALU.mult, V, V)
        p.op("dve", lambda e: e.tensor_copy(LAMA[:, 0:32], lamr[LB]), V, V)
        p.op("dve", lambda e: e.tensor_copy(LAMA[:, 32:64], lamr[LB]), V, V)
        p.ts("dve", LAMB[:, 0:32], lami[LB], -1.0, None, ALU.mult, None, V, V)
        p.op("dve", lambda e: e.tensor_copy(LAMB[:, 32:64], lami[LB]), V, V)
        p.op("dve", lambda e: e.memset(Wst[0], 0.0), V, V + ["W0", "W0h"])
        LAMA2, LAMB2, NA4, NBS4, LBSW = [PS[:, 544 + i * 64:608 + i * 64] for i in range(5)]
        for half in range(2):
            hs = slice(half * 32, (half + 1) * 32)
            p.op("dve", lambda e, hs=hs: e.tensor_copy(LAMA2[:, hs], lamr[2 * LB]), V, V)
            p.op("dve", lambda e, hs=hs: e.tensor_copy(NA4[:, hs], lamr[-LB]), V, V)
        p.ts("dve", LAMB2[:, 0:32], lami[2 * LB], -1.0, None, ALU.mult, None, V, V)
        p.op("dve", lambda e: e.tensor_copy(LAMB2[:, 32:64], lami[2 * LB]), V, V)
        p.op("dve", lambda e: e.tensor_copy(NBS4[:, 0:32], lami[-LB]), V, V)
        p.ts("dve", NBS4[:, 32:64], lami[-LB], -1.0, None, ALU.mult, None, V, V)
        p.op("dve", lambda e: e.tensor_copy(LBSW[:, 0:32], lami[LB]), V, V)
        p.ts("dve", LBSW[:, 32:64], lami[LB], -1.0, None, ALU.mult, None, V, V)
        bc = lambda v: v.unsqueeze(2).to_broadcast([128, 32, 16])
        v3 = lambda t: t.rearrange("p (g e) -> p g e", e=16)
        cmul(v3(Bb_r), v3(Bb_i), bc(CFR), bc(CFI), v3(b_r), v3(b_i), shape=3)
        for ch in range(4):
            g0 = ch * 8
            Zr, Zi, ZCr, ZCni = [z.rearrange("p (g l h e) -> p g l h e", g=8, l=LB, h=2) for z in Z]
            ZB = ["s5ZB", "s5ZC"]
            zero8 = Ff[:, 7680:7808].rearrange("p (g e) -> p g e", e=16)
            if ch == 0:
                p.op("pool", lambda e: e.memset(zero8, 0.0), (), ["s5zero"])
            for (eng, outr, outi, sgn, xr, xi, neg, tA, tBf, tN, zN) in (
                    ("dve", Zr, Zi, -1, Bb_r, Bb_i, False, tmpA, tmpB, "s5tmpD", "s5ZB"),
                    ("pool", ZCr, ZCni, 1, c_r, c_i, True, Ff[:, 7424:7552], Ff[:, 7552:7680], "s5tmpP", "s5ZC")):
                ta = v3(tA)[:, 0:8, :]
                tb = v3(tBf)[:, 0:8, :]
                sl = lambda t: v3(t)[:, g0:g0 + 8, :]
                lb = lambda v: v[:, g0:g0 + 8].unsqueeze(2).to_broadcast([128, 8, 16])
                for l in range(LB):
                    kk = sgn * l
                    lr_, li_ = lb(lamr[kk]), lb(lami[kk])
                    p.tt(eng, ta, lr_, sl(xr), ALU.mult, V + [tN], [tN])
                    p.tt(eng, tb, li_, sl(xi), ALU.mult, V + [tN], [tN])
                    for h in range(2):
                        p.tt(eng, outr[:, :, l, h, :], ta, tb, ALU.subtract, [tN, zN], [zN])
                    p.tt(eng, ta, lr_, sl(xi), ALU.mult, V + [tN], [tN])
                    p.tt(eng, tb, li_, sl(xr), ALU.mult, V + [tN], [tN])
                    if neg:
                        p.tt(eng, ta, ta, tb, ALU.add, [tN], [tN])
                        for h in range(2):
                            p.tt(eng, outi[:, :, l, h, :], zero8, ta, ALU.subtract, [tN, zN, "s5zero"], [zN])
                    else:
                        for h in range(2):
                            p.tt(eng, outi[:, :, l, h, :], ta, tb, ALU.add, [tN, zN], [zN])
                for z4 in (outr, outi):
                    p.op(eng, lambda e, z4=z4: e.memset(z4[0:64, :, :, 1, :], 0.0), [zN], [zN])
                    p.op(eng, lambda e, z4=z4: e.memset(z4[64:128, :, :, 0, :], 0.0), [zN], [zN])
            ccol = slice(g0 * 128, (g0 + 8) * 128)
            p.op("act", lambda e, ccol=ccol: e.copy(CCr[:, ccol], Z[2]), ZB, ["s5tab"])
            p.op("act", lambda e, ccol=ccol: e.copy(CCni[:, ccol], Z[3]), ZB, ["s5tab"])
            for q4 in range(2):
                cols4 = slice((g0 + 4 * q4) * 128, (g0 + 4 * q4 + 4) * 128)
                for i, dstT in ((0, BBTr), (1, BBTi)):
                    b = i
                    for gj in range(4):
                        gi = 4 * q4 + gj
                        p.mm(p.bank[b][:, gj * 128:(gj + 1) * 128], Z[i][:, gi * 128:(gi + 1) * 128], identf,
                             True, True, ZB + [SV], ["bank%d" % b])
                    p.op("act", lambda e, dstT=dstT, cols4=cols4, b=b: e.copy(dstT[:, cols4], p.bank[b][:]),
                         ["bank%d" % b], ["s5tab"])
                b = 2 + q4
                for gj in range(4):
                    gi = 4 * q4 + gj
                    o_ = p.bank[b][:, gj * 128:(gj + 1) * 128]
                    p.mm(o_, Z[0][:, gi * 128:(gi + 1) * 128], Z[2][:, gi * 128:(gi + 1) * 128], True, False,
                         ZB, ["bank%d" % b])
                    p.mm(o_, Z[1][:, gi * 128:(gi + 1) * 128], Z[3][:, gi * 128:(gi + 1) * 128], False, True,
                         ZB, ["bank%d" % b])
                m4 = maskLT.unsqueeze(1).to_broadcast([128, 4, 128])
                i4 = identf.unsqueeze(1).to_broadcast([128, 4, 128])
                d4 = DCOL[:, g0 + 4 * q4:g0 + 4 * q4 + 4].unsqueeze(2).to_broadcast([128, 4, 128])
                t4 = tmpT4.rearrange("p (g c) -> p g c", g=4)
                dm4 = dmat4.rearrange("p (g c) -> p g c", g=4)
                p.tt("dve", t4, p.bank[b][:].rearrange("p (g c) -> p g c", g=4), m4, ALU.mult,
                     ["bank%d" % b, SV], ["s5tmpT"])
                p.tt("dve", dm4, i4, d4, ALU.mult, [SV], ["s5dm"])
                p.tt("dve", T0T[:, cols4], tmpT4, dmat4, ALU.add, ["s5tmpT", "s5dm"], ["s5tab"])
        p.dump("small", Ff[:, 0:1664], [SV], 1664)
        p.dump("Bb", Ff[:, 4096:5120], [SV], 1024)
        p.dump("BBTr", BBTr, ["s5tab"], 4096)
        p.dump("BBTi", BBTi, ["s5tab"], 4096)
        p.dump("CCr", CCr, ["s5tab"], 4096)
        p.dump("CCni", CCni, ["s5tab"], 4096)
        p.dump("T0T", T0T, ["s5tab"], 4096)
        p.sc.barrier()

        for tt in range(4):
            uT = zT
            for blk in range(4):
                bi = blk % 2
                p.dma("pool", wbuf[bi], p.s5_win[blk], (), ["wbuf%d" % bi], max_dma_last_dim=4096)
                for oi in range(2):
                    oc = blk * 2 + oi
                    b = oc % 4
                    for dc in range(DC):
                        p.mm(p.bank[b][:], wbuf[bi][:, dc * 256 + oi * 128:dc * 256 + (oi + 1) * 128],
                             p.ht(dc, tt), dc == 0, dc == DC - 1,
                             ["wbuf%d" % bi, "h_%d_%d" % (dc, tt)], ["bank%d" % b])
                    dst = uT[:, oc * 512:(oc + 1) * 512]
                    if oc % 2:
                        p.op("act", lambda e, dst=dst, b=b: e.copy(dst, p.bank[b][:]), ["bank%d" % b],
                             ["uT%d" % oc, "zT%d" % oc])
                    else:
                        p.op("dve", lambda e, dst=dst, b=b: e.tensor_copy(dst, p.bank[b][:]), ["bank%d" % b],
                             ["uT%d" % oc, "zT%d" % oc])
                    if p.deferred:
                        p.deferred.pop(0)()
            last_scr = None
            for q4 in range(8):
                b = q4 % 4
                for j in range(4):
                    pi_ = q4 * 4 + j
                    dc = q4
                    for l in range(LB):
                        rhs = uT[:, dc * 512 + l:(dc + 1) * 512:LB]
                        last_scr = p.mm(p.bank[b][32 * l:32 * l + 32, j * NBK:(j + 1) * NBK],
                                        sel2[:, j * 32:(j + 1) * 32], rhs, True, True,
                                        ["sel2", "uT%d" % dc], ["bank%d" % b], tile_position=(0, 32 * l))
                names = ["U%d" % (q4 * 4 + j) for j in range(4)]
                dstU = U[:, q4 * 512:(q4 + 1) * 512]
                if q4 % 2:
                    p.op("act", lambda e, dstU=dstU, b=b: e.copy(dstU, p.bank[b][:]), ["bank%d" % b], names)
                else:
                    p.op("dve", lambda e, dstU=dstU, b=b: e.tensor_copy(dstU, p.bank[b][:]), ["bank%d" % b], names)
                if p.deferred:
                    p.deferred.pop(0)()
            first_f = {"dve": True, "act": True}
            nfb = 0
            for q4 in range(8):
                for i, tabT in ((0, BBTr), (1, BBTi)):
                    b = nfb % 4
                    nfb += 1
                    for j in range(4):
                        pi_ = q4 * 4 + j
                        col = slice(pi_ * 128, (pi_ + 1) * 128)
                        p.mm(p.bank[b][:, j * NBK:(j + 1) * NBK], tabT[:, col], U3[:, pi_, :], True, True,
                             ["s5tab", "U%d" % pi_], ["bank%d" % b])
                    c0 = i * 32 + q4 * 4
                    dst = F3[:, :, c0:c0 + 4].rearrange("p b c -> p c b")
                    src = p.bank[b][:].rearrange("p (c b) -> p c b", c=4)
                    if i == 0:
                        fi = p.op("dve", lambda e, dst=dst, src=src: e.tensor_copy(dst, src), ["bank%d" % b], ["F"])
                    else:
                        fi = p.op("act", lambda e, dst=dst, src=src: e.copy(dst, src), ["bank%d" % b], ["F"])
                    if first_f[fi.eng]:
                        first_f[fi.eng] = False
                        fi.deps.append(last_scr)
            p.flush_deferred()
            NH = NBK // 2
            HSPL = 44
            RNG = ((0, HSPL), (HSPL, NH))
            tmpf = af[:, 0:4096].rearrange("p (b c) -> p b c", c=64)
            for hh_, eng in ((0, "dve"), (1, "pool")):
                lo_, hi_ = RNG[hh_]
                bcb = lambda t, n_=hi_ - lo_: t.unsqueeze(1).to_broadcast([128, n_, 64])
                Fe = F3[:, 2 * lo_:2 * hi_:2, :]
                Fo = F3[:, 2 * lo_ + 1:2 * hi_:2, :]
                tf = tmpf[:, lo_:hi_, :]
                fB, tB = "Fh%d" % hh_, "tmph%d" % hh_
                RB = ["F", SV]
                LNB = ["mean0", "mean1", "rstd0", "rstd1", "tsc0", "tsc1"]
                p.tt(eng, tf, bcb(NA4), Fo, ALU.mult, RB, [tB] + LNB)
                p.tt(eng, tf, tf, Fe, ALU.add, RB + [tB], [tB])
                p.tt(eng, Fo, Fo, bcb(NBS4), ALU.mult, RB, [fB])
                p.tt(eng, tf[:, :, 0:32], tf[:, :, 0:32], Fo[:, :, 32:64], ALU.add, [fB, tB], [tB])
                p.tt(eng, tf[:, :, 32:64], tf[:, :, 32:64], Fo[:, :, 0:32], ALU.add, [fB, tB], [tB])
                p.op(eng, lambda e, Fo=Fo, tf=tf: e.tensor_copy(Fo, tf), [tB, fB], [fB])
            for k2 in range(NH):
                g = tt * NH + k2
                fB, tB = "Fh%d" % (k2 >= HSPL), "tmph%d" % (k2 >= HSPL)
                Wc, Wn = Wst[g % 2], Wst[(g + 1) % 2]
                wcB, wnB = "W%d" % (g % 2), "W%d" % ((g + 1) % 2)
                p.op("act", lambda e, k2=k2, Wc=Wc: e.copy(Sb3[:, 2 * k2, :], Wc[:, 0:64]), [wcB, SV], ["Sb", tB])
                p.tt("dve", Tst[:, 0:64], Wc[:, 0:64], F3[:, 2 * k2 + 1, :], ALU.add, [wcB, fB, SV], ["Tlo"])
                p.tt("pool", Tst[:, 64:96], Wc[:, 64:96], F3[:, 2 * k2 + 1, 0:32], ALU.add, [wcB + "h", fB, SV], ["Thi"])
                p.tt("dve", M1, LAMA2, Tst[:, 0:64], ALU.mult, ["Tlo", SV], ["M1"])
                p.tt("pool", M2, LAMB2, Tst[:, 32:96], ALU.mult, ["Tlo", "Thi", SV], ["M2"])
                p.tt("dve", Wn[:, 0:64], M1, M2, ALU.add, ["M1", "M2"], [wnB])
                p.tt("pool", Wn[:, 64:96], M1[:, 0:32], M2[:, 0:32], ALU.add, ["M1", "M2"], [wnB + "h"])
            for hh_, eng in ((0, "dve"), (1, "pool")):
                lo_, hi_ = RNG[hh_]
                bcb = lambda t, n_=hi_ - lo_: t.unsqueeze(1).to_broadcast([128, n_, 64])
                Fe = F3[:, 2 * lo_:2 * hi_:2, :]
                Fo = F3[:, 2 * lo_ + 1:2 * hi_:2, :]
                Se = Sb3[:, 2 * lo_:2 * hi_:2, :]
                So = Sb3[:, 2 * lo_ + 1:2 * hi_:2, :]
                fB, soB = "Fh%d" % hh_, "So%d" % hh_
                p.tt(eng, Fe, Fe, Se, ALU.add, [fB, "F", "Sb", SV], [fB])
                p.tt(eng, Fo, bcb(LAMA), Fe, ALU.mult, [fB, SV], [fB])
                p.tt(eng, Fe, Fe, bcb(LBSW), ALU.mult, [fB, SV], [fB])
                p.tt(eng, So[:, :, 0:32], Fo[:, :, 0:32], Fe[:, :, 32:64], ALU.add, [fB], [soB])
                p.tt(eng, So[:, :, 32:64], Fo[:, :, 32:64], Fe[:, :, 0:32], ALU.add, [fB], [soB])
            if tt == 0:
                p.dump("U", U, ["U%d" % i for i in range(32)], 4096)
                p.dump("F", Ff, ["F"], 8192)
                p.dump("Sb", Sb, ["Sb", "So0", "So1"], 8192)
            p.sc.barrier()
            for q4 in range(8):
                b = q4 % 4
                for j in range(4):
                    pi_ = q4 * 4 + j
                    col = slice(pi_ * 128, (pi_ + 1) * 128)
                    o_ = p.bank[b][:, j * NBK:(j + 1) * NBK]
                    p.mm(o_, T0T[:, col], U3[:, pi_, :], True, False, ["s5tab", "Uq%d" % q4], ["bank%d" % b])
                    p.mm(o_, CCr[:, col], Sb3[:, :, pi_], False, False, ["s5tab", "Sb", "So0", "So1"], ["bank%d" % b])
                    p.mm(o_, CCni[:, col], Sb3[:, :, 32 + pi_], False, True, ["s5tab", "Sb", "So0", "So1"], ["bank%d" % b])
                r = q4 % 2
                sq, inner = scr[2 * r], scr[2 * r + 1]
                Y = p.bank[b][:]
                yB = "bank%d" % b
                p.act(sq, Y, AF.Square, [yB], ["sq%d" % r])
                p.ts("dve", sq, sq, 0.044715, 1.0, ALU.mult, ALU.add, ["sq%d" % r], ["sq%d" % r])
                p.tt("dve", inner, sq, Y, ALU.mult, ["sq%d" % r, yB], ["in%d" % r])
                p.act(inner, inner, AF.Sigmoid, ["in%d" % r], ["in%d" % r], scale=GK)
                p.tt("dve", U[:, q4 * 512:(q4 + 1) * 512], inner, Y, ALU.mult, ["in%d" % r, yB], ["Uq%d" % q4])
            for dc in range(DC):
                b = dc % 4
                for l in range(LB):
                    for j in range(4):
                        pi_ = dc * 4 + j
                        p.mm(p.bank[b][:, l * NBK:(l + 1) * NBK],
                             selb[:, (j * LB + l) * 128:(j * LB + l + 1) * 128], U3[:, pi_, :],
                             j == 0, j == 3, ["selb", "Uq%d" % dc], ["bank%d" % b])
                dst = zT[:, dc * 512:(dc + 1) * 512].rearrange("p (b l) -> p l b", l=LB)
                src = p.bank[b][:].rearrange("p (l b) -> p l b", l=LB)
                if dc % 2:
                    p.op("act", lambda e, dst=dst, src=src: e.copy(dst, src), ["bank%d" % b], ["zT%d" % dc])
                else:
                    p.op("dve", lambda e, dst=dst, src=src: e.tensor_copy(dst, src), ["bank%d" % b], ["zT%d" % dc])
            if tt == 0:
                p.dump("zs", U, ["Uq%d" % i for i in range(8)], 4096)
                p.dump("zT", zT, ["zT%d" % i for i in range(8)], 4096)
            p.sc.barrier()
            nsc = 0
            for blk in range(4):
                bi = blk % 2
                p.dma("pool", wbuf[bi], p.s5_wg[blk], (), ["wbuf%d" % bi], max_dma_last_dim=4096)
                for oi in range(2):
                    oc = blk * 2 + oi
                    b = oc % 4
                    for dc in range(DC):
                        p.mm(p.bank[b][:], wbuf[bi][:, dc * 256 + oi * 128:dc * 256 + (oi + 1) * 128],
                             zT[:, dc * 512:(dc + 1) * 512], dc == 0, dc == DC - 1,
                             ["wbuf%d" % bi, "zT%d" % dc], ["bank%d" % b])
                    r = nsc % 4
                    nsc += 1
                    p.act(scr[r], p.bank[b][:], AF.Sigmoid, ["bank%d" % b], ["scr%d" % r])
                    p.tt("dve", gz[:, oc * 512:(oc + 1) * 512], scr[r], zT[:, oc * 512:(oc + 1) * 512], ALU.mult,
                         ["scr%d" % r, "zT%d" % oc], ["gz%d" % oc])
            p.proj_resid_ln(s, 0, 8, lambda k, t_: (gz[:, k * 512:(k + 1) * 512], "gz%d" % k),
                            wbuf, lambda dblk: p.s5_wo[dblk], "wbuf", has_next, store, tts=(tt,),
                            defer=(tt < 3))

    def build(self):
        p = self
        p.out_dmas = []
        p.deferred = []
        p.prologue()
        subs = p.subs
        p.first_modulate(subs[0])
        for i, s in enumerate(subs):
            last = i == len(subs) - 1
            if s % 3 != 1:
                p.ffn(s, not last, last)
            elif s == 1:
                p.attn(s, not last, last)
            else:
                p.s5(s, not last, last)
            if not last:
                nxt = subs[i + 1]
                if s % 3 != 1 and nxt % 3 != 1:
                    continue
                p.flush_deferred()
                p.sc.barrier()
        p.flush_deferred()
        p.sc.emit(p.nc, p.out_dmas)
        return p.nc


def host_layout(inp):
    f = lambda a: np.ascontiguousarray(a, dtype=np.float32)
    out = {}
    ada_w = np.asarray(inp["ada_w"])
    out["ada_r"] = f(ada_w.reshape(2, DC, 128, 18, 512).transpose(0, 3, 2, 1, 4).reshape(2, 18, 128, DC * 512))
    ada_b = np.asarray(inp["ada_b"])
    out["ada_b"] = f(ada_b.reshape(2, 72, 128).transpose(2, 0, 1).reshape(128, 144))
    out["ln_g"] = f(np.asarray(inp["ln_g"]).reshape(6, DC, 128).transpose(2, 0, 1).reshape(128, 6 * DC))
    out["ln_b"] = f(np.asarray(inp["ln_b"]).reshape(6, DC, 128).transpose(2, 0, 1).reshape(128, 6 * DC))
    w1 = np.asarray(inp["ffn_w1"])
    w3 = np.asarray(inp["ffn_w3"])
    w2 = np.asarray(inp["ffn_w2"])
    out["w1r"] = f(w1.reshape(2, 2, DC, 128, 11, 256).transpose(0, 1, 4, 3, 2, 5).reshape(2, 2, 11, 128, DC * 256))
    out["w3r"] = f(w3.reshape(2, 2, DC, 128, 11, 256).transpose(0, 1, 4, 3, 2, 5).reshape(2, 2, 11, 128, DC * 256))
    out["w2r"] = f(w2.reshape(2, 2, FC, 128, 4, 256).transpose(0, 1, 4, 3, 2, 5).reshape(2, 2, 4, 128, FC * 256))
    win = np.asarray(inp["attn_w_in"])[0]
    out["wqkv_r"] = f(win.reshape(DC, 128, 3, 8, 128).transpose(3, 1, 2, 0, 4).reshape(8, 128, 3072))
    wo = np.asarray(inp["attn_w_out"])[0]
    out["wout_r"] = f(wo.reshape(8, 128, 4, 256).transpose(2, 1, 0, 3).reshape(4, 128, 8 * 256))
    out["lam_b"] = f(np.broadcast_to(np.asarray(inp["attn_lam"])[0].reshape(1, 256), (128, 256)))
    out["subln_b"] = f(np.broadcast_to(np.asarray(inp["attn_subln_g"])[0].reshape(1, 128), (128, 128)))
    out["ident"] = np.eye(128, dtype=np.float32)
    gn = lambda a: f(np.asarray(a)[0].reshape(32, 2, 64).transpose(1, 2, 0).reshape(128, 32))
    out["s5_are"] = gn(inp["ssm_a_re"])
    out["s5_aim"] = gn(inp["ssm_a_im"])
    ldt = np.asarray(inp["ssm_log_dt"])[0].reshape(32, 2)
    out["s5_ldt"] = f(np.broadcast_to(ldt.transpose(1, 0)[:, None, :], (2, 64, 32)).reshape(128, 32))
    dd = np.asarray(inp["ssm_d"])[0].reshape(32, 2, 16).transpose(1, 2, 0).reshape(32, 32)
    out["s5_dcol"] = f(np.tile(dd, (4, 1)))
    bl = lambda a: f(np.asarray(a)[0].reshape(32, 2, 64, 16).transpose(1, 2, 0, 3).reshape(128, 512))
    cl = lambda a: f(np.asarray(a)[0].reshape(32, 2, 16, 64).transpose(1, 3, 0, 2).reshape(128, 512))
    out["s5_br"] = bl(inp["ssm_b_re"])
    out["s5_bi"] = bl(inp["ssm_b_im"])
    out["s5_cr"] = cl(inp["ssm_c_re"])
    out["s5_ci"] = cl(inp["ssm_c_im"])
    lrow = np.arange(128) // 32
    out["s5_mask"] = f((lrow[None, :] >= lrow[:, None]))
    sel = np.zeros((4, 4, 128, 128), np.float32)
    for j in range(4):
        for l in range(4):
            for gp in range(2):
                for pp in range(16):
                    sel[j, l, l * 32 + gp * 16 + pp, (2 * j + gp) * 16 + pp] = 1.0
    out["s5_sel"] = f(sel.reshape(16, 128, 128).transpose(1, 0, 2).reshape(128, 16 * 128))
    wblk = lambda w: f(np.asarray(w)[0].reshape(DC, 128, 4, 256).transpose(2, 1, 0, 3).reshape(4, 128, DC * 256))
    out["s5_win"] = wblk(inp["ssm_w_in"])
    sel2 = np.zeros((128, 4, 32), np.float32)
    for j in range(4):
        for gp in range(2):
            for pp in range(16):
                sel2[(2 * j + gp) * 16 + pp, j, gp * 16 + pp] = 1.0
    out["s5_sel2"] = f(sel2.reshape(128, 128))
    out["s5_wg"] = wblk(inp["ssm_w_gate"])
    out["s5_wo"] = wblk(inp["ssm_w_out"])
    return out


def core_inputs(shared, x_b, c_b):
    m = dict(shared)
    m["xT"] = np.ascontiguousarray(np.asarray(x_b, dtype=np.float32).T)
    m["cvec"] = np.ascontiguousarray(np.asarray(c_b, dtype=np.float32).reshape(DC, 128).T)
    return m


_NC_CACHE = {}


def run_subs(subs, shared, x, c, n_cores=NB):
    key = tuple(subs)
    if key not in _NC_CACHE:
        bld = Builder(list(subs))
        _NC_CACHE[key] = bld.build()
        if DEBUG:
            global LAST_MAP
            LAST_MAP = bld.dbg_map
    nc = _NC_CACHE[key]
    in_maps = [core_inputs(shared, x[b], c[b]) for b in range(n_cores)]
    res = run_bass_kernel_spmd(nc, in_maps, core_ids=list(range(n_cores)))
    if DEBUG:
        global LAST_DBG
        LAST_DBG = (res.results[0]["dbgf"], res.results[0]["dbgb"])
    return np.stack([np.ascontiguousarray(r["yT"].T) for r in res.results], axis=0)


def kernel(**inputs):
    shared = host_layout(inputs)
    x = np.asarray(inputs["x"], dtype=np.float32)
    c = np.asarray(inputs["c"], dtype=np.float32)
    y = run_subs([0, 1, 2, 3, 4, 5], shared, x, c)
    return y.astype(np.float32)
```
